# Optimizing a Trainium2 kernel written in Bass

```python
import jax, jax.numpy as jnp
from jax import lax
import numpy as np


D_MODEL = 1024
BATCH = 8
SEQ = 8192
DEPTH = 2

GRID_W = 64
CTX_LEN = 256
HEAD_DIM = 64
A_Q_HEADS = 6
A_KV_HEADS = 2
B_HEADS = 6
C_GROUPS = 4
C_WIDTH = C_GROUPS * HEAD_DIM
A_Q_W = A_Q_HEADS * HEAD_DIM
A_KV_W = A_KV_HEADS * HEAD_DIM
B_W = B_HEADS * HEAD_DIM
Q_W = A_Q_W + B_W
KV_W = 2 * A_KV_W + 2 * B_W
MIX_W = A_Q_W + B_W + C_WIDTH
PROJ_W = Q_W + KV_W + 3 * C_WIDTH
PROJ_SPLIT = (A_Q_W, Q_W, Q_W + KV_W, Q_W + KV_W + C_WIDTH, Q_W + KV_W + 2 * C_WIDTH)
KV_SPLIT = (A_KV_W, 2 * A_KV_W, 2 * A_KV_W + B_W)
Q_BLOCK = 128
WIN_R = 8
WIN_C = 16
ROPE_FREQS = HEAD_DIM // 4
ROPE_THETA = 10000.0
CONV_W = 3
FFN_DIM = 2816
N_MOD = 9
EPS = 1e-6

kernel_name = 'hybrid_parallel_heads_diffusion_block'


def rms_norm(x, g):
    xf = x.astype(jnp.float32)
    y = xf * lax.rsqrt(jnp.mean(xf * xf, axis=-1, keepdims=True) + EPS)
    return y.astype(x.dtype) * g


def heads(t, n):
    return t.reshape(t.shape[:-1] + (n, HEAD_DIM))


def axial_rope_tables(seq, dtype):
    pos = jnp.arange(seq, dtype=jnp.int32)
    row = (pos // GRID_W).astype(jnp.float32)
    col = (pos % GRID_W).astype(jnp.float32)
    freq = 1.0 / (ROPE_THETA ** (jnp.arange(ROPE_FREQS, dtype=jnp.float32) / ROPE_FREQS))
    ang = jnp.stack([row[:, None] * freq, col[:, None] * freq], axis=1)
    return jnp.cos(ang).astype(dtype), jnp.sin(ang).astype(dtype)


def apply_axial_rope(x, cos, sin):
    xr = x.reshape(x.shape[:-1] + (2, 2, ROPE_FREQS))
    x1, x2 = xr[..., 0, :], xr[..., 1, :]
    cs, sn = cos[None, :, None], sin[None, :, None]
    return jnp.stack([x1 * cs - x2 * sn, x2 * cs + x1 * sn], axis=-2).reshape(x.shape)


def gqa_attention(q, k, v):
    bsz, s_len, hq, dh = q.shape
    hkv = k.shape[2]
    grp = hq // hkv
    nb = s_len // Q_BLOCK
    qb = q.reshape(bsz, nb, Q_BLOCK, hkv, grp, dh).transpose(1, 0, 2, 3, 4, 5)
    scale = dh ** -0.5

    def block(qi):
        s = jnp.einsum('bqhgd,bkhd->bhgqk', qi, k).astype(jnp.float32) * scale
        p = jax.nn.softmax(s, axis=-1).astype(v.dtype)
        return jnp.einsum('bhgqk,bkhd->bqhgd', p, v)

    o = lax.map(block, qb)
    return o.transpose(1, 0, 2, 3, 4, 5).reshape(bsz, s_len, hq * dh)


def neighbourhood_attention(q, k, v, k_ctx, v_ctx, rpb, rows):
    bsz, s_len, nh, dh = q.shape
    wr = min(WIN_R, rows)
    n_win = wr * WIN_C
    scale = dh ** -0.5
    qg = q.reshape(bsz, rows, GRID_W, nh, dh)
    kg = k.reshape(bsz, rows, GRID_W, nh, dh)
    vg = v.reshape(bsz, rows, GRID_W, nh, dh)
    col = jnp.arange(GRID_W)
    cs = jnp.clip(col - WIN_C // 2, 0, GRID_W - WIN_C)
    col_idx = cs[:, None] + jnp.arange(WIN_C)[None, :]
    dc_idx = col_idx - col[:, None] + (WIN_C - 1)

    def row_block(r):
        rs = jnp.clip(r - wr // 2, 0, rows - wr)
        qr = lax.dynamic_index_in_dim(qg, r, axis=1, keepdims=False)
        kw = lax.dynamic_slice_in_dim(kg, rs, wr, axis=1)[:, :, col_idx]
        vw = lax.dynamic_slice_in_dim(vg, rs, wr, axis=1)[:, :, col_idx]
        dr_idx = rs + jnp.arange(wr) - r + (WIN_R - 1)
        bias = rpb[:, dr_idx][:, :, dc_idx].transpose(0, 2, 1, 3)
        s_win = jnp.einsum('bqhd,brqjhd->bhqrj', qr, kw).astype(jnp.float32) * scale + bias.astype(jnp.float32)[None]
        s_ctx = jnp.einsum('bqhd,bkhd->bhqk', qr, k_ctx).astype(jnp.float32) * scale
        s = jnp.concatenate([s_win.reshape(bsz, nh, GRID_W, n_win), s_ctx], axis=-1)
        p = jax.nn.softmax(s, axis=-1).astype(v.dtype)
        p_win = p[..., :n_win].reshape(bsz, nh, GRID_W, wr, WIN_C)
        return (jnp.einsum('bhqrj,brqjhd->bqhd', p_win, vw)
                + jnp.einsum('bhqk,bkhd->bqhd', p[..., n_win:], v_ctx))

    o = lax.map(row_block, jnp.arange(rows, dtype=jnp.int32))
    return o.transpose(1, 0, 2, 3, 4).reshape(bsz, s_len, nh * dh)


def short_conv_mixer(xi, gate_b, gate_c, w):
    z = gate_c * xi
    y = lax.conv_general_dilated(z, w[:, None, :], window_strides=(1,),
                                 padding=((CONV_W // 2, CONV_W // 2),),
                                 dimension_numbers=('NWC', 'WIO', 'NWC'),
                                 feature_group_count=z.shape[-1])
    return gate_b * y


def swiglu(u, wi, wo):
    g, up = jnp.split(u @ wi, 2, axis=-1)
    return (jax.nn.silu(g) * up) @ wo


def split_kv(pkv, k_gain):
    ka, va, kb, vb = jnp.split(pkv, KV_SPLIT, axis=-1)
    return (rms_norm(heads(ka, A_KV_HEADS), k_gain), heads(va, A_KV_HEADS),
            heads(kb, B_HEADS), heads(vb, B_HEADS))


def context_mix(p, qk_g, conv_w):
    qa, qb, pkv, cx, cb, cc = jnp.split(p, PROJ_SPLIT, axis=-1)
    ka, va, kb, vb = split_kv(pkv, qk_g[1])
    o_a = gqa_attention(rms_norm(heads(qa, A_Q_HEADS), qk_g[0]), ka, va)
    o_b = gqa_attention(heads(qb, B_HEADS), kb, vb)
    o_c = short_conv_mixer(cx, cb, cc, conv_w)
    return jnp.concatenate([o_a, o_b, o_c], axis=-1), (ka, va, kb, vb)


def latent_mix(p, kv_ctx, cos, sin, qk_g, rpb, conv_w, rows):
    qa, qb, pkv, cx, cb, cc = jnp.split(p, PROJ_SPLIT, axis=-1)
    ka, va, kb, vb = split_kv(pkv, qk_g[1])
    ka_c, va_c, kb_c, vb_c = kv_ctx
    qa = apply_axial_rope(rms_norm(heads(qa, A_Q_HEADS), qk_g[0]), cos, sin)
    ka = apply_axial_rope(ka, cos, sin)
    o_a = gqa_attention(qa, jnp.concatenate([ka, ka_c], axis=1), jnp.concatenate([va, va_c], axis=1))
    o_b = neighbourhood_attention(heads(qb, B_HEADS), kb, vb, kb_c, vb_c, rpb, rows)
    o_c = short_conv_mixer(cx, cb, cc, conv_w)
    return jnp.concatenate([o_a, o_b, o_c], axis=-1)


def hybrid_layer(h, hc, c, c_ctx, w_ada, b_ada, g, w_in, qk_g, rpb, conv_w, w_o, ffn_wi, ffn_wo,
                 cos, sin, rows, last):
    m = (jax.nn.silu(c) @ w_ada + b_ada).reshape(c.shape[0], 1, N_MOD, D_MODEL)
    mc = (jax.nn.silu(c_ctx) @ w_ada + b_ada).reshape(1, 1, N_MOD, D_MODEL)

    def pre(t, mm, i, gi):
        return rms_norm(t, g[gi]) * (1 + mm[:, :, i + 1]) + mm[:, :, i]

    def post(t, y, mm, i, gi, wres):
        return t + wres * mm[:, :, i + 2] * rms_norm(y, g[gi])

    h = post(h, swiglu(pre(h, m, 0, 0), ffn_wi[0], ffn_wo[0]), m, 0, 1, 0.5)
    hc = post(hc, swiglu(pre(hc, mc, 0, 0), ffn_wi[0], ffn_wo[0]), mc, 0, 1, 0.5)
    u = pre(h, m, 3, 2)
    uc = pre(hc, mc, 3, 2)
    if last:
        kv_c = split_kv(uc @ w_in[:, Q_W:Q_W + KV_W], qk_g[1])
    else:
        oc, kv_c = context_mix(uc @ w_in, qk_g, conv_w)
        hc = post(hc, oc @ w_o, mc, 3, 3, 1.0)
    o = latent_mix(u @ w_in, kv_c, cos, sin, qk_g, rpb, conv_w, rows)
    h = post(h, o @ w_o, m, 3, 3, 1.0)
    h = post(h, swiglu(pre(h, m, 6, 4), ffn_wi[1], ffn_wo[1]), m, 6, 5, 0.5)
    if not last:
        hc = post(hc, swiglu(pre(hc, mc, 6, 4), ffn_wi[1], ffn_wo[1]), mc, 6, 5, 0.5)
    return h, hc


def setup_inputs(seed: int = 0) -> dict:
    key = jax.random.key(seed)
    ks = jax.random.split(key, 14)
    nrm = jax.random.normal
    f32 = jnp.float32
    return {
        'x': nrm(ks[0], (BATCH, SEQ, D_MODEL), f32),
        'c': nrm(ks[1], (BATCH, D_MODEL), f32),
        'ctx': nrm(ks[2], (BATCH, CTX_LEN, D_MODEL), f32),
        'c_ctx': nrm(ks[3], (D_MODEL,), f32),
        'w_ada': nrm(ks[4], (DEPTH, D_MODEL, N_MOD * D_MODEL), f32) * (0.5 * D_MODEL ** -0.5),
        'b_ada': 0.01 * nrm(ks[5], (DEPTH, N_MOD * D_MODEL), f32),
        'norm_g': 1.0 + 0.05 * nrm(ks[6], (DEPTH, 6, D_MODEL), f32),
        'w_in': nrm(ks[7], (DEPTH, D_MODEL, PROJ_W), f32) * D_MODEL ** -0.5,
        'qk_g': 1.0 + 0.05 * nrm(ks[8], (DEPTH, 2, HEAD_DIM), f32),
        'rpb': 0.1 * nrm(ks[9], (DEPTH, B_HEADS, 2 * WIN_R - 1, 2 * WIN_C - 1), f32),
        'conv_w': nrm(ks[10], (DEPTH, CONV_W, C_WIDTH), f32) * CONV_W ** -0.5,
        'w_o': nrm(ks[11], (DEPTH, MIX_W, D_MODEL), f32) * MIX_W ** -0.5,
        'ffn_wi': nrm(ks[12], (DEPTH, 2, D_MODEL, 2 * FFN_DIM), f32) * D_MODEL ** -0.5,
        'ffn_wo': nrm(ks[13], (DEPTH, 2, FFN_DIM, D_MODEL), f32) * FFN_DIM ** -0.5,
    }


def reference(x, c, ctx, c_ctx, w_ada, b_ada, norm_g, w_in, qk_g, rpb, conv_w, w_o, ffn_wi, ffn_wo):
    seq = x.shape[1]
    rows = seq // GRID_W
    cos, sin = axial_rope_tables(seq, x.dtype)
    h, hc = x, ctx
    for i in range(DEPTH):
        h, hc = hybrid_layer(h, hc, c, c_ctx, w_ada[i], b_ada[i], norm_g[i], w_in[i], qk_g[i], rpb[i],
                             conv_w[i], w_o[i], ffn_wi[i], ffn_wo[i], cos, sin, rows, i == DEPTH - 1)
    return h
```

```python
import numpy as np
from contextlib import ExitStack
import concourse.bass as bass
import concourse.mybir as mybir
from concourse.bass_utils import run_bass_kernel_spmd

F32 = mybir.dt.float32
BF16 = mybir.dt.bfloat16
AF = mybir.ActivationFunctionType
ALU = mybir.AluOpType

D = 1024
KD = 8
FFN = 2816
KF = 22
LC = 256
GW = 64
NEG = -30000.0
EPS = 1e-6
NS = 132
NW = 3072
TF = 256
NDMA = 24
SAME_ENG_SYNC = True


class Buf:
    __slots__ = ("name", "w", "rc", "rd")

    def __init__(self, name):
        self.name = name
        self.w = []
        self.rc = {}
        self.rd = []


class Eng:
    def __init__(self, name, sem):
        self.name = name
        self.sem = sem
        self.base = 0
        self.reset()

    def reset(self):
        self.items = []
        self.marks = []
        self.waited = {}
        self.vals = []


class KB:
    HM = {"pe": "tensor", "act": "scalar", "dve": "vector", "pool": "gpsimd", "sp": "sync"}

    def __init__(self, nc, es):
        self.nc = nc
        self.engs = {}
        for n in self.HM:
            self.engs[n] = Eng(n, es.enter_context(nc.semaphore("s_" + n)))
        self.dsem = [es.enter_context(nc.semaphore("d%d" % i)) for i in range(NDMA)]
        self.dcnt = [0] * NDMA
        self.drr = 0
        self.bufs = []

    def B(self, name):
        b = Buf(name)
        self.bufs.append(b)
        return b

    def Bs(self, name, n):
        return [self.B("%s%d" % (name, i)) for i in range(n)]

    def _wait(self, eng, tok):
        if tok[0] == "c":
            pe, seq = tok[1], tok[2]
            if pe is eng and (eng.name == "pe" or eng.name == "sp" or not SAME_ENG_SYNC):
                return
            if eng.waited.get(pe.name, -1) >= seq:
                return
            eng.waited[pe.name] = seq
            pe.marks[seq] = True
            eng.items.append(("wc", pe, seq))
        else:
            idx, val = tok[1], tok[2]
            if eng.waited.get(("d", idx), 0) >= val:
                return
            eng.waited[("d", idx)] = val
            eng.items.append(("wd", idx, val))

    def _deps(self, eng, reads, writes):
        for b in reads:
            for t in b.w:
                self._wait(eng, t)
        for b in writes:
            for t in b.w:
                self._wait(eng, t)
            for n, s in b.rc.items():
                self._wait(eng, ("c", self.engs[n], s))
            for t in b.rd:
                self._wait(eng, t)

    def op(self, en, fn, reads=(), writes=()):
        eng = self.engs[en]
        self._deps(eng, reads, writes)
        seq = len(eng.marks)
        eng.marks.append(False)
        eng.items.append(("op", fn, seq))
        tok = ("c", eng, seq)
        for b in writes:
            b.w = [tok]
            b.rc = {}
            b.rd = []
        for b in reads:
            b.rc[en] = seq

    def dma(self, en, out, in_, reads=(), writes=()):
        eng = self.engs[en]
        self._deps(eng, reads, writes)
        idx = self.drr
        self.drr = (self.drr + 1) % NDMA
        self.dcnt[idx] += 16
        tok = ("d", idx, self.dcnt[idx])
        eng.items.append(("dma", out, in_, idx))
        for b in writes:
            b.w = [tok]
            b.rc = {}
            b.rd = []
        for b in reads:
            b.rd.append(tok)

    def flush(self):
        engs = self.engs
        for e in engs.values():
            if e.marks:
                e.marks[-1] = True
        for f in engs.values():
            for e in engs.values():
                if e is not f and e.marks:
                    f.items.append(("wc", e, len(e.marks) - 1))
            for i in range(NDMA):
                if self.dcnt[i] > 0:
                    f.items.append(("wd", i, self.dcnt[i]))
        for e in engs.values():
            v = e.base
            vals = []
            for m in e.marks:
                if m:
                    v += 1
                vals.append(v)
            e.vals = vals
            e.newbase = v
        dsem = self.dsem
        with self.nc.Block() as block:
            for name, e in engs.items():
                def mk(e):
                    def body(h):
                        for it in e.items:
                            k = it[0]
                            if k == "op":
                                ins = it[1](h)
                                if e.marks[it[2]]:
                                    ins.then_inc(e.sem, 1)
                            elif k == "wc":
                                h.wait_ge(it[1].sem, it[1].vals[it[2]])
                            elif k == "wd":
                                h.wait_ge(dsem[it[1]], it[2])
                            else:
                                h.dma_start(out=it[1], in_=it[2], allow_slow_non_contiguous=True).then_inc(dsem[it[3]], 16)
                    return body
                getattr(block, self.HM[name])(mk(e))
        for e in engs.values():
            e.base = e.newbase
            e.reset()
        for b in self.bufs:
            b.w = []
            b.rc = {}
            b.rd = []


def build_program(S, debug=False):
    NT = S + LC
    NQT = S // 128
    NKT = NT // 128
    nc = bass.Bass("TRN2", target_bir_lowering=False)
    dk = "ExternalOutput" if debug else "Internal"

    def din(name, shape, dt=F32):
        return nc.dram_tensor(name, list(shape), dt, kind="ExternalInput").ap()

    def dscr(name, shape, dt=F32):
        return nc.dram_tensor(name, list(shape), dt, kind=dk).ap()

    xT = din("xT", [D, NT])
    cvec = din("cvec", [128, KD, 2])
    smallp = din("smallp", [2, 128, NS])
    w_ada = din("w_ada", [2, D, 9 * D])
    w_inx = din("w_inx", [2, D, NW])
    w_o = din("w_o", [2, D, D])
    ffn_wi = din("ffn_wi", [2, 2, D, 2 * FFN])
    ffn_wo = din("ffn_wo", [2, 2, FFN, D])
    ropeC = din("ropeC", [128, S])
    ropeS = din("ropeS", [128, S])
    biasI = din("biasI", [2, 128, 5 * 6 * 128])
    biasE = din("biasE", [2, 128, 4 * 4 * 6 * 128])
    outT = nc.dram_tensor("outT", [D, S], F32, kind="ExternalOutput").ap()

    HA = dscr("HA", [D, NT])
    HB = dscr("HB", [D, NT])
    QA = dscr("QA", [128, 3, NT], BF16)
    KA = dscr("KA", [128, NT], BF16)
    QB = dscr("QB", [128, 3, NT], BF16)
    KBd = dscr("KBd", [128, 3, NT], BF16)
    VV = dscr("VV", [NT, 8, 65], BF16)
    ZZ = dscr("ZZ", [128, 2, NT + 4])
    CBd = dscr("CBd", [128, 2, NT])

    es = ExitStack()
    with es:
        K = KB(nc, es)

        uid = [0]

        def sb(name, shape, dt=F32, stack=es):
            uid[0] += 1
            return stack.enter_context(nc.sbuf_tensor("%s_%d" % (name, uid[0]), list(shape), dt))

        ps = [es.enter_context(nc.psum_tensor("ps%d" % i, [128, 512], F32)) for i in range(8)]
        psB = K.Bs("psB", 8)
        par = sb("par", [128, 2, 2, 9, KD])
        smp = sb("smp", [128, 2, NS])
        qsc = sb("qsc", [128, 2, 2])
        onesm = sb("onesm", [128, 128], BF16)
        bd64 = sb("bd64", [128, 128], BF16)
        onesb = sb("onesb", [128, 64], BF16)
        cB = K.B("consts")
        dramB = {}

        def DB(name, i):
            key = (name, i)
            if key not in dramB:
                dramB[key] = K.B("dram_%s_%d" % (name, i))
            return dramB[key]

        with ExitStack() as ph:
            cv = sb("cv", [128, KD, 2], F32, ph)
            scv = sb("scv", [128, KD, 2], F32, ph)
            wa = [sb("wa%d" % i, [128, KD, 1024], F32, ph) for i in range(2)]
            waB = K.Bs("waB", 2)
            mods = sb("mods", [128, 2, 72], F32, ph)
            cvB, scB, modB = K.B("cv"), K.B("scv"), K.B("mods")
            K.op("dve", lambda h: h.memset(onesm[:], 1.0 / 1024), writes=[cB])
            K.op("dve", lambda h: h.memset(bd64[:], 0.0), writes=[cB])
            K.op("dve", lambda h: h.memset(bd64[0:64, 0:64], 1.0 / 64), writes=[cB])
            K.op("dve", lambda h: h.memset(bd64[64:128, 64:128], 1.0 / 64), writes=[cB])
            K.op("dve", lambda h: h.memset(onesb[:], 1.0), writes=[cB])
            K.dma("sp", cv[:], cvec[:, :, :], writes=[cvB])
            K.dma("sp", smp[:], smallp.rearrange("l p n -> p l n"), writes=[cB])
            K.op("act", lambda h: h.activation(out=scv[:], in_=cv[:], func=AF.Silu), reads=[cvB], writes=[scB])
            it = 0
            for l in range(2):
                for i in range(9):
                    b = it % 2
                    it += 1
                    K.dma("sp", wa[b][:], w_ada[l].rearrange("(k p) n -> p k n", p=128)[:, :, i * 1024:(i + 1) * 1024],
                          writes=[waB[b]])
                    for j in range(8):
                        idx = i * 8 + j
                        for k in range(KD):
                            K.op("pe", (lambda b=b, j=j, k=k, idx=idx: lambda h: h.matmul(
                                ps[0][:, 2 * idx:2 * idx + 2], lhsT=wa[b][:, k, j * 128:(j + 1) * 128], rhs=scv[:, k, :],
                                start=(k == 0), stop=(k == KD - 1)))(), reads=[waB[b], scB], writes=[psB[0]])
                pv = ps[0][:, 0:144].rearrange("p (n t) -> p n t", t=2)
                for X in range(2):
                    K.op("dve", (lambda X=X, l=l, pv=pv: lambda h: h.tensor_tensor(
                        out=mods[:, X, :], in0=pv[:, :, X], in1=smp[:, l, 0:72], op=ALU.add))(),
                        reads=[psB[0], cB], writes=[modB])

                    def m(i, X=X):
                        return mods[:, X, i * 8:(i + 1) * 8]

                    def g(gi, l=l):
                        return smp[:, l, 72 + gi * 8:72 + (gi + 1) * 8]
                    for (kind, mi, gi, wres) in ((0, 1, 0, None), (3, 4, 2, None), (6, 7, 4, None)):
                        K.op("dve", (lambda kind=kind, mi=mi, gi=gi, X=X, l=l, m=m, g=g: lambda h: h.scalar_tensor_tensor(
                            out=par[:, l, X, kind, :], in0=m(mi), scalar=1.0, in1=g(gi), op0=ALU.add, op1=ALU.mult))(),
                            reads=[modB, cB], writes=[cB])
                    for (kind, mi) in ((1, 0), (4, 3), (7, 6)):
                        K.op("dve", (lambda kind=kind, mi=mi, X=X, l=l, m=m: lambda h: h.tensor_copy(
                            out=par[:, l, X, kind, :], in_=m(mi)))(), reads=[modB], writes=[cB])
                    for (kind, mi, gi, wres) in ((2, 2, 1, 0.5), (5, 5, 3, 1.0), (8, 8, 5, 0.5)):
                        K.op("dve", (lambda kind=kind, mi=mi, gi=gi, wres=wres, X=X, l=l, m=m, g=g: lambda h: h.scalar_tensor_tensor(
                            out=par[:, l, X, kind, :], in0=m(mi), scalar=wres, in1=g(gi), op0=ALU.mult, op1=ALU.mult))(),
                            reads=[modB, cB], writes=[cB])
                K.op("dve", (lambda l=l: lambda h: h.tensor_scalar(
                    out=qsc[:, l, :], in0=smp[:, l, 120:122], scalar1=0.125, scalar2=None, op0=ALU.mult))(),
                    reads=[cB], writes=[cB])
            K.flush()

        def phase_ffn(l, which, Hin, Hout, hin_name, hout_name, do_ctx, final_out=None):
            kA, kB_, kG = (0, 1, 2) if which == 0 else (6, 7, 8)
            with ExitStack() as ph:
                wi = sb("wi", [128, KD, 2 * FFN], BF16, ph)
                wo = sb("wo", [128, KF, D], BF16, ph)
                wiB = K.Bs("wiB", KD)
                woB = K.Bs("woB", 2)
                hT = [sb("hT%d" % i, [128, KD, TF], F32, ph) for i in range(2)]
                hB = [K.Bs("hB%d_" % i, KD) for i in range(2)]
                u = sb("u", [128, KD, TF], BF16, ph)
                uB = K.Bs("uB", KD)
                act = sb("actb", [128, KF, TF], BF16, ph)
                actB = K.Bs("actB", KF)
                ysb = sb("ysb", [128, KD, TF], F32, ph)
                yB = K.Bs("yB", KD)
                rstd = sb("rstd", [128, TF], F32, ph)
                rB = K.B("rstd")
                tmp = [sb("tmp%d" % i, [128, TF], F32, ph) for i in range(2)]
                tB = K.Bs("tmpB", 2)
                sg = [sb("sg%d" % i, [128, TF], F32, ph) for i in range(2)]
                sgB = K.Bs("sgB", 2)
                for k in range(KD):
                    K.dma("pool", wi[:, k, :], ffn_wi[l, which, k * 128:(k + 1) * 128, :], writes=[wiB[k]])
                wov = ffn_wo[l, which].rearrange("(m p) n -> p m n", p=128)
                K.dma("pool", wo[:, 0:11, :], wov[:, 0:11, :], writes=[woB[0]])
                K.dma("pool", wo[:, 11:22, :], wov[:, 11:22, :], writes=[woB[1]])
                tiles = [(t * TF, 0) for t in range(S // TF)]
                if do_ctx:
                    tiles.append((S, 1))

                def load(i):
                    c0, X = tiles[i]
                    b = i % 2
                    K.dma("sp", hT[b][:], Hin[:, c0:c0 + TF].rearrange("(k p) t -> p k t", p=128),
                          reads=[DB(hin_name, c0 // TF)], writes=hB[b])
                load(0)
                for i, (c0, X) in enumerate(tiles):
                    b = i % 2
                    if i + 1 < len(tiles):
                        load(i + 1)
                    h_, hb = hT[b], hB[b]

                    def stats(srcbufs):
                        for k in range(KD):
                            K.op("pe", (lambda k=k: lambda h: h.matmul(ps[0][:, 0:TF], lhsT=onesm[:], rhs=u[:, k, :],
                                                                       start=(k == 0), stop=(k == KD - 1)))(),
                                 reads=[uB[k]], writes=[psB[0]])
                        K.op("act", lambda h: h.activation(out=rstd[:], in_=ps[0][:, 0:TF], func=AF.Sqrt, bias=EPS, scale=1.0),
                             reads=[psB[0]], writes=[rB])
                        K.op("dve", lambda h: h.reciprocal(out=rstd[:], in_=rstd[:]), reads=[rB], writes=[rB])
                    for k in range(KD):
                        K.op("act", (lambda k=k, h_=h_: lambda h: h.activation(out=u[:, k, :], in_=h_[:, k, :], func=AF.Square))(),
                             reads=[hb[k]], writes=[uB[k]])
                    stats(None)
                    for k in range(KD):
                        tb = k % 2
                        K.op("dve", (lambda k=k, tb=tb, h_=h_, X=X: lambda h: h.scalar_tensor_tensor(
                            out=tmp[tb][:], in0=h_[:, k, :], scalar=par[:, l, X, kA, k:k + 1], in1=rstd[:],
                            op0=ALU.mult, op1=ALU.mult))(), reads=[hb[k], rB], writes=[tB[tb]])
                        K.op("act", (lambda k=k, tb=tb, X=X: lambda h: h.activation(
                            out=u[:, k, :], in_=tmp[tb][:], func=AF.Identity, bias=par[:, l, X, kB_, k:k + 1], scale=1.0))(),
                            reads=[tB[tb]], writes=[uB[k]])
                    for m in range(KF):
                        pb = m % 2
                        for half, bank in ((0, 1 + pb), (1, 3 + pb)):
                            for k in range(KD):
                                c_ = half * FFN + m * 128
                                K.op("pe", (lambda k=k, c_=c_, bank=bank: lambda h: h.matmul(
                                    ps[bank][:, 0:TF], lhsT=wi[:, k, c_:c_ + 128], rhs=u[:, k, :],
                                    start=(k == 0), stop=(k == KD - 1)))(), reads=[wiB[k], uB[k]], writes=[psB[bank]])
                        K.op("act", (lambda pb=pb: lambda h: h.activation(out=sg[pb][:], in_=ps[1 + pb][:, 0:TF], func=AF.Silu))(),
                             reads=[psB[1 + pb]], writes=[sgB[pb]])
                        K.op("dve", (lambda pb=pb, m=m: lambda h: h.tensor_tensor(
                            out=act[:, m, :], in0=sg[pb][:], in1=ps[3 + pb][:, 0:TF], op=ALU.mult))(),
                            reads=[sgB[pb], psB[3 + pb]], writes=[actB[m]])
                    for j in range(KD):
                        bank = 5 + j % 2
                        for m in range(KF):
                            K.op("pe", (lambda j=j, m=m, bank=bank: lambda h: h.matmul(
                                ps[bank][:, 0:TF], lhsT=wo[:, m, j * 128:(j + 1) * 128], rhs=act[:, m, :],
                                start=(m == 0), stop=(m == KF - 1)))(), reads=[woB[m // 11], actB[m]], writes=[psB[bank]])
                        K.op("act", (lambda j=j, bank=bank: lambda h: h.activation(out=ysb[:, j, :], in_=ps[bank][:, 0:TF], func=AF.Copy))(),
                             reads=[psB[bank]], writes=[yB[j]])
                        K.op("act", (lambda j=j, bank=bank: lambda h: h.activation(out=u[:, j, :], in_=ps[bank][:, 0:TF], func=AF.Square))(),
                             reads=[psB[bank]], writes=[uB[j]])
                    stats(None)
                    for j in range(KD):
                        tb = j % 2
                        K.op("dve", (lambda j=j, tb=tb, X=X: lambda h: h.scalar_tensor_tensor(
                            out=tmp[tb][:], in0=ysb[:, j, :], scalar=par[:, l, X, kG, j:j + 1], in1=rstd[:],
                            op0=ALU.mult, op1=ALU.mult))(), reads=[yB[j], rB], writes=[tB[tb]])
                        K.op("pool", (lambda j=j, tb=tb, h_=h_: lambda h: h.tensor_tensor(
                            out=h_[:, j, :], in0=h_[:, j, :], in1=tmp[tb][:], op=ALU.add))(),
                            reads=[tB[tb]], writes=[hb[j]])
                    if final_out is not None and X == 0:
                        K.dma("sp", final_out[:, c0:c0 + TF].rearrange("(k p) t -> p k t", p=128), h_[:], reads=hb)
                    else:
                        K.dma("sp", Hout[:, c0:c0 + TF].rearrange("(k p) t -> p k t", p=128), h_[:],
                              reads=hb, writes=[DB(hout_name, c0 // TF)])
                K.flush()

        def phase_proj(l, Hin, hin_name):
            TP = 512
            with ExitStack() as ph:
                win = sb("win", [128, KD, NW], BF16, ph)
                winB = K.Bs("winB", KD)
                hT = [sb("hT%d" % i, [128, KD, TP], F32, ph) for i in range(2)]
                hB = [K.Bs("hB%d_" % i, KD) for i in range(2)]
                rc = [sb("rc%d" % i, [128, TP], F32, ph) for i in range(2)]
                rs_ = [sb("rs%d" % i, [128, TP], F32, ph) for i in range(2)]
                rcB = K.Bs("rcB", 2)
                u = sb("u", [128, KD, TP], BF16, ph)
                uB = K.Bs("uB", KD)
                rstd = sb("rstd", [128, TP], F32, ph)
                rB = K.B("rstd")
                tmp = [sb("tmp%d" % i, [128, TP], F32, ph) for i in range(2)]
                tB = K.Bs("tmpB", 2)
                sqp = sb("sqp", [128, TP], BF16, ph)
                sqB = K.B("sqp")
                rq = sb("rq", [128, TP], F32, ph)
                rqB = K.B("rq")
                t1 = sb("t1", [128, TP], F32, ph)
                t2 = sb("t2", [128, TP], F32, ph)
                t1B, t2B = K.B("t1"), K.B("t2")
                qa = sb("qa", [128, 3, TP], BF16, ph)
                ka = sb("ka", [128, TP], BF16, ph)
                qb = sb("qb", [128, 3, TP], BF16, ph)
                kb = sb("kb", [128, 3, TP], BF16, ph)
                zz = sb("zz", [128, 2, TP], F32, ph)
                cbs = sb("cbs", [128, 2, TP], F32, ph)
                cxs = sb("cxs", [128, TP], F32, ph)
                vv = sb("vv", [128, 4, 8, 65], BF16, ph)
                zer = sb("zer", [128, 2, 2], F32, ph)
                qaB, kaB, qbB, kbB, zzB, cbB, cxB, vvB, zeB = (K.B(n) for n in ("qa", "ka", "qb", "kb", "zz", "cbs", "cxs", "vv", "zer"))
                for k in range(KD):
                    K.dma("pool", win[:, k, :], w_inx[l, k * 128:(k + 1) * 128, :], writes=[winB[k]])
                K.op("dve", lambda h: h.memset(vv[:], 1.0), writes=[vvB])
                K.op("dve", lambda h: h.memset(zer[:], 0.0), writes=[zeB])
                K.dma("sp", ZZ[:, :, 0:1], zer[:, :, 0:1], reads=[zeB])
                K.dma("sp", ZZ[:, :, S + 1:S + 3], zer[:, :, 0:2], reads=[zeB])
                K.dma("sp", ZZ[:, :, NT + 3:NT + 4], zer[:, :, 0:1], reads=[zeB])
                tiles = [(t * TP, TP, 0) for t in range(S // TP)] + [(S, LC, 1)]

                def load(i):
                    c0, n, X = tiles[i]
                    b = i % 2
                    K.dma("sp", hT[b][:, :, 0:n], Hin[:, c0:c0 + n].rearrange("(k p) t -> p k t", p=128),
                          reads=[DB(hin_name, j) for j in range(c0 // TF, (c0 + n) // TF)], writes=hB[b])
                    if X == 0:
                        K.dma("sp", rc[b][:], ropeC[:, c0:c0 + n], writes=[rcB[b]])
                        K.dma("sp", rs_[b][:], ropeS[:, c0:c0 + n], writes=[rcB[b]])
                load(0)
                for i, (c0, n, X) in enumerate(tiles):
                    b = i % 2
                    if i + 1 < len(tiles):
                        load(i + 1)
                    h_, hb = hT[b], hB[b]
                    for k in range(KD):
                        K.op("act", (lambda k=k, h_=h_, n=n: lambda h: h.activation(out=u[:, k, 0:n], in_=h_[:, k, 0:n], func=AF.Square))(),
                             reads=[hb[k]], writes=[uB[k]])
                    for k in range(KD):
                        K.op("pe", (lambda k=k, n=n: lambda h: h.matmul(ps[0][:, 0:n], lhsT=onesm[:], rhs=u[:, k, 0:n],
                                                                       start=(k == 0), stop=(k == KD - 1)))(),
                             reads=[uB[k]], writes=[psB[0]])
                    K.op("act", (lambda n=n: lambda h: h.activation(out=rstd[:, 0:n], in_=ps[0][:, 0:n], func=AF.Sqrt, bias=EPS, scale=1.0))(),
                         reads=[psB[0]], writes=[rB])
                    K.op("dve", (lambda n=n: lambda h: h.reciprocal(out=rstd[:, 0:n], in_=rstd[:, 0:n]))(), reads=[rB], writes=[rB])
                    for k in range(KD):
                        tb = k % 2
                        K.op("dve", (lambda k=k, tb=tb, h_=h_, X=X, n=n: lambda h: h.scalar_tensor_tensor(
                            out=tmp[tb][:, 0:n], in0=h_[:, k, 0:n], scalar=par[:, l, X, 3, k:k + 1], in1=rstd[:, 0:n],
                            op0=ALU.mult, op1=ALU.mult))(), reads=[hb[k], rB], writes=[tB[tb]])
                        K.op("act", (lambda k=k, tb=tb, X=X, n=n: lambda h: h.activation(
                            out=u[:, k, 0:n], in_=tmp[tb][:, 0:n], func=AF.Identity, bias=par[:, l, X, 4, k:k + 1], scale=1.0))(),
                            reads=[tB[tb]], writes=[uB[k]])

                    def proj(g, bank):
                        for k in range(KD):
                            K.op("pe", (lambda k=k, g=g, bank=bank, n=n: lambda h: h.matmul(
                                ps[bank][:, 0:n], lhsT=win[:, k, g * 128:(g + 1) * 128], rhs=u[:, k, 0:n],
                                start=(k == 0), stop=(k == KD - 1)))(), reads=[winB[k], uB[k]], writes=[psB[bank]])

                    ng = [(c, 3 + c, ("q", 0), ("q", 1), qa[:, c, 0:n], qaB) for c in range(3)]
                    ng.append((6, 7, ("k", 122), ("k", 123), ka[:, 0:n], kaB))
                    for gi, (g0, g1, s0, s1, dst, dB) in enumerate(ng):
                        ba, bb_ = 1 + (gi % 2) * 2, 2 + (gi % 2) * 2
                        proj(g0, ba)
                        if X == 0:
                            proj(g1, bb_)
                        sc0 = qsc[:, l, 0:1] if s0[0] == "q" else smp[:, l, s0[1]:s0[1] + 1]
                        sc1 = qsc[:, l, 1:2] if s1[0] == "q" else smp[:, l, s1[1]:s1[1] + 1]
                        K.op("act", (lambda ba=ba, n=n: lambda h: h.activation(out=sqp[:, 0:n], in_=ps[ba][:, 0:n], func=AF.Square))(),
                             reads=[psB[ba]], writes=[sqB])
                        K.op("pe", (lambda n=n: lambda h: h.matmul(ps[5][:, 0:n], lhsT=bd64[:], rhs=sqp[:, 0:n], start=True, stop=True))(),
                             reads=[sqB, cB], writes=[psB[5]])
                        K.op("act", (lambda n=n: lambda h: h.activation(out=rq[:, 0:n], in_=ps[5][:, 0:n], func=AF.Sqrt, bias=EPS, scale=1.0))(),
                             reads=[psB[5]], writes=[rqB])
                        K.op("dve", (lambda n=n: lambda h: h.reciprocal(out=rq[:, 0:n], in_=rq[:, 0:n]))(), reads=[rqB], writes=[rqB])
                        if X == 0:
                            K.op("dve", (lambda ba=ba, sc0=sc0, n=n: lambda h: h.scalar_tensor_tensor(
                                out=t1[:, 0:n], in0=ps[ba][:, 0:n], scalar=sc0, in1=rq[:, 0:n], op0=ALU.mult, op1=ALU.mult))(),
                                reads=[psB[ba], rqB], writes=[t1B])
                            K.op("dve", (lambda bb_=bb_, sc1=sc1, n=n: lambda h: h.scalar_tensor_tensor(
                                out=t2[:, 0:n], in0=ps[bb_][:, 0:n], scalar=sc1, in1=rq[:, 0:n], op0=ALU.mult, op1=ALU.mult))(),
                                reads=[psB[bb_], rqB], writes=[t2B])
                            K.op("pool", (lambda b=b, n=n: lambda h: h.tensor_tensor(out=t1[:, 0:n], in0=t1[:, 0:n], in1=rc[b][:, 0:n], op=ALU.mult))(),
                                 reads=[rcB[b]], writes=[t1B])
                            K.op("pool", (lambda b=b, n=n: lambda h: h.tensor_tensor(out=t2[:, 0:n], in0=t2[:, 0:n], in1=rs_[b][:, 0:n], op=ALU.mult))(),
                                 reads=[rcB[b]], writes=[t2B])
                            K.op("pool", (lambda dst=dst, n=n: lambda h: h.tensor_tensor(out=dst, in0=t1[:, 0:n], in1=t2[:, 0:n], op=ALU.add))(),
                                 reads=[t1B, t2B], writes=[dB])
                        else:
                            K.op("dve", (lambda ba=ba, sc0=sc0, dst=dst, n=n: lambda h: h.scalar_tensor_tensor(
                                out=dst, in0=ps[ba][:, 0:n], scalar=sc0, in1=rq[:, 0:n], op0=ALU.mult, op1=ALU.mult))(),
                                reads=[psB[ba], rqB], writes=[dB])
                    for c in range(3):
                        bank = 6 + c % 2
                        proj(8 + c, bank)
                        K.op("act", (lambda c=c, bank=bank, n=n: lambda h: h.activation(out=qb[:, c, 0:n], in_=ps[bank][:, 0:n], func=AF.Copy, scale=0.125))(),
                             reads=[psB[bank]], writes=[qbB])
                    for c in range(3):
                        bank = 6 + (c + 1) % 2
                        proj(11 + c, bank)
                        K.op("act", (lambda c=c, bank=bank, n=n: lambda h: h.activation(out=kb[:, c, 0:n], in_=ps[bank][:, 0:n], func=AF.Copy))(),
                             reads=[psB[bank]], writes=[kbB])
                    for c in range(2):
                        proj(14 + c, 6)
                        proj(18 + c, 7)
                        K.op("act", (lambda n=n: lambda h: h.activation(out=cxs[:, 0:n], in_=ps[6][:, 0:n], func=AF.Copy))(),
                             reads=[psB[6]], writes=[cxB])
                        K.op("dve", (lambda c=c, n=n: lambda h: h.tensor_tensor(out=zz[:, c, 0:n], in0=cxs[:, 0:n], in1=ps[7][:, 0:n], op=ALU.mult))(),
                             reads=[cxB, psB[7]], writes=[zzB])
                        proj(16 + c, 6)
                        K.op("act", (lambda c=c, n=n: lambda h: h.activation(out=cbs[:, c, 0:n], in_=ps[6][:, 0:n], func=AF.Copy))(),
                             reads=[psB[6]], writes=[cbB])
                    for s in range(n // 128):
                        bank = 1 + s % 2
                        for k in range(KD):
                            K.op("pe", (lambda k=k, s=s, bank=bank: lambda h: h.matmul(
                                ps[bank][:, :], lhsT=u[:, k, s * 128:(s + 1) * 128], rhs=win[:, k, 2560:3072],
                                start=(k == 0), stop=(k == KD - 1)))(), reads=[winB[k], uB[k]], writes=[psB[bank]])
                        K.op("dve", (lambda s=s, bank=bank: lambda h: h.tensor_copy(
                            out=vv[:, s, :, 0:64], in_=ps[bank][:, :].rearrange("p (h d) -> p h d", d=64)))(),
                            reads=[psB[bank]], writes=[vvB])
                    tl = [DB("proj", j) for j in range(c0 // 128, (c0 + n) // 128)]
                    K.dma("sp", QA[:, :, c0:c0 + n], qa[:, :, 0:n], reads=[qaB], writes=tl)
                    K.dma("sp", KA[:, c0:c0 + n], ka[:, 0:n], reads=[kaB], writes=tl)
                    K.dma("sp", QB[:, :, c0:c0 + n], qb[:, :, 0:n], reads=[qbB], writes=tl)
                    K.dma("sp", KBd[:, :, c0:c0 + n], kb[:, :, 0:n], reads=[kbB], writes=tl)
                    zc0 = c0 + 1 if X == 0 else c0 + 3
                    K.dma("sp", ZZ[:, :, zc0:zc0 + n], zz[:, :, 0:n], reads=[zzB], writes=tl)
                    K.dma("sp", CBd[:, :, c0:c0 + n], cbs[:, :, 0:n], reads=[cbB], writes=tl)
                    K.dma("sp", VV[c0:c0 + n].rearrange("(s p) h d -> p s h d", p=128), vv[:, 0:n // 128], reads=[vvB], writes=tl)
                K.flush()

        def phase_attn(l, Hin, Hout, hin_name, hout_name, do_ctx):
            TP = 512
            HP = 256
            with ExitStack() as ph:
                kaT = sb("kaT", [128, NT], BF16, ph)
                vaT = sb("vaT", [128, NKT, 2, 65], BF16, ph)
                kbc = sb("kbc", [128, 3, LC], BF16, ph)
                vbc = sb("vbc", [128, 2, 6, 65], BF16, ph)
                woh = sb("woh", [64, 12, D], BF16, ph)
                woc = sb("woc", [128, 2, D], BF16, ph)
                bI = sb("bI", [128, 5, 6, 128], F32, ph)
                bE = sb("bE", [128, 2, 4, 2, 128], F32, ph)
                resB, bEB = K.B("resident"), K.B("bE")
                qa = [sb("qa%d" % i, [128, 3, TP], BF16, ph) for i in range(2)]
                qaB = K.Bs("qaB", 2)
                qb = sb("qb", [128, 3, TP], BF16, ph)
                kbw = sb("kbw", [128, 3, 1024], BF16, ph)
                vbw = sb("vbw", [128, 8, 6, 65], BF16, ph)
                zt = sb("zt", [128, 2, TP + 2], F32, ph)
                cbt = sb("cbt", [128, 2, TP], F32, ph)
                hT = sb("hT", [128, KD, HP], F32, ph)
                ib = K.B("inB")
                hb = K.Bs("hB", KD)
                pT = [sb("pT%d" % i, [128, 7 * 128], BF16, ph) for i in range(4)]
                pB = K.Bs("pB", 4)
                sbias = sb("sbias", [128, 5, 128], F32, ph)
                sbB = K.B("sbB")
                oT = sb("oT", [64, 12, TP], BF16, ph)
                oB = K.Bs("oB", 12)
                ocT = sb("ocT", [128, 2, TP], BF16, ph)
                ocB = K.B("ocT")
                cacc = sb("cacc", [128, 2, TP], F32, ph)
                caB = K.B("cacc")
                rec = sb("rec", [128, TP], F32, ph)
                rech = sb("rech", [128, TP], BF16, ph)
                recl = sb("recl", [128, TP], BF16, ph)
                recB = K.B("rec")
                bcs = sb("bcs", [64, TP], F32, ph)
                bcB = K.B("bcs")
                ysb = sb("ysb", [128, KD, HP], F32, ph)
                yB = K.Bs("yB", KD)
                sq = sb("sq", [128, KD, HP], BF16, ph)
                sqB = K.Bs("sqB", KD)
                rstd = sb("rstd", [128, HP], F32, ph)
                rB = K.B("rstd")
                tmp = [sb("tmp%d" % i, [128, HP], F32, ph) for i in range(2)]
                tB = K.Bs("tmpB", 2)
                K.dma("sp", kaT[:], KA[:, :], writes=[resB])
                K.dma("sp", vaT[:], VV[:, 0:2, :].rearrange("(s p) h d -> p s h d", p=128), writes=[resB])
                K.dma("sp", kbc[:], KBd[:, :, S:NT], writes=[resB])
                K.dma("sp", vbc[:], VV[S:NT, 2:8, :].rearrange("(s p) h d -> p s h d", p=128), writes=[resB])
                K.dma("pool", woh[:], w_o[l, 0:768, :].rearrange("(h p) n -> p h n", p=64), writes=[resB])
                K.dma("pool", woc[:], w_o[l, 768:1024, :].rearrange("(k p) n -> p k n", p=128), writes=[resB])
                K.dma("sp", bI[:], biasI[l].rearrange("p (a h q) -> p a h q", a=5, h=6), writes=[resB])
                tiles = [(t * TP, TP, 0) for t in range(S // TP)]
                if do_ctx:
                    tiles.append((S, LC, 1))
                ntl = S // TP
                assert ntl >= 2
                bEv = biasE[l].rearrange("p (c a h q) -> p c a h q", c=4, a=4, h=6)

                def kt0_of(t):
                    return max(0, min(4 * t - 2, NQT - 8))

                def load_q(i):
                    c0, n, X = tiles[i]
                    K.dma("sp", qa[i % 2][:, :, 0:n], QA[:, :, c0:c0 + n], writes=[qaB[i % 2]])

                load_q(0)
                pcount = [0]
                for i, (c0, n, X) in enumerate(tiles):
                    b = i % 2
                    t = c0 // TP
                    if i + 1 < len(tiles):
                        load_q(i + 1)
                    K.dma("sp", qb[:, :, 0:n], QB[:, :, c0:c0 + n], writes=[ib])
                    zc0 = c0 if X == 0 else c0 + 2
                    K.dma("sp", zt[:, :, 0:n + 2], ZZ[:, :, zc0:zc0 + n + 2], writes=[ib])
                    K.dma("sp", cbt[:, :, 0:n], CBd[:, :, c0:c0 + n], writes=[ib])
                    if X == 0:
                        k0 = kt0_of(t)
                        K.dma("sp", kbw[:], KBd[:, :, k0 * 128:k0 * 128 + 1024], writes=[ib])
                        K.dma("sp", vbw[:], VV[k0 * 128:k0 * 128 + 1024, 2:8, :].rearrange("(s p) h d -> p s h d", p=128), writes=[ib])
                    qab = qaB[b]

                    def normalize(obank, hidx, n=n):
                        K.op("dve", (lambda obank=obank, n=n: lambda h: h.reciprocal(out=rec[64:65, 0:n], in_=ps[obank][64:65, 0:n]))(),
                             reads=[psB[obank]], writes=[recB])
                        K.op("dve", (lambda n=n: lambda h: h.tensor_copy(out=rech[64:65, 0:n], in_=rec[64:65, 0:n]))(),
                             reads=[recB], writes=[recB])
                        K.op("dve", (lambda n=n: lambda h: h.tensor_tensor(out=recl[64:65, 0:n], in0=rec[64:65, 0:n], in1=rech[64:65, 0:n], op=ALU.subtract))(),
                             reads=[recB], writes=[recB])
                        K.op("pe", (lambda n=n: lambda h: h.matmul(ps[6][0:64, 0:n], lhsT=onesb[64:65, 0:64], rhs=rech[64:65, 0:n], start=True, stop=False))(),
                             reads=[recB, cB], writes=[psB[6]])
                        K.op("pe", (lambda n=n: lambda h: h.matmul(ps[6][0:64, 0:n], lhsT=onesb[64:65, 0:64], rhs=recl[64:65, 0:n], start=False, stop=True))(),
                             reads=[recB, cB], writes=[psB[6]])
                        K.op("act", (lambda n=n: lambda h: h.activation(out=bcs[:, 0:n], in_=ps[6][0:64, 0:n], func=AF.Copy))(),
                             reads=[psB[6]], writes=[bcB])
                        K.op("dve", (lambda obank=obank, hidx=hidx, n=n: lambda h: h.tensor_tensor(
                            out=oT[:, hidx, 0:n], in0=ps[obank][0:64, 0:n], in1=bcs[:, 0:n], op=ALU.mult))(),
                            reads=[psB[obank], bcB], writes=[oB[hidx]])

                    kts = list(range(NKT)) if X == 0 else list(range(NQT, NKT))
                    for c in range(3):
                        def qk(ki, c=c, n=n, b=b):
                            kt = kts[ki]
                            for half in range(2):
                                sbank = 2 * half + ki % 2
                                lo = 64 * half
                                K.op("pe", (lambda kt=kt, c=c, lo=lo, sbank=sbank, b=b, n=n: lambda h: h.matmul(
                                    ps[sbank][:, 0:n], lhsT=kaT[lo:lo + 64, kt * 128:(kt + 1) * 128], rhs=qa[b][lo:lo + 64, c, 0:n],
                                    start=True, stop=True))(), reads=[resB, qab], writes=[psB[sbank]])
                        qk(0)
                        for ki, kt in enumerate(kts):
                            if ki + 1 < len(kts):
                                qk(ki + 1)
                            for half in range(2):
                                sbank = 2 * half + ki % 2
                                pi = 2 * half + ki % 2
                                K.op("act", (lambda sbank=sbank, pi=pi, n=n: lambda h: h.activation(
                                    out=pT[pi][:, 0:n], in_=ps[sbank][:, 0:n], func=AF.Exp))(),
                                    reads=[psB[sbank]], writes=[pB[pi]])
                            for half in range(2):
                                pi = 2 * half + ki % 2
                                K.op("pe", (lambda kt=kt, half=half, pi=pi, ki=ki, n=n: lambda h: h.matmul(
                                    ps[4 + half][0:65, 0:n], lhsT=vaT[:, kt, half, :], rhs=pT[pi][:, 0:n],
                                    start=(ki == 0), stop=(ki == len(kts) - 1)))(), reads=[resB, pB[pi]], writes=[psB[4 + half]])
                        normalize(4, c)
                        normalize(5, c + 3)
                    for c in range(3):
                        if X == 0 and (t == 0 or t == ntl - 1):
                            cls0 = 0 if t == 0 else 2
                            K.dma("sp", bE[:], bEv[:, cls0:cls0 + 2, :, 2 * c:2 * c + 2, :], writes=[bEB])
                        for qs in range(n // 128):
                            q0 = qs * 128
                            if X == 0:
                                qt = c0 // 128 + qs
                                if qt < 2 or qt >= NQT - 2:
                                    wk = list(range(0, 4)) if qt < 2 else list(range(NQT - 4, NQT))
                                    cls = qt if qt < 2 else qt - (NQT - 2)
                                    bsrc = lambda a0, a1, half, cls=cls: bE[:, cls, a0:a1, half, :]
                                    brd = bEB
                                else:
                                    wk = list(range(qt - 2, qt + 3))
                                    bsrc = lambda a0, a1, half, c=c: bI[:, a0:a1, 2 * c + half, :]
                                    brd = resB
                            else:
                                wk = []
                            nw = len(wk)
                            for half in range(2):
                                lo = 64 * half
                                for a, kt in enumerate(wk):
                                    bank = 2 * half + (0 if a < 4 else 1)
                                    col = (a % 4) * 128
                                    sl = (kt - k0) * 128
                                    K.op("pe", (lambda bank=bank, col=col, sl=sl, lo=lo, c=c, q0=q0: lambda h: h.matmul(
                                        ps[bank][:, col:col + 128], lhsT=kbw[lo:lo + 64, c, sl:sl + 128],
                                        rhs=qb[lo:lo + 64, c, q0:q0 + 128], start=True, stop=True))(),
                                        reads=[ib], writes=[psB[bank]])
                                for a in range(2):
                                    bank = 2 * half + 1
                                    col = 128 + a * 128
                                    K.op("pe", (lambda bank=bank, col=col, a=a, lo=lo, c=c, q0=q0: lambda h: h.matmul(
                                        ps[bank][:, col:col + 128], lhsT=kbc[lo:lo + 64, c, a * 128:(a + 1) * 128],
                                        rhs=qb[lo:lo + 64, c, q0:q0 + 128], start=True, stop=True))(),
                                        reads=[ib, resB], writes=[psB[bank]])
                            for half in range(2):
                                hh = 2 * c + half
                                pi = pcount[0] % 4
                                pcount[0] += 1
                                if nw:
                                    na = min(nw, 4)
                                    K.op("dve", (lambda half=half, na=na, bsrc=bsrc: lambda h: h.tensor_tensor(
                                        out=sbias[:, 0:na, :], in0=ps[2 * half][:, 0:na * 128].rearrange("p (a q) -> p a q", q=128),
                                        in1=bsrc(0, na, half), op=ALU.add))(),
                                        reads=[psB[2 * half], brd], writes=[sbB])
                                    if nw > 4:
                                        K.op("dve", (lambda half=half, bsrc=bsrc: lambda h: h.tensor_tensor(
                                            out=sbias[:, 4:5, :], in0=ps[2 * half + 1][:, 0:128].rearrange("p (a q) -> p a q", q=128),
                                            in1=bsrc(4, 5, half), op=ALU.add))(),
                                            reads=[psB[2 * half + 1], brd], writes=[sbB])
                                    K.op("act", (lambda pi=pi, nw=nw: lambda h: h.activation(
                                        out=pT[pi][:, 0:nw * 128].rearrange("p (a q) -> p a q", q=128), in_=sbias[:, 0:nw, :], func=AF.Exp))(),
                                        reads=[sbB], writes=[pB[pi]])
                                bank = 2 * half + 1
                                K.op("act", (lambda pi=pi, bank=bank, nw=nw: lambda h: h.activation(
                                    out=pT[pi][:, nw * 128:(nw + 2) * 128], in_=ps[bank][:, 128:384], func=AF.Exp))(),
                                    reads=[psB[bank]], writes=[pB[pi]])
                                nk = nw + 2
                                for a in range(nk):
                                    if a < nw:
                                        lhs = (lambda hh=hh, sl=wk[a] - k0: vbw[:, sl, hh, :])
                                    else:
                                        lhs = (lambda a=a, hh=hh, nw=nw: vbc[:, a - nw, hh, :])
                                    K.op("pe", (lambda a=a, lhs=lhs, half=half, pi=pi, q0=q0, nk=nk: lambda h: h.matmul(
                                        ps[4 + half][0:65, q0:q0 + 128], lhsT=lhs(), rhs=pT[pi][:, a * 128:(a + 1) * 128],
                                        start=(a == 0), stop=(a == nk - 1)))(), reads=[ib, resB, pB[pi]], writes=[psB[4 + half]])
                        normalize(4, 6 + 2 * c)
                        normalize(5, 6 + 2 * c + 1)
                    for c in range(2):
                        w0 = smp[:, l, 124 + c * 3:125 + c * 3]
                        w1 = smp[:, l, 125 + c * 3:126 + c * 3]
                        w2 = smp[:, l, 126 + c * 3:127 + c * 3]
                        K.op("pool", (lambda c=c, w0=w0, n=n: lambda h: h.tensor_scalar(
                            out=cacc[:, 0, 0:n], in0=zt[:, c, 0:n], scalar1=w0, scalar2=None, op0=ALU.mult))(),
                            reads=[ib, cB], writes=[caB])
                        for (wt_, off) in ((w1, 1), (w2, 2)):
                            K.op("pool", (lambda c=c, wt_=wt_, off=off, n=n: lambda h: h.tensor_scalar(
                                out=cacc[:, 1, 0:n], in0=zt[:, c, off:n + off], scalar1=wt_, scalar2=None, op0=ALU.mult))(),
                                reads=[ib, cB], writes=[caB])
                            K.op("pool", (lambda n=n: lambda h: h.tensor_tensor(
                                out=cacc[:, 0, 0:n], in0=cacc[:, 0, 0:n], in1=cacc[:, 1, 0:n], op=ALU.add))(),
                                reads=[caB], writes=[caB])
                        K.op("pool", (lambda c=c, n=n: lambda h: h.tensor_tensor(
                            out=ocT[:, c, 0:n], in0=cacc[:, 0, 0:n], in1=cbt[:, c, 0:n], op=ALU.mult))(),
                            reads=[ib, caB], writes=[ocB])
                    for hf in range(n // HP):
                        o0 = hf * HP
                        K.dma("sp", hT[:], Hin[:, c0 + o0:c0 + o0 + HP].rearrange("(k p) t -> p k t", p=128), writes=hb)
                        for j in range(KD):
                            bank = 6 + j % 2
                            for hh in range(12):
                                K.op("pe", (lambda j=j, hh=hh, bank=bank, o0=o0: lambda h: h.matmul(
                                    ps[bank][:, 0:HP], lhsT=woh[:, hh, j * 128:(j + 1) * 128], rhs=oT[:, hh, o0:o0 + HP],
                                    start=(hh == 0), stop=False))(), reads=[resB, oB[hh]], writes=[psB[bank]])
                            for k in range(2):
                                K.op("pe", (lambda j=j, k=k, bank=bank, o0=o0: lambda h: h.matmul(
                                    ps[bank][:, 0:HP], lhsT=woc[:, k, j * 128:(j + 1) * 128], rhs=ocT[:, k, o0:o0 + HP],
                                    start=False, stop=(k == 1)))(), reads=[resB, ocB], writes=[psB[bank]])
                            K.op("act", (lambda j=j, bank=bank: lambda h: h.activation(out=ysb[:, j, :], in_=ps[bank][:, 0:HP], func=AF.Copy))(),
                                 reads=[psB[bank]], writes=[yB[j]])
                            K.op("act", (lambda j=j, bank=bank: lambda h: h.activation(out=sq[:, j, :], in_=ps[bank][:, 0:HP], func=AF.Square))(),
                                 reads=[psB[bank]], writes=[sqB[j]])
                        for k in range(KD):
                            K.op("pe", (lambda k=k: lambda h: h.matmul(ps[0][:, 0:HP], lhsT=onesm[:], rhs=sq[:, k, :],
                                                                       start=(k == 0), stop=(k == KD - 1)))(),
                                 reads=[sqB[k], cB], writes=[psB[0]])
                        K.op("act", lambda h: h.activation(out=rstd[:], in_=ps[0][:, 0:HP], func=AF.Sqrt, bias=EPS, scale=1.0),
                             reads=[psB[0]], writes=[rB])
                        K.op("dve", lambda h: h.reciprocal(out=rstd[:], in_=rstd[:]), reads=[rB], writes=[rB])
                        for j in range(KD):
                            tb = j % 2
                            K.op("dve", (lambda j=j, tb=tb, X=X: lambda h: h.scalar_tensor_tensor(
                                out=tmp[tb][:], in0=ysb[:, j, :], scalar=par[:, l, X, 5, j:j + 1], in1=rstd[:],
                                op0=ALU.mult, op1=ALU.mult))(), reads=[yB[j], rB], writes=[tB[tb]])
                            K.op("pool", (lambda j=j, tb=tb: lambda h: h.tensor_tensor(
                                out=hT[:, j, :], in0=hT[:, j, :], in1=tmp[tb][:], op=ALU.add))(),
                                reads=[tB[tb]], writes=[hb[j]])
                        K.dma("sp", Hout[:, c0 + o0:c0 + o0 + HP].rearrange("(k p) t -> p k t", p=128), hT[:], reads=hb)
                K.flush()

        phase_ffn(0, 0, xT, HA, "x", "HA", True)
        phase_proj(0, HA, "HA")
        phase_attn(0, HA, HB, "HA", "HB", True)
        phase_ffn(0, 1, HB, HA, "HB", "HA", True)
        phase_ffn(1, 0, HA, HB, "HA", "HB", True)
        phase_proj(1, HB, "HB")
        phase_attn(1, HB, HA, "HB", "HA", False)
        phase_ffn(1, 1, HA, None, "HA", "none", False, final_out=outT)
    return nc


def _rope_tables(S):
    pos = np.arange(S, dtype=np.int32)
    row = (pos // GW).astype(np.float32)
    col = (pos % GW).astype(np.float32)
    freq = (1.0 / (np.float32(10000.0) ** (np.arange(16, dtype=np.float32) / np.float32(16)))).astype(np.float32)
    C = np.zeros((128, S), np.float32)
    Sg = np.zeros((128, S), np.float32)
    for p in range(128):
        d = p % 64
        a, j, f = d // 32, (d // 16) % 2, d % 16
        ang = ((row if a == 0 else col) * freq[f]).astype(np.float32)
        C[p] = np.cos(ang)
        Sg[p] = np.sin(ang) * (-1.0 if j == 0 else 1.0)
    return C, Sg


def _bias_mats(rpb_l, qt, kts, rows):
    out = np.full((128, len(kts), 6, 128), NEG, np.float32)
    j = np.arange(128)
    for a, kt in enumerate(kts):
        kr = 2 * kt + j // 64
        kc = j % 64
        qr = 2 * qt + j // 64
        qc = j % 64
        rs = np.clip(qr - 4, 0, rows - 8)
        cs = np.clip(qc - 8, 0, GW - 16)
        okr = (kr[:, None] >= rs[None, :]) & (kr[:, None] < rs[None, :] + 8)
        okc = (kc[:, None] >= cs[None, :]) & (kc[:, None] < cs[None, :] + 16)
        ok = okr & okc
        dr = np.clip(kr[:, None] - qr[None, :] + 7, 0, 14)
        dc = np.clip(kc[:, None] - qc[None, :] + 15, 0, 30)
        for hh in range(6):
            vals = rpb_l[hh][dr, dc]
            out[:, a, hh, :] = np.where(ok, vals, np.float32(NEG))
    return out


def _prep_inputs(inp, S):
    x, c, ctx, c_ctx = inp["x"], inp["c"], inp["ctx"], inp["c_ctx"]
    B = x.shape[0]
    rows = S // GW
    NQT = S // 128
    w_in = inp["w_in"]
    perm = np.arange(64) ^ 16
    cols = []
    qa_h = lambda h: np.arange(h * 64, (h + 1) * 64)
    for cc in range(3):
        cols += [qa_h(cc), qa_h(cc + 3)]
    for cc in range(3):
        cols += [qa_h(cc)[perm], qa_h(cc + 3)[perm]]
    ka0 = 768
    cols += [np.arange(ka0, ka0 + 64), np.arange(ka0 + 64, ka0 + 128)]
    cols += [np.arange(ka0, ka0 + 64)[perm], np.arange(ka0 + 64, ka0 + 128)[perm]]
    cols += [np.arange(384, 768)]
    cols += [np.arange(768 + 256, 768 + 256 + 384)]
    cx0 = 768 + 1024
    cols += [np.arange(cx0, cx0 + 768)]
    cols += [np.arange(768 + 128, 768 + 256)]
    cols += [np.arange(768 + 256 + 384, 768 + 1024)]
    cols = np.concatenate(cols)
    assert cols.shape[0] == NW
    w_inx = np.ascontiguousarray(w_in[:, :, cols])
    smallp = np.zeros((2, 128, NS), np.float32)
    p = np.arange(128)
    for l in range(2):
        smallp[l, :, 0:72] = inp["b_ada"][l].reshape(72, 128).T
        smallp[l, :, 72:120] = inp["norm_g"][l].reshape(6, 8, 128).transpose(2, 0, 1).reshape(128, 48)
        smallp[l, :, 120] = inp["qk_g"][l, 0][p % 64]
        smallp[l, :, 121] = inp["qk_g"][l, 0][(p % 64) ^ 16]
        smallp[l, :, 122] = inp["qk_g"][l, 1][p % 64]
        smallp[l, :, 123] = inp["qk_g"][l, 1][(p % 64) ^ 16]
        smallp[l, :, 124:130] = inp["conv_w"][l].reshape(3, 2, 128).transpose(2, 1, 0).reshape(128, 6)
    C, Sg = _rope_tables(S)
    biasI = np.zeros((2, 128, 5 * 6 * 128), np.float32)
    biasE = np.zeros((2, 128, 4 * 4 * 6 * 128), np.float32)
    for l in range(2):
        biasI[l] = _bias_mats(inp["rpb"][l], 2, [0, 1, 2, 3, 4], rows).reshape(128, -1)
        be = [_bias_mats(inp["rpb"][l], qt, kts, rows) for qt, kts in
              ((0, [0, 1, 2, 3]), (1, [0, 1, 2, 3]), (NQT - 2, list(range(NQT - 4, NQT))), (NQT - 1, list(range(NQT - 4, NQT))))]
        biasE[l] = np.stack(be, axis=1).reshape(128, -1)
    shared = dict(smallp=smallp, w_ada=np.ascontiguousarray(inp["w_ada"]), w_inx=w_inx, w_o=np.ascontiguousarray(inp["w_o"]),
                  ffn_wi=np.ascontiguousarray(inp["ffn_wi"]), ffn_wo=np.ascontiguousarray(inp["ffn_wo"]),
                  ropeC=C, ropeS=Sg, biasI=biasI, biasE=biasE)
    maps = []
    for b in range(B):
        m = dict(shared)
        m["xT"] = np.ascontiguousarray(np.concatenate([x[b].T, ctx[b].T], axis=1))
        cv = np.stack([c[b].reshape(8, 128).T, c_ctx.reshape(8, 128).T], axis=-1)
        m["cvec"] = np.ascontiguousarray(cv.astype(np.float32))
        maps.append(m)
    return maps


_CACHE = {}


def run(inp, debug=False):
    inp = {k: np.asarray(v, dtype=np.float32) for k, v in inp.items()}
    S = inp["x"].shape[1]
    B = inp["x"].shape[0]
    key = (S, debug)
    if key not in _CACHE:
        _CACHE[key] = build_program(S, debug)
    nc = _CACHE[key]
    maps = _prep_inputs(inp, S)
    res = run_bass_kernel_spmd(nc, maps, core_ids=list(range(B)))
    out = np.stack([np.ascontiguousarray(r["outT"].T) for r in res.results], axis=0)
    return out.astype(np.float32), res


def kernel(**inputs):
    out, _ = run(inputs)
    return out
```

```python
import numpy as np
from contextlib import ExitStack
import concourse.bass as bass
import concourse.mybir as mybir
from concourse.bass_utils import run_bass_kernel_spmd

F32 = mybir.dt.float32
BF16 = mybir.dt.bfloat16
AF = mybir.ActivationFunctionType
ALU = mybir.AluOpType

D = 1024
KD = 8
FFN = 2816
KF = 22
LC = 256
GW = 64
NEG = -30000.0
EPS = 1e-6
NS = 132
NW = 3072
TF = 256
NDMA = 24
SAME_ENG_SYNC = True


class Buf:
    __slots__ = ("name", "w", "rc", "rd")

    def __init__(self, name):
        self.name = name
        self.w = []
        self.rc = {}
        self.rd = []


class Eng:
    def __init__(self, name, sem):
        self.name = name
        self.sem = sem
        self.base = 0
        self.reset()

    def reset(self):
        self.items = []
        self.marks = []
        self.waited = {}
        self.vals = []


class KB:
    HM = {"pe": "tensor", "act": "scalar", "dve": "vector", "pool": "gpsimd", "sp": "sync"}

    def __init__(self, nc, es):
        self.nc = nc
        self.engs = {}
        for n in self.HM:
            self.engs[n] = Eng(n, es.enter_context(nc.semaphore("s_" + n)))
        self.dsem = [es.enter_context(nc.semaphore("d%d" % i)) for i in range(NDMA)]
        self.dcnt = [0] * NDMA
        self.drr = 0
        self.bufs = []

    def B(self, name):
        b = Buf(name)
        self.bufs.append(b)
        return b

    def Bs(self, name, n):
        return [self.B("%s%d" % (name, i)) for i in range(n)]

    def _wait(self, eng, tok):
        if tok[0] == "c":
            pe, seq = tok[1], tok[2]
            if pe is eng and (eng.name == "pe" or eng.name == "sp" or not SAME_ENG_SYNC):
                return
            if eng.waited.get(pe.name, -1) >= seq:
                return
            eng.waited[pe.name] = seq
            pe.marks[seq] = True
            eng.items.append(("wc", pe, seq))
        else:
            idx, val = tok[1], tok[2]
            if eng.waited.get(("d", idx), 0) >= val:
                return
            eng.waited[("d", idx)] = val
            eng.items.append(("wd", idx, val))

    def _deps(self, eng, reads, writes):
        for b in reads:
            for t in b.w:
                self._wait(eng, t)
        for b in writes:
            for t in b.w:
                self._wait(eng, t)
            for n, s in b.rc.items():
                self._wait(eng, ("c", self.engs[n], s))
            for t in b.rd:
                self._wait(eng, t)

    def op(self, en, fn, reads=(), writes=()):
        eng = self.engs[en]
        self._deps(eng, reads, writes)
        seq = len(eng.marks)
        eng.marks.append(False)
        eng.items.append(("op", fn, seq))
        tok = ("c", eng, seq)
        for b in writes:
            b.w = [tok]
            b.rc = {}
            b.rd = []
        for b in reads:
            b.rc[en] = seq

    def dma(self, en, out, in_, reads=(), writes=()):
        eng = self.engs[en]
        self._deps(eng, reads, writes)
        idx = self.drr
        self.drr = (self.drr + 1) % NDMA
        self.dcnt[idx] += 16
        tok = ("d", idx, self.dcnt[idx])
        eng.items.append(("dma", out, in_, idx))
        for b in writes:
            b.w = [tok]
            b.rc = {}
            b.rd = []
        for b in reads:
            b.rd.append(tok)

    def flush(self):
        engs = self.engs
        for e in engs.values():
            if e.marks:
                e.marks[-1] = True
        for f in engs.values():
            for e in engs.values():
                if e is not f and e.marks:
                    f.items.append(("wc", e, len(e.marks) - 1))
            for i in range(NDMA):
                if self.dcnt[i] > 0:
                    f.items.append(("wd", i, self.dcnt[i]))
        for e in engs.values():
            v = e.base
            vals = []
            for m in e.marks:
                if m:
                    v += 1
                vals.append(v)
            e.vals = vals
            e.newbase = v
        dsem = self.dsem
        with self.nc.Block() as block:
            for name, e in engs.items():
                def mk(e):
                    def body(h):
                        for it in e.items:
                            k = it[0]
                            if k == "op":
                                ins = it[1](h)
                                if e.marks[it[2]]:
                                    ins.then_inc(e.sem, 1)
                            elif k == "wc":
                                h.wait_ge(it[1].sem, it[1].vals[it[2]])
                            elif k == "wd":
                                h.wait_ge(dsem[it[1]], it[2])
                            else:
                                h.dma_start(out=it[1], in_=it[2], allow_slow_non_contiguous=True).then_inc(dsem[it[3]], 16)
                    return body
                getattr(block, self.HM[name])(mk(e))
        for e in engs.values():
            e.base = e.newbase
            e.reset()
        for b in self.bufs:
            b.w = []
            b.rc = {}
            b.rd = []


def build_program(S, debug=False):
    NT = S + LC
    NQT = S // 128
    NKT = NT // 128
    nc = bass.Bass("TRN2", target_bir_lowering=False)
    dk = "ExternalOutput" if debug else "Internal"

    def din(name, shape, dt=F32):
        return nc.dram_tensor(name, list(shape), dt, kind="ExternalInput").ap()

    def dscr(name, shape, dt=F32):
        return nc.dram_tensor(name, list(shape), dt, kind=dk).ap()

    xT = din("xT", [D, NT])
    cvec = din("cvec", [128, KD, 2])
    smallp = din("smallp", [2, 128, NS])
    w_ada = din("w_ada", [2, D, 9 * D])
    w_inx = din("w_inx", [2, D, NW])
    w_o = din("w_o", [2, D, D])
    ffn_wi = din("ffn_wi", [2, 2, D, 2 * FFN])
    ffn_wo = din("ffn_wo", [2, 2, FFN, D])
    ropeC = din("ropeC", [128, S])
    ropeS = din("ropeS", [128, S])
    biasI = din("biasI", [2, 128, 5 * 6 * 128])
    biasE = din("biasE", [2, 128, 4 * 4 * 6 * 128])
    outT = nc.dram_tensor("outT", [D, S], F32, kind="ExternalOutput").ap()

    HA = dscr("HA", [D, NT])
    HB = dscr("HB", [D, NT])
    QA = dscr("QA", [128, 3, NT], BF16)
    KA = dscr("KA", [128, NT], BF16)
    QB = dscr("QB", [128, 3, NT], BF16)
    KBd = dscr("KBd", [128, 3, NT], BF16)
    VV = dscr("VV", [NT, 8, 65], BF16)
    ZZ = dscr("ZZ", [128, 2, NT + 4])
    CBd = dscr("CBd", [128, 2, NT])

    es = ExitStack()
    with es:
        K = KB(nc, es)

        uid = [0]

        def sb(name, shape, dt=F32, stack=es):
            uid[0] += 1
            return stack.enter_context(nc.sbuf_tensor("%s_%d" % (name, uid[0]), list(shape), dt))

        psall = es.enter_context(nc.psum_tensor("psall", [128, 4096], F32))
        ps = [psall[:, i * 512:(i + 1) * 512] for i in range(8)]
        psB = K.Bs("psB", 8)
        par = sb("par", [128, 2, 2, 9, KD])
        smp = sb("smp", [128, 2, NS])
        qsc = sb("qsc", [128, 2, 2])
        onesm = sb("onesm", [128, 128], BF16)
        bd64 = sb("bd64", [128, 128], BF16)
        onesb = sb("onesb", [128, 64], BF16)
        cB = K.B("consts")
        dramB = {}

        def DB(name, i):
            key = (name, i)
            if key not in dramB:
                dramB[key] = K.B("dram_%s_%d" % (name, i))
            return dramB[key]

        with ExitStack() as ph:
            cv = sb("cv", [128, KD, 2], F32, ph)
            scv = sb("scv", [128, KD, 2], F32, ph)
            wa = [sb("wa%d" % i, [128, KD, 1024], F32, ph) for i in range(2)]
            waB = K.Bs("waB", 2)
            mods = sb("mods", [128, 2, 72], F32, ph)
            cvB, scB, modB = K.B("cv"), K.B("scv"), K.B("mods")
            K.op("dve", lambda h: h.memset(onesm[:], 1.0 / 1024), writes=[cB])
            K.op("dve", lambda h: h.memset(bd64[:], 0.0), writes=[cB])
            K.op("dve", lambda h: h.memset(bd64[0:64, 0:64], 1.0 / 64), writes=[cB])
            K.op("dve", lambda h: h.memset(bd64[64:128, 64:128], 1.0 / 64), writes=[cB])
            K.op("dve", lambda h: h.memset(onesb[:], 1.0), writes=[cB])
            K.dma("sp", cv[:], cvec[:, :, :], writes=[cvB])
            K.dma("sp", smp[:], smallp.rearrange("l p n -> p l n"), writes=[cB])
            K.op("act", lambda h: h.activation(out=scv[:], in_=cv[:], func=AF.Silu), reads=[cvB], writes=[scB])
            it = 0
            for l in range(2):
                for i in range(9):
                    b = it % 2
                    it += 1
                    K.dma("sp", wa[b][:], w_ada[l].rearrange("(k p) n -> p k n", p=128)[:, :, i * 1024:(i + 1) * 1024],
                          writes=[waB[b]])
                    for j in range(8):
                        idx = i * 8 + j
                        for k in range(KD):
                            K.op("pe", (lambda b=b, j=j, k=k, idx=idx: lambda h: h.matmul(
                                ps[0][:, 2 * idx:2 * idx + 2], lhsT=wa[b][:, k, j * 128:(j + 1) * 128], rhs=scv[:, k, :],
                                start=(k == 0), stop=(k == KD - 1)))(), reads=[waB[b], scB], writes=[psB[0]])
                pv = ps[0][:, 0:144].rearrange("p (n t) -> p n t", t=2)
                for X in range(2):
                    K.op("dve", (lambda X=X, l=l, pv=pv: lambda h: h.tensor_tensor(
                        out=mods[:, X, :], in0=pv[:, :, X], in1=smp[:, l, 0:72], op=ALU.add))(),
                        reads=[psB[0], cB], writes=[modB])

                    def m(i, X=X):
                        return mods[:, X, i * 8:(i + 1) * 8]

                    def g(gi, l=l):
                        return smp[:, l, 72 + gi * 8:72 + (gi + 1) * 8]
                    for (kind, mi, gi, wres) in ((0, 1, 0, None), (3, 4, 2, None), (6, 7, 4, None)):
                        K.op("dve", (lambda kind=kind, mi=mi, gi=gi, X=X, l=l, m=m, g=g: lambda h: h.scalar_tensor_tensor(
                            out=par[:, l, X, kind, :], in0=m(mi), scalar=1.0, in1=g(gi), op0=ALU.add, op1=ALU.mult))(),
                            reads=[modB, cB], writes=[cB])
                    for (kind, mi) in ((1, 0), (4, 3), (7, 6)):
                        K.op("dve", (lambda kind=kind, mi=mi, X=X, l=l, m=m: lambda h: h.tensor_copy(
                            out=par[:, l, X, kind, :], in_=m(mi)))(), reads=[modB], writes=[cB])
                    for (kind, mi, gi, wres) in ((2, 2, 1, 0.5), (5, 5, 3, 1.0), (8, 8, 5, 0.5)):
                        K.op("dve", (lambda kind=kind, mi=mi, gi=gi, wres=wres, X=X, l=l, m=m, g=g: lambda h: h.scalar_tensor_tensor(
                            out=par[:, l, X, kind, :], in0=m(mi), scalar=wres, in1=g(gi), op0=ALU.mult, op1=ALU.mult))(),
                            reads=[modB, cB], writes=[cB])
                K.op("dve", (lambda l=l: lambda h: h.tensor_scalar(
                    out=qsc[:, l, :], in0=smp[:, l, 120:122], scalar1=0.125, scalar2=None, op0=ALU.mult))(),
                    reads=[cB], writes=[cB])
            K.flush()

        def phase_ffn(l, which, Hin, Hout, hin_name, hout_name, do_ctx, final_out=None):
            kA, kB_, kG = (0, 1, 2) if which == 0 else (6, 7, 8)
            with ExitStack() as ph:
                wi = sb("wi", [128, KD, 2 * FFN], BF16, ph)
                wo = sb("wo", [128, KF, D], BF16, ph)
                wiB = K.Bs("wiB", KD)
                woB = K.Bs("woB", 2)
                hT = [sb("hT%d" % i, [128, KD, TF], F32, ph) for i in range(2)]
                hB = [K.Bs("hB%d_" % i, KD) for i in range(2)]
                u2 = [sb("u%d" % i, [128, KD, TF], BF16, ph) for i in range(2)]
                u2B = [K.Bs("uB%d_" % i, KD) for i in range(2)]
                rstd2 = sb("rstd2", [128, TF], F32, ph)
                r2B = K.B("rstd2")
                act = sb("actb", [128, KF, TF], BF16, ph)
                actB = K.Bs("actB", KF)
                ysb = sb("ysb", [128, KD, TF], F32, ph)
                yB = K.Bs("yB", KD)
                rstd = sb("rstd", [128, TF], F32, ph)
                rB = K.B("rstd")
                tmp = [sb("tmp%d" % i, [128, TF], F32, ph) for i in range(2)]
                tB = K.Bs("tmpB", 2)
                sg = [sb("sg%d" % i, [128, TF], F32, ph) for i in range(2)]
                sgB = K.Bs("sgB", 2)
                for k in range(KD):
                    K.dma("pool", wi[:, k, :], ffn_wi[l, which, k * 128:(k + 1) * 128, :], writes=[wiB[k]])
                wov = ffn_wo[l, which].rearrange("(m p) n -> p m n", p=128)
                K.dma("pool", wo[:, 0:11, :], wov[:, 0:11, :], writes=[woB[0]])
                K.dma("pool", wo[:, 11:22, :], wov[:, 11:22, :], writes=[woB[1]])
                tiles = [(t * TF, 0) for t in range(S // TF)]
                if do_ctx:
                    tiles.append((S, 1))

                def load(i):
                    c0, X = tiles[i]
                    b = i % 2
                    K.dma("sp", hT[b][:], Hin[:, c0:c0 + TF].rearrange("(k p) t -> p k t", p=128),
                          reads=[DB(hin_name, c0 // TF)], writes=hB[b])

                def stats(ub, ubB, rs, rsB):
                    for k in range(KD):
                        K.op("pe", (lambda k=k, ub=ub: lambda h: h.matmul(ps[0][:, 0:TF], lhsT=onesm[:], rhs=ub[:, k, :],
                                                                          start=(k == 0), stop=(k == KD - 1)))(),
                             reads=[ubB[k], cB], writes=[psB[0]])
                    K.op("act", (lambda rs=rs: lambda h: h.activation(out=rs[:], in_=ps[0][:, 0:TF], func=AF.Sqrt, bias=EPS, scale=1.0))(),
                         reads=[psB[0]], writes=[rsB])
                    K.op("dve", (lambda rs=rs: lambda h: h.reciprocal(out=rs[:], in_=rs[:]))(), reads=[rsB], writes=[rsB])

                def pre_sq(i, k):
                    b = i % 2
                    K.op("act", (lambda k=k, b=b: lambda h: h.activation(out=u2[b][:, k, :], in_=hT[b][:, k, :], func=AF.Square))(),
                         reads=[hB[b][k]], writes=[u2B[b][k]])

                def pre_u(i, k):
                    b = i % 2
                    X = tiles[i][1]
                    tb = k % 2
                    K.op("dve", (lambda k=k, tb=tb, b=b, X=X: lambda h: h.scalar_tensor_tensor(
                        out=tmp[tb][:], in0=hT[b][:, k, :], scalar=par[:, l, X, kA, k:k + 1], in1=rstd[:],
                        op0=ALU.mult, op1=ALU.mult))(), reads=[hB[b][k], rB], writes=[tB[tb]])
                    K.op("act", (lambda k=k, tb=tb, b=b, X=X: lambda h: h.activation(
                        out=u2[b][:, k, :], in_=tmp[tb][:], func=AF.Identity, bias=par[:, l, X, kB_, k:k + 1], scale=1.0))(),
                        reads=[tB[tb]], writes=[u2B[b][k]])

                load(0)
                for k in range(KD):
                    pre_sq(0, k)
                stats(u2[0], u2B[0], rstd, rB)
                for k in range(KD):
                    pre_u(0, k)
                for i, (c0, X) in enumerate(tiles):
                    b = i % 2
                    nxt = i + 1 < len(tiles)
                    if nxt:
                        load(i + 1)
                    h_, hb = hT[b], hB[b]
                    u, uB = u2[b], u2B[b]
                    for m in range(KF):
                        pb = m % 2
                        for half, bank in ((0, 1 + pb), (1, 3 + pb)):
                            for k in range(KD):
                                c_ = half * FFN + m * 128
                                K.op("pe", (lambda k=k, c_=c_, bank=bank, u=u: lambda h: h.matmul(
                                    ps[bank][:, 0:TF], lhsT=wi[:, k, c_:c_ + 128], rhs=u[:, k, :],
                                    start=(k == 0), stop=(k == KD - 1)))(), reads=[wiB[k], uB[k]], writes=[psB[bank]])
                        K.op("act", (lambda pb=pb: lambda h: h.activation(out=sg[pb][:], in_=ps[1 + pb][:, 0:TF], func=AF.Silu))(),
                             reads=[psB[1 + pb]], writes=[sgB[pb]])
                        K.op("dve", (lambda pb=pb, m=m: lambda h: h.tensor_tensor(
                            out=act[:, m, :], in0=sg[pb][:], in1=ps[3 + pb][:, 0:TF], op=ALU.mult))(),
                            reads=[sgB[pb], psB[3 + pb]], writes=[actB[m]])
                        if nxt and 6 <= m < 6 + KD:
                            pre_sq(i + 1, m - 6)
                    if nxt:
                        stats(u2[1 - b], u2B[1 - b], rstd, rB)
                    for j in range(KD):
                        bank = 5 + j % 2
                        for m in range(KF):
                            K.op("pe", (lambda j=j, m=m, bank=bank: lambda h: h.matmul(
                                ps[bank][:, 0:TF], lhsT=wo[:, m, j * 128:(j + 1) * 128], rhs=act[:, m, :],
                                start=(m == 0), stop=(m == KF - 1)))(), reads=[woB[m // 11], actB[m]], writes=[psB[bank]])
                        if nxt:
                            pre_u(i + 1, j)
                        K.op("act", (lambda j=j, bank=bank: lambda h: h.activation(out=ysb[:, j, :], in_=ps[bank][:, 0:TF], func=AF.Copy))(),
                             reads=[psB[bank]], writes=[yB[j]])
                        K.op("act", (lambda j=j, bank=bank, u=u: lambda h: h.activation(out=u[:, j, :], in_=ps[bank][:, 0:TF], func=AF.Square))(),
                             reads=[psB[bank]], writes=[uB[j]])
                    stats(u, uB, rstd2, r2B)
                    for j in range(KD):
                        tb = j % 2
                        K.op("dve", (lambda j=j, tb=tb, X=X: lambda h: h.scalar_tensor_tensor(
                            out=tmp[tb][:], in0=ysb[:, j, :], scalar=par[:, l, X, kG, j:j + 1], in1=rstd2[:],
                            op0=ALU.mult, op1=ALU.mult))(), reads=[yB[j], r2B], writes=[tB[tb]])
                        K.op("pool", (lambda j=j, tb=tb, h_=h_: lambda h: h.tensor_tensor(
                            out=h_[:, j, :], in0=h_[:, j, :], in1=tmp[tb][:], op=ALU.add))(),
                            reads=[tB[tb]], writes=[hb[j]])
                    if final_out is not None and X == 0:
                        K.dma("sp", final_out[:, c0:c0 + TF].rearrange("(k p) t -> p k t", p=128), h_[:], reads=hb)
                    else:
                        K.dma("sp", Hout[:, c0:c0 + TF].rearrange("(k p) t -> p k t", p=128), h_[:],
                              reads=hb, writes=[DB(hout_name, c0 // TF)])
                K.flush()

        def phase_proj(l, Hin, hin_name):
            TP = 512
            with ExitStack() as ph:
                win = sb("win", [128, KD, NW], BF16, ph)
                winB = K.Bs("winB", KD)
                hT = [sb("hT%d" % i, [128, KD, TP], F32, ph) for i in range(2)]
                hB = [K.Bs("hB%d_" % i, KD) for i in range(2)]
                rc = [sb("rc%d" % i, [128, TP], F32, ph) for i in range(2)]
                rs_ = [sb("rs%d" % i, [128, TP], F32, ph) for i in range(2)]
                rcB = K.Bs("rcB", 2)
                u = sb("u", [128, KD, TP], BF16, ph)
                uB = K.Bs("uB", KD)
                rstd = sb("rstd", [128, TP], F32, ph)
                rB = K.B("rstd")
                tmp = [sb("tmp%d" % i, [128, TP], F32, ph) for i in range(2)]
                tB = K.Bs("tmpB", 2)
                sqp = sb("sqp", [128, TP], BF16, ph)
                sqB = K.B("sqp")
                rq = sb("rq", [128, TP], F32, ph)
                rqB = K.B("rq")
                t1 = sb("t1", [128, TP], F32, ph)
                t2 = sb("t2", [128, TP], F32, ph)
                t1B, t2B = K.B("t1"), K.B("t2")
                qa = sb("qa", [128, 3, TP], BF16, ph)
                ka = sb("ka", [128, TP], BF16, ph)
                qb = sb("qb", [128, 3, TP], BF16, ph)
                kb = sb("kb", [128, 3, TP], BF16, ph)
                zz = sb("zz", [128, 2, TP], F32, ph)
                cbs = sb("cbs", [128, 2, TP], F32, ph)
                cxs = sb("cxs", [128, TP], F32, ph)
                vv = sb("vv", [128, 4, 8, 65], BF16, ph)
                zer = sb("zer", [128, 2, 2], F32, ph)
                qaB, kaB, qbB, kbB, zzB, cbB, cxB, vvB, zeB = (K.B(n) for n in ("qa", "ka", "qb", "kb", "zz", "cbs", "cxs", "vv", "zer"))
                for k in range(KD):
                    K.dma("pool", win[:, k, :], w_inx[l, k * 128:(k + 1) * 128, :], writes=[winB[k]])
                K.op("dve", lambda h: h.memset(vv[:], 1.0), writes=[vvB])
                K.op("dve", lambda h: h.memset(zer[:], 0.0), writes=[zeB])
                K.dma("sp", ZZ[:, :, 0:1], zer[:, :, 0:1], reads=[zeB])
                K.dma("sp", ZZ[:, :, S + 1:S + 3], zer[:, :, 0:2], reads=[zeB])
                K.dma("sp", ZZ[:, :, NT + 3:NT + 4], zer[:, :, 0:1], reads=[zeB])
                tiles = [(t * TP, TP, 0) for t in range(S // TP)] + [(S, LC, 1)]

                def load(i):
                    c0, n, X = tiles[i]
                    b = i % 2
                    K.dma("sp", hT[b][:, :, 0:n], Hin[:, c0:c0 + n].rearrange("(k p) t -> p k t", p=128),
                          reads=[DB(hin_name, j) for j in range(c0 // TF, (c0 + n) // TF)], writes=hB[b])
                    if X == 0:
                        K.dma("sp", rc[b][:], ropeC[:, c0:c0 + n], writes=[rcB[b]])
                        K.dma("sp", rs_[b][:], ropeS[:, c0:c0 + n], writes=[rcB[b]])
                load(0)
                for i, (c0, n, X) in enumerate(tiles):
                    b = i % 2
                    if i + 1 < len(tiles):
                        load(i + 1)
                    h_, hb = hT[b], hB[b]
                    for k in range(KD):
                        K.op("act", (lambda k=k, h_=h_, n=n: lambda h: h.activation(out=u[:, k, 0:n], in_=h_[:, k, 0:n], func=AF.Square))(),
                             reads=[hb[k]], writes=[uB[k]])
                    for k in range(KD):
                        K.op("pe", (lambda k=k, n=n: lambda h: h.matmul(ps[0][:, 0:n], lhsT=onesm[:], rhs=u[:, k, 0:n],
                                                                       start=(k == 0), stop=(k == KD - 1)))(),
                             reads=[uB[k]], writes=[psB[0]])
                    K.op("act", (lambda n=n: lambda h: h.activation(out=rstd[:, 0:n], in_=ps[0][:, 0:n], func=AF.Sqrt, bias=EPS, scale=1.0))(),
                         reads=[psB[0]], writes=[rB])
                    K.op("dve", (lambda n=n: lambda h: h.reciprocal(out=rstd[:, 0:n], in_=rstd[:, 0:n]))(), reads=[rB], writes=[rB])
                    for k in range(KD):
                        tb = k % 2
                        K.op("dve", (lambda k=k, tb=tb, h_=h_, X=X, n=n: lambda h: h.scalar_tensor_tensor(
                            out=tmp[tb][:, 0:n], in0=h_[:, k, 0:n], scalar=par[:, l, X, 3, k:k + 1], in1=rstd[:, 0:n],
                            op0=ALU.mult, op1=ALU.mult))(), reads=[hb[k], rB], writes=[tB[tb]])
                        K.op("act", (lambda k=k, tb=tb, X=X, n=n: lambda h: h.activation(
                            out=u[:, k, 0:n], in_=tmp[tb][:, 0:n], func=AF.Identity, bias=par[:, l, X, 4, k:k + 1], scale=1.0))(),
                            reads=[tB[tb]], writes=[uB[k]])

                    def proj(g, bank):
                        for k in range(KD):
                            K.op("pe", (lambda k=k, g=g, bank=bank, n=n: lambda h: h.matmul(
                                ps[bank][:, 0:n], lhsT=win[:, k, g * 128:(g + 1) * 128], rhs=u[:, k, 0:n],
                                start=(k == 0), stop=(k == KD - 1)))(), reads=[winB[k], uB[k]], writes=[psB[bank]])

                    ng = [(c, 3 + c, ("q", 0), ("q", 1), qa[:, c, 0:n], qaB) for c in range(3)]
                    ng.append((6, 7, ("k", 122), ("k", 123), ka[:, 0:n], kaB))
                    for gi, (g0, g1, s0, s1, dst, dB) in enumerate(ng):
                        ba, bb_ = 1 + (gi % 2) * 2, 2 + (gi % 2) * 2
                        proj(g0, ba)
                        if X == 0:
                            proj(g1, bb_)
                        sc0 = qsc[:, l, 0:1] if s0[0] == "q" else smp[:, l, s0[1]:s0[1] + 1]
                        sc1 = qsc[:, l, 1:2] if s1[0] == "q" else smp[:, l, s1[1]:s1[1] + 1]
                        K.op("act", (lambda ba=ba, n=n: lambda h: h.activation(out=sqp[:, 0:n], in_=ps[ba][:, 0:n], func=AF.Square))(),
                             reads=[psB[ba]], writes=[sqB])
                        K.op("pe", (lambda n=n: lambda h: h.matmul(ps[5][:, 0:n], lhsT=bd64[:], rhs=sqp[:, 0:n], start=True, stop=True))(),
                             reads=[sqB, cB], writes=[psB[5]])
                        K.op("act", (lambda n=n: lambda h: h.activation(out=rq[:, 0:n], in_=ps[5][:, 0:n], func=AF.Sqrt, bias=EPS, scale=1.0))(),
                             reads=[psB[5]], writes=[rqB])
                        K.op("dve", (lambda n=n: lambda h: h.reciprocal(out=rq[:, 0:n], in_=rq[:, 0:n]))(), reads=[rqB], writes=[rqB])
                        if X == 0:
                            K.op("dve", (lambda ba=ba, sc0=sc0, n=n: lambda h: h.scalar_tensor_tensor(
                                out=t1[:, 0:n], in0=ps[ba][:, 0:n], scalar=sc0, in1=rq[:, 0:n], op0=ALU.mult, op1=ALU.mult))(),
                                reads=[psB[ba], rqB], writes=[t1B])
                            K.op("dve", (lambda bb_=bb_, sc1=sc1, n=n: lambda h: h.scalar_tensor_tensor(
                                out=t2[:, 0:n], in0=ps[bb_][:, 0:n], scalar=sc1, in1=rq[:, 0:n], op0=ALU.mult, op1=ALU.mult))(),
                                reads=[psB[bb_], rqB], writes=[t2B])
                            K.op("pool", (lambda b=b, n=n: lambda h: h.tensor_tensor(out=t1[:, 0:n], in0=t1[:, 0:n], in1=rc[b][:, 0:n], op=ALU.mult))(),
                                 reads=[rcB[b]], writes=[t1B])
                            K.op("pool", (lambda b=b, n=n: lambda h: h.tensor_tensor(out=t2[:, 0:n], in0=t2[:, 0:n], in1=rs_[b][:, 0:n], op=ALU.mult))(),
                                 reads=[rcB[b]], writes=[t2B])
                            K.op("pool", (lambda dst=dst, n=n: lambda h: h.tensor_tensor(out=dst, in0=t1[:, 0:n], in1=t2[:, 0:n], op=ALU.add))(),
                                 reads=[t1B, t2B], writes=[dB])
                        else:
                            K.op("dve", (lambda ba=ba, sc0=sc0, dst=dst, n=n: lambda h: h.scalar_tensor_tensor(
                                out=dst, in0=ps[ba][:, 0:n], scalar=sc0, in1=rq[:, 0:n], op0=ALU.mult, op1=ALU.mult))(),
                                reads=[psB[ba], rqB], writes=[dB])
                    for c in range(3):
                        bank = 6 + c % 2
                        proj(8 + c, bank)
                        K.op("act", (lambda c=c, bank=bank, n=n: lambda h: h.activation(out=qb[:, c, 0:n], in_=ps[bank][:, 0:n], func=AF.Copy, scale=0.125))(),
                             reads=[psB[bank]], writes=[qbB])
                    for c in range(3):
                        bank = 6 + (c + 1) % 2
                        proj(11 + c, bank)
                        K.op("act", (lambda c=c, bank=bank, n=n: lambda h: h.activation(out=kb[:, c, 0:n], in_=ps[bank][:, 0:n], func=AF.Copy))(),
                             reads=[psB[bank]], writes=[kbB])
                    for c in range(2):
                        proj(14 + c, 6)
                        proj(18 + c, 7)
                        K.op("act", (lambda n=n: lambda h: h.activation(out=cxs[:, 0:n], in_=ps[6][:, 0:n], func=AF.Copy))(),
                             reads=[psB[6]], writes=[cxB])
                        K.op("dve", (lambda c=c, n=n: lambda h: h.tensor_tensor(out=zz[:, c, 0:n], in0=cxs[:, 0:n], in1=ps[7][:, 0:n], op=ALU.mult))(),
                             reads=[cxB, psB[7]], writes=[zzB])
                        proj(16 + c, 6)
                        K.op("act", (lambda c=c, n=n: lambda h: h.activation(out=cbs[:, c, 0:n], in_=ps[6][:, 0:n], func=AF.Copy))(),
                             reads=[psB[6]], writes=[cbB])
                    for s in range(n // 128):
                        bank = 1 + s % 2
                        for k in range(KD):
                            K.op("pe", (lambda k=k, s=s, bank=bank: lambda h: h.matmul(
                                ps[bank][:, :], lhsT=u[:, k, s * 128:(s + 1) * 128], rhs=win[:, k, 2560:3072],
                                start=(k == 0), stop=(k == KD - 1)))(), reads=[winB[k], uB[k]], writes=[psB[bank]])
                        K.op("dve", (lambda s=s, bank=bank: lambda h: h.tensor_copy(
                            out=vv[:, s, :, 0:64], in_=ps[bank][:, :].rearrange("p (h d) -> p h d", d=64)))(),
                            reads=[psB[bank]], writes=[vvB])
                    tl = [DB("proj", j) for j in range(c0 // 128, (c0 + n) // 128)]
                    K.dma("sp", QA[:, :, c0:c0 + n], qa[:, :, 0:n], reads=[qaB], writes=tl)
                    K.dma("sp", KA[:, c0:c0 + n], ka[:, 0:n], reads=[kaB], writes=tl)
                    K.dma("sp", QB[:, :, c0:c0 + n], qb[:, :, 0:n], reads=[qbB], writes=tl)
                    K.dma("sp", KBd[:, :, c0:c0 + n], kb[:, :, 0:n], reads=[kbB], writes=tl)
                    zc0 = c0 + 1 if X == 0 else c0 + 3
                    K.dma("sp", ZZ[:, :, zc0:zc0 + n], zz[:, :, 0:n], reads=[zzB], writes=tl)
                    K.dma("sp", CBd[:, :, c0:c0 + n], cbs[:, :, 0:n], reads=[cbB], writes=tl)
                    K.dma("sp", VV[c0:c0 + n].rearrange("(s p) h d -> p s h d", p=128), vv[:, 0:n // 128], reads=[vvB], writes=tl)
                K.flush()

        def phase_attn(l, Hin, Hout, hin_name, hout_name, do_ctx):
            TP = 512
            HP = 256
            with ExitStack() as ph:
                kaT = sb("kaT", [128, NT], BF16, ph)
                vaT = sb("vaT", [128, NKT, 2, 65], BF16, ph)
                kbc = sb("kbc", [128, 3, LC], BF16, ph)
                vbc = sb("vbc", [128, 2, 6, 65], BF16, ph)
                woh = sb("woh", [64, 12, D], BF16, ph)
                woc = sb("woc", [128, 2, D], BF16, ph)
                bI = sb("bI", [128, 5, 6, 128], F32, ph)
                bE = sb("bE", [128, 2, 4, 2, 128], F32, ph)
                resB, bEB = K.B("resident"), K.B("bE")
                qa = [sb("qa%d" % i, [128, 3, TP], BF16, ph) for i in range(2)]
                qaB = K.Bs("qaB", 2)
                qb = sb("qb", [128, 3, TP], BF16, ph)
                kbw = sb("kbw", [128, 3, 1024], BF16, ph)
                vbw = sb("vbw", [128, 8, 6, 65], BF16, ph)
                zt = sb("zt", [128, 2, TP + 2], F32, ph)
                cbt = sb("cbt", [128, 2, TP], F32, ph)
                hT = sb("hT", [128, KD, HP], F32, ph)
                ib = K.B("inB")
                hb = K.Bs("hB", KD)
                pT = [sb("pT%d" % i, [128, 7 * 128], BF16, ph) for i in range(4)]
                pB = K.Bs("pB", 4)
                pA = [sb("pA%d" % i, [128, 2, TP], BF16, ph) for i in range(2)]
                pAB = K.Bs("pAB", 2)
                sbias = sb("sbias", [128, 5, 128], F32, ph)
                sbB = K.B("sbB")
                oT = sb("oT", [64, 12, TP], BF16, ph)
                oB = K.Bs("oB", 12)
                ocT = sb("ocT", [128, 2, TP], BF16, ph)
                ocB = K.B("ocT")
                cacc = sb("cacc", [128, 2, TP], F32, ph)
                caB = K.B("cacc")
                rec = sb("rec", [128, TP], F32, ph)
                rech = sb("rech", [128, TP], BF16, ph)
                recl = sb("recl", [128, TP], BF16, ph)
                recB = K.B("rec")
                bcs = sb("bcs", [64, TP], F32, ph)
                bcB = K.B("bcs")
                ysb = sb("ysb", [128, KD, HP], F32, ph)
                yB = K.Bs("yB", KD)
                sq = sb("sq", [128, KD, HP], BF16, ph)
                sqB = K.Bs("sqB", KD)
                rstd = sb("rstd", [128, HP], F32, ph)
                rB = K.B("rstd")
                tmp = [sb("tmp%d" % i, [128, HP], F32, ph) for i in range(2)]
                tB = K.Bs("tmpB", 2)
                K.dma("sp", kaT[:], KA[:, :], writes=[resB])
                K.dma("sp", vaT[:], VV[:, 0:2, :].rearrange("(s p) h d -> p s h d", p=128), writes=[resB])
                K.dma("sp", kbc[:], KBd[:, :, S:NT], writes=[resB])
                K.dma("sp", vbc[:], VV[S:NT, 2:8, :].rearrange("(s p) h d -> p s h d", p=128), writes=[resB])
                K.dma("pool", woh[:], w_o[l, 0:768, :].rearrange("(h p) n -> p h n", p=64), writes=[resB])
                K.dma("pool", woc[:], w_o[l, 768:1024, :].rearrange("(k p) n -> p k n", p=128), writes=[resB])
                K.dma("sp", bI[:], biasI[l].rearrange("p (a h q) -> p a h q", a=5, h=6), writes=[resB])
                tiles = [(t * TP, TP, 0) for t in range(S // TP)]
                if do_ctx:
                    tiles.append((S, LC, 1))
                ntl = S // TP
                assert ntl >= 2
                bEv = biasE[l].rearrange("p (c a h q) -> p c a h q", c=4, a=4, h=6)

                def kt0_of(t):
                    return max(0, min(4 * t - 2, NQT - 8))

                def load_q(i):
                    c0, n, X = tiles[i]
                    K.dma("sp", qa[i % 2][:, :, 0:n], QA[:, :, c0:c0 + n], writes=[qaB[i % 2]])

                load_q(0)
                pcount = [0]
                for i, (c0, n, X) in enumerate(tiles):
                    b = i % 2
                    t = c0 // TP
                    if i + 1 < len(tiles):
                        load_q(i + 1)
                    K.dma("sp", qb[:, :, 0:n], QB[:, :, c0:c0 + n], writes=[ib])
                    zc0 = c0 if X == 0 else c0 + 2
                    K.dma("sp", zt[:, :, 0:n + 2], ZZ[:, :, zc0:zc0 + n + 2], writes=[ib])
                    K.dma("sp", cbt[:, :, 0:n], CBd[:, :, c0:c0 + n], writes=[ib])
                    if X == 0:
                        k0 = kt0_of(t)
                        K.dma("sp", kbw[:], KBd[:, :, k0 * 128:k0 * 128 + 1024], writes=[ib])
                        K.dma("sp", vbw[:], VV[k0 * 128:k0 * 128 + 1024, 2:8, :].rearrange("(s p) h d -> p s h d", p=128), writes=[ib])
                    qab = qaB[b]

                    def normalize(obank, hidx, n=n):
                        K.op("dve", (lambda obank=obank, n=n: lambda h: h.reciprocal(out=rec[64:65, 0:n], in_=ps[obank][64:65, 0:n]))(),
                             reads=[psB[obank]], writes=[recB])
                        K.op("dve", (lambda n=n: lambda h: h.tensor_copy(out=rech[64:65, 0:n], in_=rec[64:65, 0:n]))(),
                             reads=[recB], writes=[recB])
                        K.op("dve", (lambda n=n: lambda h: h.tensor_tensor(out=recl[64:65, 0:n], in0=rec[64:65, 0:n], in1=rech[64:65, 0:n], op=ALU.subtract))(),
                             reads=[recB], writes=[recB])
                        K.op("pe", (lambda n=n: lambda h: h.matmul(ps[6][0:64, 0:n], lhsT=onesb[64:65, 0:64], rhs=rech[64:65, 0:n], start=True, stop=False))(),
                             reads=[recB, cB], writes=[psB[6]])
                        K.op("pe", (lambda n=n: lambda h: h.matmul(ps[6][0:64, 0:n], lhsT=onesb[64:65, 0:64], rhs=recl[64:65, 0:n], start=False, stop=True))(),
                             reads=[recB, cB], writes=[psB[6]])
                        K.op("act", (lambda n=n: lambda h: h.activation(out=bcs[:, 0:n], in_=ps[6][0:64, 0:n], func=AF.Copy))(),
                             reads=[psB[6]], writes=[bcB])
                        K.op("dve", (lambda obank=obank, hidx=hidx, n=n: lambda h: h.tensor_tensor(
                            out=oT[:, hidx, 0:n], in0=ps[obank][0:64, 0:n], in1=bcs[:, 0:n], op=ALU.mult))(),
                            reads=[psB[obank], bcB], writes=[oB[hidx]])

                    kts = list(range(NKT)) if X == 0 else list(range(NQT, NKT))
                    for c in range(3):
                        def qk(ki, c=c, n=n, b=b):
                            kt = kts[ki]
                            for half in range(2):
                                sbank = 2 * (ki % 2) + half
                                lo = 64 * half
                                K.op("pe", (lambda kt=kt, c=c, lo=lo, sbank=sbank, b=b, n=n: lambda h: h.matmul(
                                    ps[sbank][:, 0:n], lhsT=kaT[lo:lo + 64, kt * 128:(kt + 1) * 128], rhs=qa[b][lo:lo + 64, c, 0:n],
                                    start=True, stop=True))(), reads=[resB, qab], writes=[psB[sbank]])
                        qk(0)
                        for ki, kt in enumerate(kts):
                            if ki + 1 < len(kts):
                                qk(ki + 1)
                            pp = ki % 2
                            K.op("act", (lambda pp=pp, n=n: lambda h: h.activation(
                                out=pA[pp][:, :, 0:n], in_=psall[:, pp * 1024:(pp + 1) * 1024].rearrange("p (h q) -> p h q", h=2)[:, :, 0:n],
                                func=AF.Exp))(), reads=[psB[2 * pp], psB[2 * pp + 1]], writes=[pAB[pp]])
                            for half in range(2):
                                K.op("pe", (lambda kt=kt, half=half, pp=pp, ki=ki, n=n: lambda h: h.matmul(
                                    ps[4 + half][0:65, 0:n], lhsT=vaT[:, kt, half, :], rhs=pA[pp][:, half, 0:n],
                                    start=(ki == 0), stop=(ki == len(kts) - 1)))(), reads=[resB, pAB[pp]], writes=[psB[4 + half]])
                        normalize(4, c)
                        normalize(5, c + 3)
                    for c in range(3):
                        if X == 0 and (t == 0 or t == ntl - 1):
                            cls0 = 0 if t == 0 else 2
                            K.dma("sp", bE[:], bEv[:, cls0:cls0 + 2, :, 2 * c:2 * c + 2, :], writes=[bEB])
                        for qs in range(n // 128):
                            q0 = qs * 128
                            if X == 0:
                                qt = c0 // 128 + qs
                                if qt < 2 or qt >= NQT - 2:
                                    wk = list(range(0, 4)) if qt < 2 else list(range(NQT - 4, NQT))
                                    cls = qt if qt < 2 else qt - (NQT - 2)
                                    bsrc = lambda a0, a1, half, cls=cls: bE[:, cls, a0:a1, half, :]
                                    brd = bEB
                                else:
                                    wk = list(range(qt - 2, qt + 3))
                                    bsrc = lambda a0, a1, half, c=c: bI[:, a0:a1, 2 * c + half, :]
                                    brd = resB
                            else:
                                wk = []
                            nw = len(wk)
                            for half in range(2):
                                lo = 64 * half
                                for a, kt in enumerate(wk):
                                    bank = 2 * half + (0 if a < 4 else 1)
                                    col = (a % 4) * 128
                                    sl = (kt - k0) * 128
                                    K.op("pe", (lambda bank=bank, col=col, sl=sl, lo=lo, c=c, q0=q0: lambda h: h.matmul(
                                        ps[bank][:, col:col + 128], lhsT=kbw[lo:lo + 64, c, sl:sl + 128],
                                        rhs=qb[lo:lo + 64, c, q0:q0 + 128], start=True, stop=True))(),
                                        reads=[ib], writes=[psB[bank]])
                                for a in range(2):
                                    bank = 2 * half + 1
                                    col = 128 + a * 128
                                    K.op("pe", (lambda bank=bank, col=col, a=a, lo=lo, c=c, q0=q0: lambda h: h.matmul(
                                        ps[bank][:, col:col + 128], lhsT=kbc[lo:lo + 64, c, a * 128:(a + 1) * 128],
                                        rhs=qb[lo:lo + 64, c, q0:q0 + 128], start=True, stop=True))(),
                                        reads=[ib, resB], writes=[psB[bank]])
                            for half in range(2):
                                hh = 2 * c + half
                                pi = pcount[0] % 4
                                pcount[0] += 1
                                if nw:
                                    na = min(nw, 4)
                                    K.op("dve", (lambda half=half, na=na, bsrc=bsrc: lambda h: h.tensor_tensor(
                                        out=sbias[:, 0:na, :], in0=ps[2 * half][:, 0:na * 128].rearrange("p (a q) -> p a q", q=128),
                                        in1=bsrc(0, na, half), op=ALU.add))(),
                                        reads=[psB[2 * half], brd], writes=[sbB])
                                    if nw > 4:
                                        K.op("dve", (lambda half=half, bsrc=bsrc: lambda h: h.tensor_tensor(
                                            out=sbias[:, 4:5, :], in0=ps[2 * half + 1][:, 0:128].rearrange("p (a q) -> p a q", q=128),
                                            in1=bsrc(4, 5, half), op=ALU.add))(),
                                            reads=[psB[2 * half + 1], brd], writes=[sbB])
                                    K.op("act", (lambda pi=pi, nw=nw: lambda h: h.activation(
                                        out=pT[pi][:, 0:nw * 128].rearrange("p (a q) -> p a q", q=128), in_=sbias[:, 0:nw, :], func=AF.Exp))(),
                                        reads=[sbB], writes=[pB[pi]])
                                bank = 2 * half + 1
                                K.op("act", (lambda pi=pi, bank=bank, nw=nw: lambda h: h.activation(
                                    out=pT[pi][:, nw * 128:(nw + 2) * 128], in_=ps[bank][:, 128:384], func=AF.Exp))(),
                                    reads=[psB[bank]], writes=[pB[pi]])
                                nk = nw + 2
                                for a in range(nk):
                                    if a < nw:
                                        lhs = (lambda hh=hh, sl=wk[a] - k0: vbw[:, sl, hh, :])
                                    else:
                                        lhs = (lambda a=a, hh=hh, nw=nw: vbc[:, a - nw, hh, :])
                                    K.op("pe", (lambda a=a, lhs=lhs, half=half, pi=pi, q0=q0, nk=nk: lambda h: h.matmul(
                                        ps[4 + half][0:65, q0:q0 + 128], lhsT=lhs(), rhs=pT[pi][:, a * 128:(a + 1) * 128],
                                        start=(a == 0), stop=(a == nk - 1)))(), reads=[ib, resB, pB[pi]], writes=[psB[4 + half]])
                        normalize(4, 6 + 2 * c)
                        normalize(5, 6 + 2 * c + 1)
                    for c in range(2):
                        w0 = smp[:, l, 124 + c * 3:125 + c * 3]
                        w1 = smp[:, l, 125 + c * 3:126 + c * 3]
                        w2 = smp[:, l, 126 + c * 3:127 + c * 3]
                        K.op("pool", (lambda c=c, w0=w0, n=n: lambda h: h.tensor_scalar(
                            out=cacc[:, 0, 0:n], in0=zt[:, c, 0:n], scalar1=w0, scalar2=None, op0=ALU.mult))(),
                            reads=[ib, cB], writes=[caB])
                        for (wt_, off) in ((w1, 1), (w2, 2)):
                            K.op("pool", (lambda c=c, wt_=wt_, off=off, n=n: lambda h: h.tensor_scalar(
                                out=cacc[:, 1, 0:n], in0=zt[:, c, off:n + off], scalar1=wt_, scalar2=None, op0=ALU.mult))(),
                                reads=[ib, cB], writes=[caB])
                            K.op("pool", (lambda n=n: lambda h: h.tensor_tensor(
                                out=cacc[:, 0, 0:n], in0=cacc[:, 0, 0:n], in1=cacc[:, 1, 0:n], op=ALU.add))(),
                                reads=[caB], writes=[caB])
                        K.op("pool", (lambda c=c, n=n: lambda h: h.tensor_tensor(
                            out=ocT[:, c, 0:n], in0=cacc[:, 0, 0:n], in1=cbt[:, c, 0:n], op=ALU.mult))(),
                            reads=[ib, caB], writes=[ocB])
                    for hf in range(n // HP):
                        o0 = hf * HP
                        K.dma("sp", hT[:], Hin[:, c0 + o0:c0 + o0 + HP].rearrange("(k p) t -> p k t", p=128), writes=hb)
                        for j in range(KD):
                            bank = 6 + j % 2
                            for hh in range(12):
                                K.op("pe", (lambda j=j, hh=hh, bank=bank, o0=o0: lambda h: h.matmul(
                                    ps[bank][:, 0:HP], lhsT=woh[:, hh, j * 128:(j + 1) * 128], rhs=oT[:, hh, o0:o0 + HP],
                                    start=(hh == 0), stop=False))(), reads=[resB, oB[hh]], writes=[psB[bank]])
                            for k in range(2):
                                K.op("pe", (lambda j=j, k=k, bank=bank, o0=o0: lambda h: h.matmul(
                                    ps[bank][:, 0:HP], lhsT=woc[:, k, j * 128:(j + 1) * 128], rhs=ocT[:, k, o0:o0 + HP],
                                    start=False, stop=(k == 1)))(), reads=[resB, ocB], writes=[psB[bank]])
                            K.op("act", (lambda j=j, bank=bank: lambda h: h.activation(out=ysb[:, j, :], in_=ps[bank][:, 0:HP], func=AF.Copy))(),
                                 reads=[psB[bank]], writes=[yB[j]])
                            K.op("act", (lambda j=j, bank=bank: lambda h: h.activation(out=sq[:, j, :], in_=ps[bank][:, 0:HP], func=AF.Square))(),
                                 reads=[psB[bank]], writes=[sqB[j]])
                        for k in range(KD):
                            K.op("pe", (lambda k=k: lambda h: h.matmul(ps[0][:, 0:HP], lhsT=onesm[:], rhs=sq[:, k, :],
                                                                       start=(k == 0), stop=(k == KD - 1)))(),
                                 reads=[sqB[k], cB], writes=[psB[0]])
                        K.op("act", lambda h: h.activation(out=rstd[:], in_=ps[0][:, 0:HP], func=AF.Sqrt, bias=EPS, scale=1.0),
                             reads=[psB[0]], writes=[rB])
                        K.op("dve", lambda h: h.reciprocal(out=rstd[:], in_=rstd[:]), reads=[rB], writes=[rB])
                        for j in range(KD):
                            tb = j % 2
                            K.op("dve", (lambda j=j, tb=tb, X=X: lambda h: h.scalar_tensor_tensor(
                                out=tmp[tb][:], in0=ysb[:, j, :], scalar=par[:, l, X, 5, j:j + 1], in1=rstd[:],
                                op0=ALU.mult, op1=ALU.mult))(), reads=[yB[j], rB], writes=[tB[tb]])
                            K.op("pool", (lambda j=j, tb=tb: lambda h: h.tensor_tensor(
                                out=hT[:, j, :], in0=hT[:, j, :], in1=tmp[tb][:], op=ALU.add))(),
                                reads=[tB[tb]], writes=[hb[j]])
                        K.dma("sp", Hout[:, c0 + o0:c0 + o0 + HP].rearrange("(k p) t -> p k t", p=128), hT[:], reads=hb)
                K.flush()

        phase_ffn(0, 0, xT, HA, "x", "HA", True)
        phase_proj(0, HA, "HA")
        phase_attn(0, HA, HB, "HA", "HB", True)
        phase_ffn(0, 1, HB, HA, "HB", "HA", True)
        phase_ffn(1, 0, HA, HB, "HA", "HB", True)
        phase_proj(1, HB, "HB")
        phase_attn(1, HB, HA, "HB", "HA", False)
        phase_ffn(1, 1, HA, None, "HA", "none", False, final_out=outT)
    return nc


def _rope_tables(S):
    pos = np.arange(S, dtype=np.int32)
    row = (pos // GW).astype(np.float32)
    col = (pos % GW).astype(np.float32)
    freq = (1.0 / (np.float32(10000.0) ** (np.arange(16, dtype=np.float32) / np.float32(16)))).astype(np.float32)
    C = np.zeros((128, S), np.float32)
    Sg = np.zeros((128, S), np.float32)
    for p in range(128):
        d = p % 64
        a, j, f = d // 32, (d // 16) % 2, d % 16
        ang = ((row if a == 0 else col) * freq[f]).astype(np.float32)
        C[p] = np.cos(ang)
        Sg[p] = np.sin(ang) * (-1.0 if j == 0 else 1.0)
    return C, Sg


def _bias_mats(rpb_l, qt, kts, rows):
    out = np.full((128, len(kts), 6, 128), NEG, np.float32)
    j = np.arange(128)
    for a, kt in enumerate(kts):
        kr = 2 * kt + j // 64
        kc = j % 64
        qr = 2 * qt + j // 64
        qc = j % 64
        rs = np.clip(qr - 4, 0, rows - 8)
        cs = np.clip(qc - 8, 0, GW - 16)
        okr = (kr[:, None] >= rs[None, :]) & (kr[:, None] < rs[None, :] + 8)
        okc = (kc[:, None] >= cs[None, :]) & (kc[:, None] < cs[None, :] + 16)
        ok = okr & okc
        dr = np.clip(kr[:, None] - qr[None, :] + 7, 0, 14)
        dc = np.clip(kc[:, None] - qc[None, :] + 15, 0, 30)
        for hh in range(6):
            vals = rpb_l[hh][dr, dc]
            out[:, a, hh, :] = np.where(ok, vals, np.float32(NEG))
    return out


def _prep_inputs(inp, S):
    x, c, ctx, c_ctx = inp["x"], inp["c"], inp["ctx"], inp["c_ctx"]
    B = x.shape[0]
    rows = S // GW
    NQT = S // 128
    w_in = inp["w_in"]
    perm = np.arange(64) ^ 16
    cols = []
    qa_h = lambda h: np.arange(h * 64, (h + 1) * 64)
    for cc in range(3):
        cols += [qa_h(cc), qa_h(cc + 3)]
    for cc in range(3):
        cols += [qa_h(cc)[perm], qa_h(cc + 3)[perm]]
    ka0 = 768
    cols += [np.arange(ka0, ka0 + 64), np.arange(ka0 + 64, ka0 + 128)]
    cols += [np.arange(ka0, ka0 + 64)[perm], np.arange(ka0 + 64, ka0 + 128)[perm]]
    cols += [np.arange(384, 768)]
    cols += [np.arange(768 + 256, 768 + 256 + 384)]
    cx0 = 768 + 1024
    cols += [np.arange(cx0, cx0 + 768)]
    cols += [np.arange(768 + 128, 768 + 256)]
    cols += [np.arange(768 + 256 + 384, 768 + 1024)]
    cols = np.concatenate(cols)
    assert cols.shape[0] == NW
    w_inx = np.ascontiguousarray(w_in[:, :, cols])
    smallp = np.zeros((2, 128, NS), np.float32)
    p = np.arange(128)
    for l in range(2):
        smallp[l, :, 0:72] = inp["b_ada"][l].reshape(72, 128).T
        smallp[l, :, 72:120] = inp["norm_g"][l].reshape(6, 8, 128).transpose(2, 0, 1).reshape(128, 48)
        smallp[l, :, 120] = inp["qk_g"][l, 0][p % 64]
        smallp[l, :, 121] = inp["qk_g"][l, 0][(p % 64) ^ 16]
        smallp[l, :, 122] = inp["qk_g"][l, 1][p % 64]
        smallp[l, :, 123] = inp["qk_g"][l, 1][(p % 64) ^ 16]
        smallp[l, :, 124:130] = inp["conv_w"][l].reshape(3, 2, 128).transpose(2, 1, 0).reshape(128, 6)
    C, Sg = _rope_tables(S)
    biasI = np.zeros((2, 128, 5 * 6 * 128), np.float32)
    biasE = np.zeros((2, 128, 4 * 4 * 6 * 128), np.float32)
    for l in range(2):
        biasI[l] = _bias_mats(inp["rpb"][l], 2, [0, 1, 2, 3, 4], rows).reshape(128, -1)
        be = [_bias_mats(inp["rpb"][l], qt, kts, rows) for qt, kts in
              ((0, [0, 1, 2, 3]), (1, [0, 1, 2, 3]), (NQT - 2, list(range(NQT - 4, NQT))), (NQT - 1, list(range(NQT - 4, NQT))))]
        biasE[l] = np.stack(be, axis=1).reshape(128, -1)
    shared = dict(smallp=smallp, w_ada=np.ascontiguousarray(inp["w_ada"]), w_inx=w_inx, w_o=np.ascontiguousarray(inp["w_o"]),
                  ffn_wi=np.ascontiguousarray(inp["ffn_wi"]), ffn_wo=np.ascontiguousarray(inp["ffn_wo"]),
                  ropeC=C, ropeS=Sg, biasI=biasI, biasE=biasE)
    maps = []
    for b in range(B):
        m = dict(shared)
        m["xT"] = np.ascontiguousarray(np.concatenate([x[b].T, ctx[b].T], axis=1))
        cv = np.stack([c[b].reshape(8, 128).T, c_ctx.reshape(8, 128).T], axis=-1)
        m["cvec"] = np.ascontiguousarray(cv.astype(np.float32))
        maps.append(m)
    return maps


_CACHE = {}


def run(inp, debug=False):
    inp = {k: np.asarray(v, dtype=np.float32) for k, v in inp.items()}
    S = inp["x"].shape[1]
    B = inp["x"].shape[0]
    key = (S, debug)
    if key not in _CACHE:
        _CACHE[key] = build_program(S, debug)
    nc = _CACHE[key]
    maps = _prep_inputs(inp, S)
    res = run_bass_kernel_spmd(nc, maps, core_ids=list(range(B)))
    out = np.stack([np.ascontiguousarray(r["outT"].T) for r in res.results], axis=0)
    return out.astype(np.float32), res


def kernel(**inputs):
    out, _ = run(inputs)
    return out
```

```python
import numpy as np
from contextlib import ExitStack
import concourse.bass as bass
import concourse.mybir as mybir
from concourse.bass_utils import run_bass_kernel_spmd

F32 = mybir.dt.float32
BF16 = mybir.dt.bfloat16
AF = mybir.ActivationFunctionType
ALU = mybir.AluOpType

D = 1024
KD = 8
FFN = 2816
KF = 22
LC = 256
GW = 64
NEG = -30000.0
EPS = 1e-6
NS = 132
NW = 3072
TF = 256
NDMA = 24
SAME_ENG_SYNC = False


class Buf:
    __slots__ = ("name", "w", "rc", "rd")

    def __init__(self, name):
        self.name = name
        self.w = []
        self.rc = {}
        self.rd = []


class Eng:
    def __init__(self, name, sem):
        self.name = name
        self.sem = sem
        self.base = 0
        self.reset()

    def reset(self):
        self.items = []
        self.marks = []
        self.waited = {}
        self.vals = []


class KB:
    HM = {"pe": "tensor", "act": "scalar", "dve": "vector", "pool": "gpsimd", "sp": "sync"}

    def __init__(self, nc, es):
        self.nc = nc
        self.engs = {}
        for n in self.HM:
            self.engs[n] = Eng(n, es.enter_context(nc.semaphore("s_" + n)))
        self.dsem = [es.enter_context(nc.semaphore("d%d" % i)) for i in range(NDMA)]
        self.dcnt = [0] * NDMA
        self.drr = 0
        self.bufs = []

    def B(self, name):
        b = Buf(name)
        self.bufs.append(b)
        return b

    def Bs(self, name, n):
        return [self.B("%s%d" % (name, i)) for i in range(n)]

    def _wait(self, eng, tok):
        if tok[0] == "c":
            pe, seq = tok[1], tok[2]
            if pe is eng and (eng.name == "pe" or eng.name == "sp" or not SAME_ENG_SYNC):
                return
            if eng.waited.get(pe.name, -1) >= seq:
                return
            eng.waited[pe.name] = seq
            pe.marks[seq] = True
            eng.items.append(("wc", pe, seq))
        else:
            idx, val = tok[1], tok[2]
            if eng.waited.get(("d", idx), 0) >= val:
                return
            eng.waited[("d", idx)] = val
            eng.items.append(("wd", idx, val))

    def _deps(self, eng, reads, writes):
        for b in reads:
            for t in b.w:
                self._wait(eng, t)
        for b in writes:
            for t in b.w:
                self._wait(eng, t)
            for n, s in b.rc.items():
                self._wait(eng, ("c", self.engs[n], s))
            for t in b.rd:
                self._wait(eng, t)

    def op(self, en, fn, reads=(), writes=()):
        eng = self.engs[en]
        self._deps(eng, reads, writes)
        seq = len(eng.marks)
        eng.marks.append(False)
        eng.items.append(("op", fn, seq))
        tok = ("c", eng, seq)
        for b in writes:
            b.w = [tok]
            b.rc = {}
            b.rd = []
        for b in reads:
            b.rc[en] = seq

    def dma(self, en, out, in_, reads=(), writes=()):
        eng = self.engs[en]
        self._deps(eng, reads, writes)
        idx = self.drr
        self.drr = (self.drr + 1) % NDMA
        self.dcnt[idx] += 16
        tok = ("d", idx, self.dcnt[idx])
        eng.items.append(("dma", out, in_, idx))
        for b in writes:
            b.w = [tok]
            b.rc = {}
            b.rd = []
        for b in reads:
            b.rd.append(tok)

    def flush(self):
        engs = self.engs
        for e in engs.values():
            if e.marks:
                e.marks[-1] = True
        for f in engs.values():
            for e in engs.values():
                if e is not f and e.marks:
                    f.items.append(("wc", e, len(e.marks) - 1))
            for i in range(NDMA):
                if self.dcnt[i] > 0:
                    f.items.append(("wd", i, self.dcnt[i]))
        for e in engs.values():
            v = e.base
            vals = []
            for m in e.marks:
                if m:
                    v += 1
                vals.append(v)
            e.vals = vals
            e.newbase = v
        dsem = self.dsem
        with self.nc.Block() as block:
            for name, e in engs.items():
                def mk(e):
                    def body(h):
                        for it in e.items:
                            k = it[0]
                            if k == "op":
                                ins = it[1](h)
                                if e.marks[it[2]]:
                                    ins.then_inc(e.sem, 1)
                            elif k == "wc":
                                h.wait_ge(it[1].sem, it[1].vals[it[2]])
                            elif k == "wd":
                                h.wait_ge(dsem[it[1]], it[2])
                            else:
                                h.dma_start(out=it[1], in_=it[2], allow_slow_non_contiguous=True).then_inc(dsem[it[3]], 16)
                    return body
                getattr(block, self.HM[name])(mk(e))
        for e in engs.values():
            e.base = e.newbase
            e.reset()
        for b in self.bufs:
            b.w = []
            b.rc = {}
            b.rd = []


def build_program(S, debug=False):
    NT = S + LC
    NQT = S // 128
    NKT = NT // 128
    nc = bass.Bass("TRN2", target_bir_lowering=False)
    dk = "ExternalOutput" if debug else "Internal"

    def din(name, shape, dt=F32):
        return nc.dram_tensor(name, list(shape), dt, kind="ExternalInput").ap()

    def dscr(name, shape, dt=F32):
        return nc.dram_tensor(name, list(shape), dt, kind=dk).ap()

    xT = din("xT", [D, NT])
    cvec = din("cvec", [128, KD, 2])
    smallp = din("smallp", [2, 128, NS])
    w_ada = din("w_ada", [2, D, 9 * D])
    w_inx = din("w_inx", [2, D, NW])
    w_o = din("w_o", [2, D, D])
    ffn_wi = din("ffn_wi", [2, 2, D, 2 * FFN])
    ffn_wo = din("ffn_wo", [2, 2, FFN, D])
    ropeC = din("ropeC", [128, S])
    ropeS = din("ropeS", [128, S])
    biasI = din("biasI", [2, 128, 5 * 6 * 128])
    biasE = din("biasE", [2, 128, 4 * 4 * 6 * 128])
    outT = nc.dram_tensor("outT", [D, S], F32, kind="ExternalOutput").ap()

    HA = dscr("HA", [D, NT])
    HB = dscr("HB", [D, NT])
    QA = dscr("QA", [128, 3, NT], BF16)
    KA = dscr("KA", [128, NT], BF16)
    QB = dscr("QB", [128, 3, NT], BF16)
    KBd = dscr("KBd", [128, 3, NT], BF16)
    VV = dscr("VV", [NT, 8, 65], BF16)
    ZZ = dscr("ZZ", [128, 2, NT + 4])
    CBd = dscr("CBd", [128, 2, NT])

    es = ExitStack()
    with es:
        K = KB(nc, es)

        uid = [0]

        def sb(name, shape, dt=F32, stack=es):
            uid[0] += 1
            return stack.enter_context(nc.sbuf_tensor("%s_%d" % (name, uid[0]), list(shape), dt))

        psall = es.enter_context(nc.psum_tensor("psall", [128, 4096], F32))
        ps = [psall[:, i * 512:(i + 1) * 512] for i in range(8)]
        psB = K.Bs("psB", 8)
        par = sb("par", [128, 2, 2, 9, KD])
        smp = sb("smp", [128, 2, NS])
        qsc = sb("qsc", [128, 2, 2])
        onesm = sb("onesm", [128, 128], BF16)
        bd64 = sb("bd64", [128, 128], BF16)
        onesb = sb("onesb", [128, 64], BF16)
        cB = K.B("consts")
        dramB = {}

        def DB(name, i):
            key = (name, i)
            if key not in dramB:
                dramB[key] = K.B("dram_%s_%d" % (name, i))
            return dramB[key]

        with ExitStack() as ph:
            cv = sb("cv", [128, KD, 2], F32, ph)
            scv = sb("scv", [128, KD, 2], F32, ph)
            wa = [sb("wa%d" % i, [128, KD, 1024], F32, ph) for i in range(2)]
            waB = K.Bs("waB", 2)
            mods = sb("mods", [128, 2, 72], F32, ph)
            cvB, scB, modB = K.B("cv"), K.B("scv"), K.B("mods")
            K.op("dve", lambda h: h.memset(onesm[:], 1.0 / 1024), writes=[cB])
            K.op("dve", lambda h: h.memset(bd64[:], 0.0), writes=[cB])
            K.op("dve", lambda h: h.memset(bd64[0:64, 0:64], 1.0 / 64), writes=[cB])
            K.op("dve", lambda h: h.memset(bd64[64:128, 64:128], 1.0 / 64), writes=[cB])
            K.op("dve", lambda h: h.memset(onesb[:], 1.0), writes=[cB])
            K.dma("sp", cv[:], cvec[:, :, :], writes=[cvB])
            K.dma("sp", smp[:], smallp.rearrange("l p n -> p l n"), writes=[cB])
            K.op("act", lambda h: h.activation(out=scv[:], in_=cv[:], func=AF.Silu), reads=[cvB], writes=[scB])
            it = 0
            for l in range(2):
                for i in range(9):
                    b = it % 2
                    it += 1
                    K.dma("sp", wa[b][:], w_ada[l].rearrange("(k p) n -> p k n", p=128)[:, :, i * 1024:(i + 1) * 1024],
                          writes=[waB[b]])
                    for j in range(8):
                        idx = i * 8 + j
                        for k in range(KD):
                            K.op("pe", (lambda b=b, j=j, k=k, idx=idx: lambda h: h.matmul(
                                ps[0][:, 2 * idx:2 * idx + 2], lhsT=wa[b][:, k, j * 128:(j + 1) * 128], rhs=scv[:, k, :],
                                start=(k == 0), stop=(k == KD - 1)))(), reads=[waB[b], scB], writes=[psB[0]])
                pv = ps[0][:, 0:144].rearrange("p (n t) -> p n t", t=2)
                for X in range(2):
                    K.op("dve", (lambda X=X, l=l, pv=pv: lambda h: h.tensor_tensor(
                        out=mods[:, X, :], in0=pv[:, :, X], in1=smp[:, l, 0:72], op=ALU.add))(),
                        reads=[psB[0], cB], writes=[modB])

                    def m(i, X=X):
                        return mods[:, X, i * 8:(i + 1) * 8]

                    def g(gi, l=l):
                        return smp[:, l, 72 + gi * 8:72 + (gi + 1) * 8]
                    for (kind, mi, gi, wres) in ((0, 1, 0, None), (3, 4, 2, None), (6, 7, 4, None)):
                        K.op("dve", (lambda kind=kind, mi=mi, gi=gi, X=X, l=l, m=m, g=g: lambda h: h.scalar_tensor_tensor(
                            out=par[:, l, X, kind, :], in0=m(mi), scalar=1.0, in1=g(gi), op0=ALU.add, op1=ALU.mult))(),
                            reads=[modB, cB], writes=[cB])
                    for (kind, mi) in ((1, 0), (4, 3), (7, 6)):
                        K.op("dve", (lambda kind=kind, mi=mi, X=X, l=l, m=m: lambda h: h.tensor_copy(
                            out=par[:, l, X, kind, :], in_=m(mi)))(), reads=[modB], writes=[cB])
                    for (kind, mi, gi, wres) in ((2, 2, 1, 0.5), (5, 5, 3, 1.0), (8, 8, 5, 0.5)):
                        K.op("dve", (lambda kind=kind, mi=mi, gi=gi, wres=wres, X=X, l=l, m=m, g=g: lambda h: h.scalar_tensor_tensor(
                            out=par[:, l, X, kind, :], in0=m(mi), scalar=wres, in1=g(gi), op0=ALU.mult, op1=ALU.mult))(),
                            reads=[modB, cB], writes=[cB])
                K.op("dve", (lambda l=l: lambda h: h.tensor_scalar(
                    out=qsc[:, l, :], in0=smp[:, l, 120:122], scalar1=0.125, scalar2=None, op0=ALU.mult))(),
                    reads=[cB], writes=[cB])
            K.flush()

        def phase_ffn(l, which, Hin, Hout, hin_name, hout_name, do_ctx, final_out=None):
            kA, kB_, kG = (0, 1, 2) if which == 0 else (6, 7, 8)
            with ExitStack() as ph:
                wi = sb("wi", [128, KD, 2 * FFN], BF16, ph)
                wo = sb("wo", [128, KF, D], BF16, ph)
                wiB = K.Bs("wiB", KD)
                woB = K.Bs("woB", 2)
                hT = [sb("hT%d" % i, [128, KD, TF], F32, ph) for i in range(2)]
                hB = [K.Bs("hB%d_" % i, KD) for i in range(2)]
                u2 = [sb("u%d" % i, [128, KD, TF], BF16, ph) for i in range(2)]
                u2B = [K.Bs("uB%d_" % i, KD) for i in range(2)]
                rstd2 = sb("rstd2", [128, TF], F32, ph)
                r2B = K.B("rstd2")
                act = sb("actb", [128, KF, TF], BF16, ph)
                actB = K.Bs("actB", KF)
                ysb = sb("ysb", [128, KD, TF], F32, ph)
                yB = K.Bs("yB", KD)
                rstd = sb("rstd", [128, TF], F32, ph)
                rB = K.B("rstd")
                tmp = [sb("tmp%d" % i, [128, TF], F32, ph) for i in range(2)]
                tB = K.Bs("tmpB", 2)
                sg = [sb("sg%d" % i, [128, TF], F32, ph) for i in range(2)]
                sgB = K.Bs("sgB", 2)
                for k in range(KD):
                    K.dma("pool", wi[:, k, :], ffn_wi[l, which, k * 128:(k + 1) * 128, :], writes=[wiB[k]])
                wov = ffn_wo[l, which].rearrange("(m p) n -> p m n", p=128)
                K.dma("pool", wo[:, 0:11, :], wov[:, 0:11, :], writes=[woB[0]])
                K.dma("pool", wo[:, 11:22, :], wov[:, 11:22, :], writes=[woB[1]])
                tiles = [(t * TF, 0) for t in range(S // TF)]
                if do_ctx:
                    tiles.append((S, 1))

                def load(i):
                    c0, X = tiles[i]
                    b = i % 2
                    K.dma("sp", hT[b][:], Hin[:, c0:c0 + TF].rearrange("(k p) t -> p k t", p=128),
                          reads=[DB(hin_name, c0 // TF)], writes=hB[b])

                def stats(ub, ubB, rs, rsB):
                    for k in range(KD):
                        K.op("pe", (lambda k=k, ub=ub: lambda h: h.matmul(ps[0][:, 0:TF], lhsT=onesm[:], rhs=ub[:, k, :],
                                                                          start=(k == 0), stop=(k == KD - 1)))(),
                             reads=[ubB[k], cB], writes=[psB[0]])
                    K.op("act", (lambda rs=rs: lambda h: h.activation(out=rs[:], in_=ps[0][:, 0:TF], func=AF.Sqrt, bias=EPS, scale=1.0))(),
                         reads=[psB[0]], writes=[rsB])
                    K.op("dve", (lambda rs=rs: lambda h: h.reciprocal(out=rs[:], in_=rs[:]))(), reads=[rsB], writes=[rsB])

                def pre_sq(i, k):
                    b = i % 2
                    K.op("act", (lambda k=k, b=b: lambda h: h.activation(out=u2[b][:, k, :], in_=hT[b][:, k, :], func=AF.Square))(),
                         reads=[hB[b][k]], writes=[u2B[b][k]])

                def pre_u(i, k):
                    b = i % 2
                    X = tiles[i][1]
                    tb = k % 2
                    K.op("dve", (lambda k=k, tb=tb, b=b, X=X: lambda h: h.scalar_tensor_tensor(
                        out=tmp[tb][:], in0=hT[b][:, k, :], scalar=par[:, l, X, kA, k:k + 1], in1=rstd[:],
                        op0=ALU.mult, op1=ALU.mult))(), reads=[hB[b][k], rB], writes=[tB[tb]])
                    K.op("act", (lambda k=k, tb=tb, b=b, X=X: lambda h: h.activation(
                        out=u2[b][:, k, :], in_=tmp[tb][:], func=AF.Identity, bias=par[:, l, X, kB_, k:k + 1], scale=1.0))(),
                        reads=[tB[tb]], writes=[u2B[b][k]])

                load(0)
                for k in range(KD):
                    pre_sq(0, k)
                stats(u2[0], u2B[0], rstd, rB)
                for k in range(KD):
                    pre_u(0, k)
                for i, (c0, X) in enumerate(tiles):
                    b = i % 2
                    nxt = i + 1 < len(tiles)
                    if nxt:
                        load(i + 1)
                    h_, hb = hT[b], hB[b]
                    u, uB = u2[b], u2B[b]
                    for m in range(KF):
                        pb = m % 2
                        for half, bank in ((0, 1 + pb), (1, 3 + pb)):
                            for k in range(KD):
                                c_ = half * FFN + m * 128
                                K.op("pe", (lambda k=k, c_=c_, bank=bank, u=u: lambda h: h.matmul(
                                    ps[bank][:, 0:TF], lhsT=wi[:, k, c_:c_ + 128], rhs=u[:, k, :],
                                    start=(k == 0), stop=(k == KD - 1)))(), reads=[wiB[k], uB[k]], writes=[psB[bank]])
                        K.op("act", (lambda pb=pb: lambda h: h.activation(out=sg[pb][:], in_=ps[1 + pb][:, 0:TF], func=AF.Silu))(),
                             reads=[psB[1 + pb]], writes=[sgB[pb]])
                        K.op("dve", (lambda pb=pb, m=m: lambda h: h.tensor_tensor(
                            out=act[:, m, :], in0=sg[pb][:], in1=ps[3 + pb][:, 0:TF], op=ALU.mult))(),
                            reads=[sgB[pb], psB[3 + pb]], writes=[actB[m]])
                        if nxt and 6 <= m < 6 + KD:
                            pre_sq(i + 1, m - 6)
                    if nxt:
                        stats(u2[1 - b], u2B[1 - b], rstd, rB)
                    for j in range(KD):
                        bank = 5 + j % 2
                        for m in range(KF):
                            K.op("pe", (lambda j=j, m=m, bank=bank: lambda h: h.matmul(
                                ps[bank][:, 0:TF], lhsT=wo[:, m, j * 128:(j + 1) * 128], rhs=act[:, m, :],
                                start=(m == 0), stop=(m == KF - 1)))(), reads=[woB[m // 11], actB[m]], writes=[psB[bank]])
                        if nxt:
                            pre_u(i + 1, j)
                        K.op("act", (lambda j=j, bank=bank: lambda h: h.activation(out=ysb[:, j, :], in_=ps[bank][:, 0:TF], func=AF.Copy))(),
                             reads=[psB[bank]], writes=[yB[j]])
                        K.op("act", (lambda j=j, bank=bank, u=u: lambda h: h.activation(out=u[:, j, :], in_=ps[bank][:, 0:TF], func=AF.Square))(),
                             reads=[psB[bank]], writes=[uB[j]])
                    stats(u, uB, rstd2, r2B)
                    for j in range(KD):
                        tb = j % 2
                        K.op("dve", (lambda j=j, tb=tb, X=X: lambda h: h.scalar_tensor_tensor(
                            out=tmp[tb][:], in0=ysb[:, j, :], scalar=par[:, l, X, kG, j:j + 1], in1=rstd2[:],
                            op0=ALU.mult, op1=ALU.mult))(), reads=[yB[j], r2B], writes=[tB[tb]])
                        K.op("pool", (lambda j=j, tb=tb, h_=h_: lambda h: h.tensor_tensor(
                            out=h_[:, j, :], in0=h_[:, j, :], in1=tmp[tb][:], op=ALU.add))(),
                            reads=[tB[tb]], writes=[hb[j]])
                    if final_out is not None and X == 0:
                        K.dma("sp", final_out[:, c0:c0 + TF].rearrange("(k p) t -> p k t", p=128), h_[:], reads=hb)
                    else:
                        K.dma("sp", Hout[:, c0:c0 + TF].rearrange("(k p) t -> p k t", p=128), h_[:],
                              reads=hb, writes=[DB(hout_name, c0 // TF)])
                K.flush()

        def phase_proj(l, Hin, hin_name):
            TP = 512
            with ExitStack() as ph:
                win = sb("win", [128, KD, NW], BF16, ph)
                winB = K.Bs("winB", KD)
                hT = [sb("hT%d" % i, [128, KD, TP], F32, ph) for i in range(2)]
                hB = [K.Bs("hB%d_" % i, KD) for i in range(2)]
                rc = [sb("rc%d" % i, [128, TP], F32, ph) for i in range(2)]
                rs_ = [sb("rs%d" % i, [128, TP], F32, ph) for i in range(2)]
                rcB = K.Bs("rcB", 2)
                u = sb("u", [128, KD, TP], BF16, ph)
                uB = K.Bs("uB", KD)
                rstd = sb("rstd", [128, TP], F32, ph)
                rB = K.B("rstd")
                tmp = [sb("tmp%d" % i, [128, TP], F32, ph) for i in range(2)]
                tB = K.Bs("tmpB", 2)
                sqp = sb("sqp", [128, TP], BF16, ph)
                sqB = K.B("sqp")
                rq = sb("rq", [128, TP], F32, ph)
                rqB = K.B("rq")
                t1 = sb("t1", [128, TP], F32, ph)
                t2 = sb("t2", [128, TP], F32, ph)
                t1B, t2B = K.B("t1"), K.B("t2")
                qa = sb("qa", [128, 3, TP], BF16, ph)
                ka = sb("ka", [128, TP], BF16, ph)
                qb = sb("qb", [128, 3, TP], BF16, ph)
                kb = sb("kb", [128, 3, TP], BF16, ph)
                zz = sb("zz", [128, 2, TP], F32, ph)
                cbs = sb("cbs", [128, 2, TP], F32, ph)
                cxs = sb("cxs", [128, TP], F32, ph)
                vv = sb("vv", [128, 4, 8, 65], BF16, ph)
                zer = sb("zer", [128, 2, 2], F32, ph)
                qaB, kaB, qbB, kbB, zzB, cbB, cxB, vvB, zeB = (K.B(n) for n in ("qa", "ka", "qb", "kb", "zz", "cbs", "cxs", "vv", "zer"))
                for k in range(KD):
                    K.dma("pool", win[:, k, :], w_inx[l, k * 128:(k + 1) * 128, :], writes=[winB[k]])
                K.op("dve", lambda h: h.memset(vv[:], 1.0), writes=[vvB])
                K.op("dve", lambda h: h.memset(zer[:], 0.0), writes=[zeB])
                K.dma("sp", ZZ[:, :, 0:1], zer[:, :, 0:1], reads=[zeB])
                K.dma("sp", ZZ[:, :, S + 1:S + 3], zer[:, :, 0:2], reads=[zeB])
                K.dma("sp", ZZ[:, :, NT + 3:NT + 4], zer[:, :, 0:1], reads=[zeB])
                tiles = [(t * TP, TP, 0) for t in range(S // TP)] + [(S, LC, 1)]

                def load(i):
                    c0, n, X = tiles[i]
                    b = i % 2
                    K.dma("sp", hT[b][:, :, 0:n], Hin[:, c0:c0 + n].rearrange("(k p) t -> p k t", p=128),
                          reads=[DB(hin_name, j) for j in range(c0 // TF, (c0 + n) // TF)], writes=hB[b])
                    if X == 0:
                        K.dma("sp", rc[b][:], ropeC[:, c0:c0 + n], writes=[rcB[b]])
                        K.dma("sp", rs_[b][:], ropeS[:, c0:c0 + n], writes=[rcB[b]])
                load(0)
                for i, (c0, n, X) in enumerate(tiles):
                    b = i % 2
                    if i + 1 < len(tiles):
                        load(i + 1)
                    h_, hb = hT[b], hB[b]
                    for k in range(KD):
                        K.op("act", (lambda k=k, h_=h_, n=n: lambda h: h.activation(out=u[:, k, 0:n], in_=h_[:, k, 0:n], func=AF.Square))(),
                             reads=[hb[k]], writes=[uB[k]])
                    for k in range(KD):
                        K.op("pe", (lambda k=k, n=n: lambda h: h.matmul(ps[0][:, 0:n], lhsT=onesm[:], rhs=u[:, k, 0:n],
                                                                       start=(k == 0), stop=(k == KD - 1)))(),
                             reads=[uB[k]], writes=[psB[0]])
                    K.op("act", (lambda n=n: lambda h: h.activation(out=rstd[:, 0:n], in_=ps[0][:, 0:n], func=AF.Sqrt, bias=EPS, scale=1.0))(),
                         reads=[psB[0]], writes=[rB])
                    K.op("dve", (lambda n=n: lambda h: h.reciprocal(out=rstd[:, 0:n], in_=rstd[:, 0:n]))(), reads=[rB], writes=[rB])
                    for k in range(KD):
                        tb = k % 2
                        K.op("dve", (lambda k=k, tb=tb, h_=h_, X=X, n=n: lambda h: h.scalar_tensor_tensor(
                            out=tmp[tb][:, 0:n], in0=h_[:, k, 0:n], scalar=par[:, l, X, 3, k:k + 1], in1=rstd[:, 0:n],
                            op0=ALU.mult, op1=ALU.mult))(), reads=[hb[k], rB], writes=[tB[tb]])
                        K.op("act", (lambda k=k, tb=tb, X=X, n=n: lambda h: h.activation(
                            out=u[:, k, 0:n], in_=tmp[tb][:, 0:n], func=AF.Identity, bias=par[:, l, X, 4, k:k + 1], scale=1.0))(),
                            reads=[tB[tb]], writes=[uB[k]])

                    def proj(g, bank):
                        for k in range(KD):
                            K.op("pe", (lambda k=k, g=g, bank=bank, n=n: lambda h: h.matmul(
                                ps[bank][:, 0:n], lhsT=win[:, k, g * 128:(g + 1) * 128], rhs=u[:, k, 0:n],
                                start=(k == 0), stop=(k == KD - 1)))(), reads=[winB[k], uB[k]], writes=[psB[bank]])

                    ng = [(c, 3 + c, ("q", 0), ("q", 1), qa[:, c, 0:n], qaB) for c in range(3)]
                    ng.append((6, 7, ("k", 122), ("k", 123), ka[:, 0:n], kaB))
                    for gi, (g0, g1, s0, s1, dst, dB) in enumerate(ng):
                        ba, bb_ = 1 + (gi % 2) * 2, 2 + (gi % 2) * 2
                        proj(g0, ba)
                        if X == 0:
                            proj(g1, bb_)
                        sc0 = qsc[:, l, 0:1] if s0[0] == "q" else smp[:, l, s0[1]:s0[1] + 1]
                        sc1 = qsc[:, l, 1:2] if s1[0] == "q" else smp[:, l, s1[1]:s1[1] + 1]
                        K.op("act", (lambda ba=ba, n=n: lambda h: h.activation(out=sqp[:, 0:n], in_=ps[ba][:, 0:n], func=AF.Square))(),
                             reads=[psB[ba]], writes=[sqB])
                        K.op("pe", (lambda n=n: lambda h: h.matmul(ps[5][:, 0:n], lhsT=bd64[:], rhs=sqp[:, 0:n], start=True, stop=True))(),
                             reads=[sqB, cB], writes=[psB[5]])
                        K.op("act", (lambda n=n: lambda h: h.activation(out=rq[:, 0:n], in_=ps[5][:, 0:n], func=AF.Sqrt, bias=EPS, scale=1.0))(),
                             reads=[psB[5]], writes=[rqB])
                        K.op("dve", (lambda n=n: lambda h: h.reciprocal(out=rq[:, 0:n], in_=rq[:, 0:n]))(), reads=[rqB], writes=[rqB])
                        if X == 0:
                            K.op("dve", (lambda ba=ba, sc0=sc0, n=n: lambda h: h.scalar_tensor_tensor(
                                out=t1[:, 0:n], in0=ps[ba][:, 0:n], scalar=sc0, in1=rq[:, 0:n], op0=ALU.mult, op1=ALU.mult))(),
                                reads=[psB[ba], rqB], writes=[t1B])
                            K.op("dve", (lambda bb_=bb_, sc1=sc1, n=n: lambda h: h.scalar_tensor_tensor(
                                out=t2[:, 0:n], in0=ps[bb_][:, 0:n], scalar=sc1, in1=rq[:, 0:n], op0=ALU.mult, op1=ALU.mult))(),
                                reads=[psB[bb_], rqB], writes=[t2B])
                            K.op("pool", (lambda b=b, n=n: lambda h: h.tensor_tensor(out=t1[:, 0:n], in0=t1[:, 0:n], in1=rc[b][:, 0:n], op=ALU.mult))(),
                                 reads=[rcB[b]], writes=[t1B])
                            K.op("pool", (lambda b=b, n=n: lambda h: h.tensor_tensor(out=t2[:, 0:n], in0=t2[:, 0:n], in1=rs_[b][:, 0:n], op=ALU.mult))(),
                                 reads=[rcB[b]], writes=[t2B])
                            K.op("pool", (lambda dst=dst, n=n: lambda h: h.tensor_tensor(out=dst, in0=t1[:, 0:n], in1=t2[:, 0:n], op=ALU.add))(),
                                 reads=[t1B, t2B], writes=[dB])
                        else:
                            K.op("dve", (lambda ba=ba, sc0=sc0, dst=dst, n=n: lambda h: h.scalar_tensor_tensor(
                                out=dst, in0=ps[ba][:, 0:n], scalar=sc0, in1=rq[:, 0:n], op0=ALU.mult, op1=ALU.mult))(),
                                reads=[psB[ba], rqB], writes=[dB])
                    for c in range(3):
                        bank = 6 + c % 2
                        proj(8 + c, bank)
                        K.op("act", (lambda c=c, bank=bank, n=n: lambda h: h.activation(out=qb[:, c, 0:n], in_=ps[bank][:, 0:n], func=AF.Copy, scale=0.125))(),
                             reads=[psB[bank]], writes=[qbB])
                    for c in range(3):
                        bank = 6 + (c + 1) % 2
                        proj(11 + c, bank)
                        K.op("act", (lambda c=c, bank=bank, n=n: lambda h: h.activation(out=kb[:, c, 0:n], in_=ps[bank][:, 0:n], func=AF.Copy))(),
                             reads=[psB[bank]], writes=[kbB])
                    for c in range(2):
                        proj(14 + c, 6)
                        proj(18 + c, 7)
                        K.op("act", (lambda n=n: lambda h: h.activation(out=cxs[:, 0:n], in_=ps[6][:, 0:n], func=AF.Copy))(),
                             reads=[psB[6]], writes=[cxB])
                        K.op("dve", (lambda c=c, n=n: lambda h: h.tensor_tensor(out=zz[:, c, 0:n], in0=cxs[:, 0:n], in1=ps[7][:, 0:n], op=ALU.mult))(),
                             reads=[cxB, psB[7]], writes=[zzB])
                        proj(16 + c, 6)
                        K.op("act", (lambda c=c, n=n: lambda h: h.activation(out=cbs[:, c, 0:n], in_=ps[6][:, 0:n], func=AF.Copy))(),
                             reads=[psB[6]], writes=[cbB])
                    for s in range(n // 128):
                        bank = 1 + s % 2
                        for k in range(KD):
                            K.op("pe", (lambda k=k, s=s, bank=bank: lambda h: h.matmul(
                                ps[bank][:, :], lhsT=u[:, k, s * 128:(s + 1) * 128], rhs=win[:, k, 2560:3072],
                                start=(k == 0), stop=(k == KD - 1)))(), reads=[winB[k], uB[k]], writes=[psB[bank]])
                        K.op("dve", (lambda s=s, bank=bank: lambda h: h.tensor_copy(
                            out=vv[:, s, :, 0:64], in_=ps[bank][:, :].rearrange("p (h d) -> p h d", d=64)))(),
                            reads=[psB[bank]], writes=[vvB])
                    tl = [DB("proj", j) for j in range(c0 // 128, (c0 + n) // 128)]
                    K.dma("sp", QA[:, :, c0:c0 + n], qa[:, :, 0:n], reads=[qaB], writes=tl)
                    K.dma("sp", KA[:, c0:c0 + n], ka[:, 0:n], reads=[kaB], writes=tl)
                    K.dma("sp", QB[:, :, c0:c0 + n], qb[:, :, 0:n], reads=[qbB], writes=tl)
                    K.dma("sp", KBd[:, :, c0:c0 + n], kb[:, :, 0:n], reads=[kbB], writes=tl)
                    zc0 = c0 + 1 if X == 0 else c0 + 3
                    K.dma("sp", ZZ[:, :, zc0:zc0 + n], zz[:, :, 0:n], reads=[zzB], writes=tl)
                    K.dma("sp", CBd[:, :, c0:c0 + n], cbs[:, :, 0:n], reads=[cbB], writes=tl)
                    K.dma("sp", VV[c0:c0 + n].rearrange("(s p) h d -> p s h d", p=128), vv[:, 0:n // 128], reads=[vvB], writes=tl)
                K.flush()

        def phase_attn(l, Hin, Hout, hin_name, hout_name, do_ctx):
            TP = 512
            HP = 256
            with ExitStack() as ph:
                kaT = sb("kaT", [128, NT], BF16, ph)
                vaT = sb("vaT", [128, NKT, 2, 65], BF16, ph)
                kbc = sb("kbc", [128, 3, LC], BF16, ph)
                vbc = sb("vbc", [128, 2, 6, 65], BF16, ph)
                woh = sb("woh", [64, 12, D], BF16, ph)
                woc = sb("woc", [128, 2, D], BF16, ph)
                bI = sb("bI", [128, 5, 6, 128], F32, ph)
                bE = sb("bE", [128, 2, 4, 2, 128], F32, ph)
                resB, bEB = K.B("resident"), K.B("bE")
                qa = [sb("qa%d" % i, [128, 3, TP], BF16, ph) for i in range(2)]
                qaB = K.Bs("qaB", 2)
                qb = sb("qb", [128, 3, TP], BF16, ph)
                kbw = sb("kbw", [128, 3, 1024], BF16, ph)
                vbw = sb("vbw", [128, 8, 6, 65], BF16, ph)
                zt = sb("zt", [128, 2, TP + 2], F32, ph)
                cbt = sb("cbt", [128, 2, TP], F32, ph)
                hT = sb("hT", [128, KD, HP], F32, ph)
                ib = K.B("inB")
                hb = K.Bs("hB", KD)
                pT = [sb("pT%d" % i, [128, 7 * 128], BF16, ph) for i in range(4)]
                pB = K.Bs("pB", 4)
                pA = [sb("pA%d" % i, [128, 2, TP], BF16, ph) for i in range(2)]
                pAB = K.Bs("pAB", 2)
                sbias = sb("sbias", [128, 5, 128], F32, ph)
                sbB = K.B("sbB")
                oT = sb("oT", [64, 12, TP], BF16, ph)
                oB = K.Bs("oB", 12)
                ocT = sb("ocT", [128, 2, TP], BF16, ph)
                ocB = K.B("ocT")
                cacc = sb("cacc", [128, 2, TP], F32, ph)
                caB = K.B("cacc")
                rec = sb("rec", [128, TP], F32, ph)
                rech = sb("rech", [128, TP], BF16, ph)
                recl = sb("recl", [128, TP], BF16, ph)
                recB = K.B("rec")
                bcs = sb("bcs", [64, TP], F32, ph)
                bcB = K.B("bcs")
                ysb = sb("ysb", [128, KD, HP], F32, ph)
                yB = K.Bs("yB", KD)
                sq = sb("sq", [128, KD, HP], BF16, ph)
                sqB = K.Bs("sqB", KD)
                rstd = sb("rstd", [128, HP], F32, ph)
                rB = K.B("rstd")
                tmp = [sb("tmp%d" % i, [128, HP], F32, ph) for i in range(2)]
                tB = K.Bs("tmpB", 2)
                K.dma("sp", kaT[:], KA[:, :], writes=[resB])
                K.dma("sp", vaT[:], VV[:, 0:2, :].rearrange("(s p) h d -> p s h d", p=128), writes=[resB])
                K.dma("sp", kbc[:], KBd[:, :, S:NT], writes=[resB])
                K.dma("sp", vbc[:], VV[S:NT, 2:8, :].rearrange("(s p) h d -> p s h d", p=128), writes=[resB])
                K.dma("pool", woh[:], w_o[l, 0:768, :].rearrange("(h p) n -> p h n", p=64), writes=[resB])
                K.dma("pool", woc[:], w_o[l, 768:1024, :].rearrange("(k p) n -> p k n", p=128), writes=[resB])
                K.dma("sp", bI[:], biasI[l].rearrange("p (a h q) -> p a h q", a=5, h=6), writes=[resB])
                tiles = [(t * TP, TP, 0) for t in range(S // TP)]
                if do_ctx:
                    tiles.append((S, LC, 1))
                ntl = S // TP
                assert ntl >= 2
                bEv = biasE[l].rearrange("p (c a h q) -> p c a h q", c=4, a=4, h=6)

                def kt0_of(t):
                    return max(0, min(4 * t - 2, NQT - 8))

                def load_q(i):
                    c0, n, X = tiles[i]
                    K.dma("sp", qa[i % 2][:, :, 0:n], QA[:, :, c0:c0 + n], writes=[qaB[i % 2]])

                load_q(0)
                pcount = [0]
                for i, (c0, n, X) in enumerate(tiles):
                    b = i % 2
                    t = c0 // TP
                    if i + 1 < len(tiles):
                        load_q(i + 1)
                    K.dma("sp", qb[:, :, 0:n], QB[:, :, c0:c0 + n], writes=[ib])
                    zc0 = c0 if X == 0 else c0 + 2
                    K.dma("sp", zt[:, :, 0:n + 2], ZZ[:, :, zc0:zc0 + n + 2], writes=[ib])
                    K.dma("sp", cbt[:, :, 0:n], CBd[:, :, c0:c0 + n], writes=[ib])
                    if X == 0:
                        k0 = kt0_of(t)
                        K.dma("sp", kbw[:], KBd[:, :, k0 * 128:k0 * 128 + 1024], writes=[ib])
                        K.dma("sp", vbw[:], VV[k0 * 128:k0 * 128 + 1024, 2:8, :].rearrange("(s p) h d -> p s h d", p=128), writes=[ib])
                    qab = qaB[b]

                    def normalize(obank, hidx, n=n):
                        K.op("dve", (lambda obank=obank, n=n: lambda h: h.reciprocal(out=rec[64:65, 0:n], in_=ps[obank][64:65, 0:n]))(),
                             reads=[psB[obank]], writes=[recB])
                        K.op("dve", (lambda n=n: lambda h: h.tensor_copy(out=rech[64:65, 0:n], in_=rec[64:65, 0:n]))(),
                             reads=[recB], writes=[recB])
                        K.op("dve", (lambda n=n: lambda h: h.tensor_tensor(out=recl[64:65, 0:n], in0=rec[64:65, 0:n], in1=rech[64:65, 0:n], op=ALU.subtract))(),
                             reads=[recB], writes=[recB])
                        K.op("pe", (lambda n=n: lambda h: h.matmul(ps[6][0:64, 0:n], lhsT=onesb[64:65, 0:64], rhs=rech[64:65, 0:n], start=True, stop=False))(),
                             reads=[recB, cB], writes=[psB[6]])
                        K.op("pe", (lambda n=n: lambda h: h.matmul(ps[6][0:64, 0:n], lhsT=onesb[64:65, 0:64], rhs=recl[64:65, 0:n], start=False, stop=True))(),
                             reads=[recB, cB], writes=[psB[6]])
                        K.op("act", (lambda n=n: lambda h: h.activation(out=bcs[:, 0:n], in_=ps[6][0:64, 0:n], func=AF.Copy))(),
                             reads=[psB[6]], writes=[bcB])
                        K.op("dve", (lambda obank=obank, hidx=hidx, n=n: lambda h: h.tensor_tensor(
                            out=oT[:, hidx, 0:n], in0=ps[obank][0:64, 0:n], in1=bcs[:, 0:n], op=ALU.mult))(),
                            reads=[psB[obank], bcB], writes=[oB[hidx]])

                    kts = list(range(NKT)) if X == 0 else list(range(NQT, NKT))
                    for c in range(3):
                        def qk(ki, c=c, n=n, b=b):
                            kt = kts[ki]
                            for half in range(2):
                                sbank = 2 * (ki % 2) + half
                                lo = 64 * half
                                K.op("pe", (lambda kt=kt, c=c, lo=lo, sbank=sbank, b=b, n=n: lambda h: h.matmul(
                                    ps[sbank][:, 0:n], lhsT=kaT[lo:lo + 64, kt * 128:(kt + 1) * 128], rhs=qa[b][lo:lo + 64, c, 0:n],
                                    start=True, stop=True))(), reads=[resB, qab], writes=[psB[sbank]])
                        qk(0)
                        for ki, kt in enumerate(kts):
                            if ki + 1 < len(kts):
                                qk(ki + 1)
                            pp = ki % 2
                            K.op("act", (lambda pp=pp, n=n: lambda h: h.activation(
                                out=pA[pp][:, :, 0:n], in_=psall[:, pp * 1024:(pp + 1) * 1024].rearrange("p (h q) -> p h q", h=2)[:, :, 0:n],
                                func=AF.Exp))(), reads=[psB[2 * pp], psB[2 * pp + 1]], writes=[pAB[pp]])
                            for half in range(2):
                                K.op("pe", (lambda kt=kt, half=half, pp=pp, ki=ki, n=n: lambda h: h.matmul(
                                    ps[4 + half][0:65, 0:n], lhsT=vaT[:, kt, half, :], rhs=pA[pp][:, half, 0:n],
                                    start=(ki == 0), stop=(ki == len(kts) - 1)))(), reads=[resB, pAB[pp]], writes=[psB[4 + half]])
                        normalize(4, c)
                        normalize(5, c + 3)
                    for c in range(3):
                        if X == 0 and (t == 0 or t == ntl - 1):
                            cls0 = 0 if t == 0 else 2
                            K.dma("sp", bE[:], bEv[:, cls0:cls0 + 2, :, 2 * c:2 * c + 2, :], writes=[bEB])
                        for qs in range(n // 128):
                            q0 = qs * 128
                            if X == 0:
                                qt = c0 // 128 + qs
                                if qt < 2 or qt >= NQT - 2:
                                    wk = list(range(0, 4)) if qt < 2 else list(range(NQT - 4, NQT))
                                    cls = qt if qt < 2 else qt - (NQT - 2)
                                    bsrc = lambda a0, a1, half, cls=cls: bE[:, cls, a0:a1, half, :]
                                    brd = bEB
                                else:
                                    wk = list(range(qt - 2, qt + 3))
                                    bsrc = lambda a0, a1, half, c=c: bI[:, a0:a1, 2 * c + half, :]
                                    brd = resB
                            else:
                                wk = []
                            nw = len(wk)
                            for half in range(2):
                                lo = 64 * half
                                for a, kt in enumerate(wk):
                                    bank = 2 * half + (0 if a < 4 else 1)
                                    col = (a % 4) * 128
                                    sl = (kt - k0) * 128
                                    K.op("pe", (lambda bank=bank, col=col, sl=sl, lo=lo, c=c, q0=q0: lambda h: h.matmul(
                                        ps[bank][:, col:col + 128], lhsT=kbw[lo:lo + 64, c, sl:sl + 128],
                                        rhs=qb[lo:lo + 64, c, q0:q0 + 128], start=True, stop=True))(),
                                        reads=[ib], writes=[psB[bank]])
                                for a in range(2):
                                    bank = 2 * half + 1
                                    col = 128 + a * 128
                                    K.op("pe", (lambda bank=bank, col=col, a=a, lo=lo, c=c, q0=q0: lambda h: h.matmul(
                                        ps[bank][:, col:col + 128], lhsT=kbc[lo:lo + 64, c, a * 128:(a + 1) * 128],
                                        rhs=qb[lo:lo + 64, c, q0:q0 + 128], start=True, stop=True))(),
                                        reads=[ib, resB], writes=[psB[bank]])
                            for half in range(2):
                                hh = 2 * c + half
                                pi = pcount[0] % 4
                                pcount[0] += 1
                                if nw:
                                    na = min(nw, 4)
                                    K.op("dve", (lambda half=half, na=na, bsrc=bsrc: lambda h: h.tensor_tensor(
                                        out=sbias[:, 0:na, :], in0=ps[2 * half][:, 0:na * 128].rearrange("p (a q) -> p a q", q=128),
                                        in1=bsrc(0, na, half), op=ALU.add))(),
                                        reads=[psB[2 * half], brd], writes=[sbB])
                                    if nw > 4:
                                        K.op("dve", (lambda half=half, bsrc=bsrc: lambda h: h.tensor_tensor(
                                            out=sbias[:, 4:5, :], in0=ps[2 * half + 1][:, 0:128].rearrange("p (a q) -> p a q", q=128),
                                            in1=bsrc(4, 5, half), op=ALU.add))(),
                                            reads=[psB[2 * half + 1], brd], writes=[sbB])
                                    K.op("act", (lambda pi=pi, nw=nw: lambda h: h.activation(
                                        out=pT[pi][:, 0:nw * 128].rearrange("p (a q) -> p a q", q=128), in_=sbias[:, 0:nw, :], func=AF.Exp))(),
                                        reads=[sbB], writes=[pB[pi]])
                                bank = 2 * half + 1
                                K.op("act", (lambda pi=pi, bank=bank, nw=nw: lambda h: h.activation(
                                    out=pT[pi][:, nw * 128:(nw + 2) * 128], in_=ps[bank][:, 128:384], func=AF.Exp))(),
                                    reads=[psB[bank]], writes=[pB[pi]])
                                nk = nw + 2
                                for a in range(nk):
                                    if a < nw:
                                        lhs = (lambda hh=hh, sl=wk[a] - k0: vbw[:, sl, hh, :])
                                    else:
                                        lhs = (lambda a=a, hh=hh, nw=nw: vbc[:, a - nw, hh, :])
                                    K.op("pe", (lambda a=a, lhs=lhs, half=half, pi=pi, q0=q0, nk=nk: lambda h: h.matmul(
                                        ps[4 + half][0:65, q0:q0 + 128], lhsT=lhs(), rhs=pT[pi][:, a * 128:(a + 1) * 128],
                                        start=(a == 0), stop=(a == nk - 1)))(), reads=[ib, resB, pB[pi]], writes=[psB[4 + half]])
                        normalize(4, 6 + 2 * c)
                        normalize(5, 6 + 2 * c + 1)
                    for c in range(2):
                        w0 = smp[:, l, 124 + c * 3:125 + c * 3]
                        w1 = smp[:, l, 125 + c * 3:126 + c * 3]
                        w2 = smp[:, l, 126 + c * 3:127 + c * 3]
                        K.op("pool", (lambda c=c, w0=w0, n=n: lambda h: h.tensor_scalar(
                            out=cacc[:, 0, 0:n], in0=zt[:, c, 0:n], scalar1=w0, scalar2=None, op0=ALU.mult))(),
                            reads=[ib, cB], writes=[caB])
                        for (wt_, off) in ((w1, 1), (w2, 2)):
                            K.op("pool", (lambda c=c, wt_=wt_, off=off, n=n: lambda h: h.tensor_scalar(
                                out=cacc[:, 1, 0:n], in0=zt[:, c, off:n + off], scalar1=wt_, scalar2=None, op0=ALU.mult))(),
                                reads=[ib, cB], writes=[caB])
                            K.op("pool", (lambda n=n: lambda h: h.tensor_tensor(
                                out=cacc[:, 0, 0:n], in0=cacc[:, 0, 0:n], in1=cacc[:, 1, 0:n], op=ALU.add))(),
                                reads=[caB], writes=[caB])
                        K.op("pool", (lambda c=c, n=n: lambda h: h.tensor_tensor(
                            out=ocT[:, c, 0:n], in0=cacc[:, 0, 0:n], in1=cbt[:, c, 0:n], op=ALU.mult))(),
                            reads=[ib, caB], writes=[ocB])
                    for hf in range(n // HP):
                        o0 = hf * HP
                        K.dma("sp", hT[:], Hin[:, c0 + o0:c0 + o0 + HP].rearrange("(k p) t -> p k t", p=128), writes=hb)
                        for j in range(KD):
                            bank = 6 + j % 2
                            for hh in range(12):
                                K.op("pe", (lambda j=j, hh=hh, bank=bank, o0=o0: lambda h: h.matmul(
                                    ps[bank][:, 0:HP], lhsT=woh[:, hh, j * 128:(j + 1) * 128], rhs=oT[:, hh, o0:o0 + HP],
                                    start=(hh == 0), stop=False))(), reads=[resB, oB[hh]], writes=[psB[bank]])
                            for k in range(2):
                                K.op("pe", (lambda j=j, k=k, bank=bank, o0=o0: lambda h: h.matmul(
                                    ps[bank][:, 0:HP], lhsT=woc[:, k, j * 128:(j + 1) * 128], rhs=ocT[:, k, o0:o0 + HP],
                                    start=False, stop=(k == 1)))(), reads=[resB, ocB], writes=[psB[bank]])
                            K.op("act", (lambda j=j, bank=bank: lambda h: h.activation(out=ysb[:, j, :], in_=ps[bank][:, 0:HP], func=AF.Copy))(),
                                 reads=[psB[bank]], writes=[yB[j]])
                            K.op("act", (lambda j=j, bank=bank: lambda h: h.activation(out=sq[:, j, :], in_=ps[bank][:, 0:HP], func=AF.Square))(),
                                 reads=[psB[bank]], writes=[sqB[j]])
                        for k in range(KD):
                            K.op("pe", (lambda k=k: lambda h: h.matmul(ps[0][:, 0:HP], lhsT=onesm[:], rhs=sq[:, k, :],
                                                                       start=(k == 0), stop=(k == KD - 1)))(),
                                 reads=[sqB[k], cB], writes=[psB[0]])
                        K.op("act", lambda h: h.activation(out=rstd[:], in_=ps[0][:, 0:HP], func=AF.Sqrt, bias=EPS, scale=1.0),
                             reads=[psB[0]], writes=[rB])
                        K.op("dve", lambda h: h.reciprocal(out=rstd[:], in_=rstd[:]), reads=[rB], writes=[rB])
                        for j in range(KD):
                            tb = j % 2
                            K.op("dve", (lambda j=j, tb=tb, X=X: lambda h: h.scalar_tensor_tensor(
                                out=tmp[tb][:], in0=ysb[:, j, :], scalar=par[:, l, X, 5, j:j + 1], in1=rstd[:],
                                op0=ALU.mult, op1=ALU.mult))(), reads=[yB[j], rB], writes=[tB[tb]])
                            K.op("pool", (lambda j=j, tb=tb: lambda h: h.tensor_tensor(
                                out=hT[:, j, :], in0=hT[:, j, :], in1=tmp[tb][:], op=ALU.add))(),
                                reads=[tB[tb]], writes=[hb[j]])
                        K.dma("sp", Hout[:, c0 + o0:c0 + o0 + HP].rearrange("(k p) t -> p k t", p=128), hT[:], reads=hb)
                K.flush()

        phase_ffn(0, 0, xT, HA, "x", "HA", True)
        phase_proj(0, HA, "HA")
        phase_attn(0, HA, HB, "HA", "HB", True)
        phase_ffn(0, 1, HB, HA, "HB", "HA", True)
        phase_ffn(1, 0, HA, HB, "HA", "HB", True)
        phase_proj(1, HB, "HB")
        phase_attn(1, HB, HA, "HB", "HA", False)
        phase_ffn(1, 1, HA, None, "HA", "none", False, final_out=outT)
    return nc


def _rope_tables(S):
    pos = np.arange(S, dtype=np.int32)
    row = (pos // GW).astype(np.float32)
    col = (pos % GW).astype(np.float32)
    freq = (1.0 / (np.float32(10000.0) ** (np.arange(16, dtype=np.float32) / np.float32(16)))).astype(np.float32)
    C = np.zeros((128, S), np.float32)
    Sg = np.zeros((128, S), np.float32)
    for p in range(128):
        d = p % 64
        a, j, f = d // 32, (d // 16) % 2, d % 16
        ang = ((row if a == 0 else col) * freq[f]).astype(np.float32)
        C[p] = np.cos(ang)
        Sg[p] = np.sin(ang) * (-1.0 if j == 0 else 1.0)
    return C, Sg


def _bias_mats(rpb_l, qt, kts, rows):
    out = np.full((128, len(kts), 6, 128), NEG, np.float32)
    j = np.arange(128)
    for a, kt in enumerate(kts):
        kr = 2 * kt + j // 64
        kc = j % 64
        qr = 2 * qt + j // 64
        qc = j % 64
        rs = np.clip(qr - 4, 0, rows - 8)
        cs = np.clip(qc - 8, 0, GW - 16)
        okr = (kr[:, None] >= rs[None, :]) & (kr[:, None] < rs[None, :] + 8)
        okc = (kc[:, None] >= cs[None, :]) & (kc[:, None] < cs[None, :] + 16)
        ok = okr & okc
        dr = np.clip(kr[:, None] - qr[None, :] + 7, 0, 14)
        dc = np.clip(kc[:, None] - qc[None, :] + 15, 0, 30)
        for hh in range(6):
            vals = rpb_l[hh][dr, dc]
            out[:, a, hh, :] = np.where(ok, vals, np.float32(NEG))
    return out


def _prep_inputs(inp, S):
    x, c, ctx, c_ctx = inp["x"], inp["c"], inp["ctx"], inp["c_ctx"]
    B = x.shape[0]
    rows = S // GW
    NQT = S // 128
    w_in = inp["w_in"]
    perm = np.arange(64) ^ 16
    cols = []
    qa_h = lambda h: np.arange(h * 64, (h + 1) * 64)
    for cc in range(3):
        cols += [qa_h(cc), qa_h(cc + 3)]
    for cc in range(3):
        cols += [qa_h(cc)[perm], qa_h(cc + 3)[perm]]
    ka0 = 768
    cols += [np.arange(ka0, ka0 + 64), np.arange(ka0 + 64, ka0 + 128)]
    cols += [np.arange(ka0, ka0 + 64)[perm], np.arange(ka0 + 64, ka0 + 128)[perm]]
    cols += [np.arange(384, 768)]
    cols += [np.arange(768 + 256, 768 + 256 + 384)]
    cx0 = 768 + 1024
    cols += [np.arange(cx0, cx0 + 768)]
    cols += [np.arange(768 + 128, 768 + 256)]
    cols += [np.arange(768 + 256 + 384, 768 + 1024)]
    cols = np.concatenate(cols)
    assert cols.shape[0] == NW
    w_inx = np.ascontiguousarray(w_in[:, :, cols])
    smallp = np.zeros((2, 128, NS), np.float32)
    p = np.arange(128)
    for l in range(2):
        smallp[l, :, 0:72] = inp["b_ada"][l].reshape(72, 128).T
        smallp[l, :, 72:120] = inp["norm_g"][l].reshape(6, 8, 128).transpose(2, 0, 1).reshape(128, 48)
        smallp[l, :, 120] = inp["qk_g"][l, 0][p % 64]
        smallp[l, :, 121] = inp["qk_g"][l, 0][(p % 64) ^ 16]
        smallp[l, :, 122] = inp["qk_g"][l, 1][p % 64]
        smallp[l, :, 123] = inp["qk_g"][l, 1][(p % 64) ^ 16]
        smallp[l, :, 124:130] = inp["conv_w"][l].reshape(3, 2, 128).transpose(2, 1, 0).reshape(128, 6)
    C, Sg = _rope_tables(S)
    biasI = np.zeros((2, 128, 5 * 6 * 128), np.float32)
    biasE = np.zeros((2, 128, 4 * 4 * 6 * 128), np.float32)
    for l in range(2):
        biasI[l] = _bias_mats(inp["rpb"][l], 2, [0, 1, 2, 3, 4], rows).reshape(128, -1)
        be = [_bias_mats(inp["rpb"][l], qt, kts, rows) for qt, kts in
              ((0, [0, 1, 2, 3]), (1, [0, 1, 2, 3]), (NQT - 2, list(range(NQT - 4, NQT))), (NQT - 1, list(range(NQT - 4, NQT))))]
        biasE[l] = np.stack(be, axis=1).reshape(128, -1)
    shared = dict(smallp=smallp, w_ada=np.ascontiguousarray(inp["w_ada"]), w_inx=w_inx, w_o=np.ascontiguousarray(inp["w_o"]),
                  ffn_wi=np.ascontiguousarray(inp["ffn_wi"]), ffn_wo=np.ascontiguousarray(inp["ffn_wo"]),
                  ropeC=C, ropeS=Sg, biasI=biasI, biasE=biasE)
    maps = []
    for b in range(B):
        m = dict(shared)
        m["xT"] = np.ascontiguousarray(np.concatenate([x[b].T, ctx[b].T], axis=1))
        cv = np.stack([c[b].reshape(8, 128).T, c_ctx.reshape(8, 128).T], axis=-1)
        m["cvec"] = np.ascontiguousarray(cv.astype(np.float32))
        maps.append(m)
    return maps


_CACHE = {}


def run(inp, debug=False):
    inp = {k: np.asarray(v, dtype=np.float32) for k, v in inp.items()}
    S = inp["x"].shape[1]
    B = inp["x"].shape[0]
    key = (S, debug)
    if key not in _CACHE:
        _CACHE[key] = build_program(S, debug)
    nc = _CACHE[key]
    maps = _prep_inputs(inp, S)
    res = run_bass_kernel_spmd(nc, maps, core_ids=list(range(B)))
    out = np.stack([np.ascontiguousarray(r["outT"].T) for r in res.results], axis=0)
    return out.astype(np.float32), res


def kernel(**inputs):
    out, _ = run(inputs)
    return out
```

```python
import numpy as np
from contextlib import ExitStack
import concourse.bass as bass
import concourse.mybir as mybir
from concourse.bass_utils import run_bass_kernel_spmd

F32 = mybir.dt.float32
BF16 = mybir.dt.bfloat16
AF = mybir.ActivationFunctionType
ALU = mybir.AluOpType

D = 1024
KD = 8
FFN = 2816
KF = 22
LC = 256
GW = 64
NEG = -30000.0
EPS = 1e-6
NS = 132
NW = 3072
TF = 256
NDMA = 24
SAME_ENG_SYNC = True


class Buf:
    __slots__ = ("name", "w", "rc", "rd")

    def __init__(self, name):
        self.name = name
        self.w = []
        self.rc = {}
        self.rd = []


class Eng:
    def __init__(self, name, sem):
        self.name = name
        self.sem = sem
        self.base = 0
        self.reset()

    def reset(self):
        self.items = []
        self.marks = []
        self.waited = {}
        self.vals = []


class KB:
    HM = {"pe": "tensor", "act": "scalar", "dve": "vector", "pool": "gpsimd", "sp": "sync"}

    def __init__(self, nc, es):
        self.nc = nc
        self.engs = {}
        for n in self.HM:
            self.engs[n] = Eng(n, es.enter_context(nc.semaphore("s_" + n)))
        self.dsem = [es.enter_context(nc.semaphore("d%d" % i)) for i in range(NDMA)]
        self.dcnt = [0] * NDMA
        self.drr = 0
        self.bufs = []

    def B(self, name):
        b = Buf(name)
        self.bufs.append(b)
        return b

    def Bs(self, name, n):
        return [self.B("%s%d" % (name, i)) for i in range(n)]

    def _wait(self, eng, tok, nosame=False):
        if tok[0] == "c":
            pe, seq = tok[1], tok[2]
            if pe is eng and (nosame or eng.name == "pe" or eng.name == "sp" or not SAME_ENG_SYNC):
                return
            if eng.waited.get(pe.name, -1) >= seq:
                return
            eng.waited[pe.name] = seq
            pe.marks[seq] = True
            eng.items.append(("wc", pe, seq))
        else:
            idx, val = tok[1], tok[2]
            if eng.waited.get(("d", idx), 0) >= val:
                return
            eng.waited[("d", idx)] = val
            eng.items.append(("wd", idx, val))

    def _deps(self, eng, reads, writes, nosame=False):
        for b in reads:
            for t in b.w:
                self._wait(eng, t)
        for b in writes:
            for t in b.w:
                self._wait(eng, t, nosame)
            for n, s in b.rc.items():
                self._wait(eng, ("c", self.engs[n], s))
            for t in b.rd:
                self._wait(eng, t)

    def op(self, en, fn, reads=(), writes=(), nosame=False):
        eng = self.engs[en]
        self._deps(eng, reads, writes, nosame)
        seq = len(eng.marks)
        eng.marks.append(False)
        eng.items.append(("op", fn, seq))
        tok = ("c", eng, seq)
        for b in writes:
            b.w = [tok]
            b.rc = {}
            b.rd = []
        for b in reads:
            b.rc[en] = seq

    def dma(self, en, out, in_, reads=(), writes=()):
        eng = self.engs[en]
        self._deps(eng, reads, writes)
        idx = self.drr
        self.drr = (self.drr + 1) % NDMA
        self.dcnt[idx] += 16
        tok = ("d", idx, self.dcnt[idx])
        eng.items.append(("dma", out, in_, idx))
        for b in writes:
            b.w = [tok]
            b.rc = {}
            b.rd = []
        for b in reads:
            b.rd.append(tok)

    def flush(self):
        engs = self.engs
        for e in engs.values():
            if e.marks:
                e.marks[-1] = True
        for f in engs.values():
            for e in engs.values():
                if e is not f and e.marks:
                    f.items.append(("wc", e, len(e.marks) - 1))
            for i in range(NDMA):
                if self.dcnt[i] > 0:
                    f.items.append(("wd", i, self.dcnt[i]))
        for e in engs.values():
            v = e.base
            vals = []
            for m in e.marks:
                if m:
                    v += 1
                vals.append(v)
            e.vals = vals
            e.newbase = v
        dsem = self.dsem
        with self.nc.Block() as block:
            for name, e in engs.items():
                def mk(e):
                    def body(h):
                        for it in e.items:
                            k = it[0]
                            if k == "op":
                                ins = it[1](h)
                                if e.marks[it[2]]:
                                    ins.then_inc(e.sem, 1)
                            elif k == "wc":
                                h.wait_ge(it[1].sem, it[1].vals[it[2]])
                            elif k == "wd":
                                h.wait_ge(dsem[it[1]], it[2])
                            else:
                                h.dma_start(out=it[1], in_=it[2], allow_slow_non_contiguous=True).then_inc(dsem[it[3]], 16)
                    return body
                getattr(block, self.HM[name])(mk(e))
        for e in engs.values():
            e.base = e.newbase
            e.reset()
        for b in self.bufs:
            b.w = []
            b.rc = {}
            b.rd = []


def build_program(S, debug=False):
    NT = S + LC
    NQT = S // 128
    NKT = NT // 128
    nc = bass.Bass("TRN2", target_bir_lowering=False)
    dk = "ExternalOutput" if debug else "Internal"

    def din(name, shape, dt=F32):
        return nc.dram_tensor(name, list(shape), dt, kind="ExternalInput").ap()

    def dscr(name, shape, dt=F32):
        return nc.dram_tensor(name, list(shape), dt, kind=dk).ap()

    xT = din("xT", [D, NT])
    cvec = din("cvec", [128, KD, 2])
    smallp = din("smallp", [2, 128, NS])
    w_ada = din("w_ada", [2, D, 9 * D])
    w_inx = din("w_inx", [2, D, NW])
    w_o = din("w_o", [2, D, D])
    ffn_wi = din("ffn_wi", [2, 2, D, 2 * FFN])
    ffn_wo = din("ffn_wo", [2, 2, FFN, D])
    ropeC = din("ropeC", [128, S])
    ropeS = din("ropeS", [128, S])
    biasI = din("biasI", [2, 128, 5 * 6 * 128])
    biasE = din("biasE", [2, 128, 4 * 4 * 6 * 128])
    outT = nc.dram_tensor("outT", [D, S], F32, kind="ExternalOutput").ap()

    HA = dscr("HA", [D, NT])
    HB = dscr("HB", [D, NT])
    QA = dscr("QA", [128, 3, NT], BF16)
    KA = dscr("KA", [128, NT], BF16)
    QB = dscr("QB", [128, 3, NT], BF16)
    KBd = dscr("KBd", [128, 3, NT], BF16)
    VV = dscr("VV", [NT, 8, 65], BF16)
    ZZ = dscr("ZZ", [128, 2, NT + 4])
    CBd = dscr("CBd", [128, 2, NT])
    RS = dscr("RS", [4, 1, 2, 512])

    es = ExitStack()
    with es:
        K = KB(nc, es)

        uid = [0]

        def sb(name, shape, dt=F32, stack=es):
            uid[0] += 1
            return stack.enter_context(nc.sbuf_tensor("%s_%d" % (name, uid[0]), list(shape), dt))

        psall = es.enter_context(nc.psum_tensor("psall", [128, 4096], F32))
        ps = [psall[:, i * 512:(i + 1) * 512] for i in range(8)]
        psB = K.Bs("psB", 8)
        par = sb("par", [128, 2, 2, 9, KD])
        smp = sb("smp", [128, 2, NS])
        qsc = sb("qsc", [128, 2, 2])
        onesm = sb("onesm", [128, 128], BF16)
        bd64 = sb("bd64", [128, 128], BF16)
        onesb = sb("onesb", [128, 64], BF16)
        cB = K.B("consts")
        dramB = {}

        def DB(name, i):
            key = (name, i)
            if key not in dramB:
                dramB[key] = K.B("dram_%s_%d" % (name, i))
            return dramB[key]

        with ExitStack() as ph:
            cv = sb("cv", [128, KD, 2], F32, ph)
            scv = sb("scv", [128, KD, 2], F32, ph)
            wa = [sb("wa%d" % i, [128, KD, 1024], F32, ph) for i in range(2)]
            waB = K.Bs("waB", 2)
            mods = sb("mods", [128, 2, 72], F32, ph)
            cvB, scB, modB = K.B("cv"), K.B("scv"), K.B("mods")
            K.op("dve", lambda h: h.memset(onesm[:], 1.0 / 1024), writes=[cB])
            K.op("dve", lambda h: h.memset(bd64[:], 0.0), writes=[cB])
            K.op("dve", lambda h: h.memset(bd64[0:64, 0:64], 1.0 / 64), writes=[cB])
            K.op("dve", lambda h: h.memset(bd64[64:128, 64:128], 1.0 / 64), writes=[cB])
            K.op("dve", lambda h: h.memset(onesb[:], 1.0), writes=[cB])
            K.dma("sp", cv[:], cvec[:, :, :], writes=[cvB])
            K.dma("sp", smp[:], smallp.rearrange("l p n -> p l n"), writes=[cB])
            K.op("act", lambda h: h.activation(out=scv[:], in_=cv[:], func=AF.Silu), reads=[cvB], writes=[scB])
            it = 0
            for l in range(2):
                for i in range(9):
                    b = it % 2
                    it += 1
                    K.dma("sp", wa[b][:], w_ada[l].rearrange("(k p) n -> p k n", p=128)[:, :, i * 1024:(i + 1) * 1024],
                          writes=[waB[b]])
                    for j in range(8):
                        idx = i * 8 + j
                        for k in range(KD):
                            K.op("pe", (lambda b=b, j=j, k=k, idx=idx: lambda h: h.matmul(
                                ps[0][:, 2 * idx:2 * idx + 2], lhsT=wa[b][:, k, j * 128:(j + 1) * 128], rhs=scv[:, k, :],
                                start=(k == 0), stop=(k == KD - 1)))(), reads=[waB[b], scB], writes=[psB[0]])
                pv = ps[0][:, 0:144].rearrange("p (n t) -> p n t", t=2)
                for X in range(2):
                    K.op("dve", (lambda X=X, l=l, pv=pv: lambda h: h.tensor_tensor(
                        out=mods[:, X, :], in0=pv[:, :, X], in1=smp[:, l, 0:72], op=ALU.add))(),
                        reads=[psB[0], cB], writes=[modB])

                    def m(i, X=X):
                        return mods[:, X, i * 8:(i + 1) * 8]

                    def g(gi, l=l):
                        return smp[:, l, 72 + gi * 8:72 + (gi + 1) * 8]
                    for (kind, mi, gi, wres) in ((0, 1, 0, None), (3, 4, 2, None), (6, 7, 4, None)):
                        K.op("dve", (lambda kind=kind, mi=mi, gi=gi, X=X, l=l, m=m, g=g: lambda h: h.scalar_tensor_tensor(
                            out=par[:, l, X, kind, :], in0=m(mi), scalar=1.0, in1=g(gi), op0=ALU.add, op1=ALU.mult))(),
                            reads=[modB, cB], writes=[cB])
                    for (kind, mi) in ((1, 0), (4, 3), (7, 6)):
                        K.op("dve", (lambda kind=kind, mi=mi, X=X, l=l, m=m: lambda h: h.tensor_copy(
                            out=par[:, l, X, kind, :], in_=m(mi)))(), reads=[modB], writes=[cB])
                    for (kind, mi, gi, wres) in ((2, 2, 1, 0.5), (5, 5, 3, 1.0), (8, 8, 5, 0.5)):
                        K.op("dve", (lambda kind=kind, mi=mi, gi=gi, wres=wres, X=X, l=l, m=m, g=g: lambda h: h.scalar_tensor_tensor(
                            out=par[:, l, X, kind, :], in0=m(mi), scalar=wres, in1=g(gi), op0=ALU.mult, op1=ALU.mult))(),
                            reads=[modB, cB], writes=[cB])
                K.op("dve", (lambda l=l: lambda h: h.tensor_scalar(
                    out=qsc[:, l, :], in0=smp[:, l, 120:122], scalar1=0.125, scalar2=None, op0=ALU.mult))(),
                    reads=[cB], writes=[cB])
            K.flush()

        def phase_ffn(l, which, Hin, Hout, hin_name, hout_name, do_ctx, final_out=None):
            kA, kB_, kG = (0, 1, 2) if which == 0 else (6, 7, 8)
            with ExitStack() as ph:
                wi = sb("wi", [128, KD, 2 * FFN], BF16, ph)
                wo = sb("wo", [128, KF, D], BF16, ph)
                wiB = K.Bs("wiB", KD)
                woB = K.Bs("woB", 2)
                hT = [sb("hT%d" % i, [128, KD, TF], F32, ph) for i in range(2)]
                hB = [K.Bs("hB%d_" % i, KD) for i in range(2)]
                u2 = [sb("u%d" % i, [128, KD, TF], BF16, ph) for i in range(2)]
                u2B = [K.Bs("uB%d_" % i, KD) for i in range(2)]
                rstd2 = sb("rstd2", [128, TF], F32, ph)
                r2B = K.B("rstd2")
                act = sb("actb", [128, KF, TF], BF16, ph)
                actB = K.Bs("actB", KF)
                ysb = sb("ysb", [128, KD, TF], F32, ph)
                yB = K.Bs("yB", KD)
                rstd = sb("rstd", [128, TF], F32, ph)
                rB = K.B("rstd")
                tmp = [sb("tmp%d" % i, [128, TF], F32, ph) for i in range(2)]
                tB = K.Bs("tmpB", 2)
                sg = [sb("sg%d" % i, [128, TF], F32, ph) for i in range(2)]
                sgB = K.Bs("sgB", 2)
                for k in range(KD):
                    K.dma("pool", wi[:, k, :], ffn_wi[l, which, k * 128:(k + 1) * 128, :], writes=[wiB[k]])
                wov = ffn_wo[l, which].rearrange("(m p) n -> p m n", p=128)
                K.dma("pool", wo[:, 0:11, :], wov[:, 0:11, :], writes=[woB[0]])
                K.dma("pool", wo[:, 11:22, :], wov[:, 11:22, :], writes=[woB[1]])
                tiles = [(t * TF, 0) for t in range(S // TF)]
                if do_ctx:
                    tiles.append((S, 1))

                def load(i):
                    c0, X = tiles[i]
                    b = i % 2
                    K.dma("sp", hT[b][:], Hin[:, c0:c0 + TF].rearrange("(k p) t -> p k t", p=128),
                          reads=[DB(hin_name, c0 // TF)], writes=hB[b])

                def stats(ub, ubB, rs, rsB):
                    for k in range(KD):
                        K.op("pe", (lambda k=k, ub=ub: lambda h: h.matmul(ps[0][:, 0:TF], lhsT=onesm[:], rhs=ub[:, k, :],
                                                                          start=(k == 0), stop=(k == KD - 1)))(),
                             reads=[ubB[k], cB], writes=[psB[0]])
                    K.op("act", (lambda rs=rs: lambda h: h.activation(out=rs[:], in_=ps[0][:, 0:TF], func=AF.Sqrt, bias=EPS, scale=1.0))(),
                         reads=[psB[0]], writes=[rsB])
                    K.op("dve", (lambda rs=rs: lambda h: h.reciprocal(out=rs[:], in_=rs[:]))(), reads=[rsB], writes=[rsB])

                def pre_sq(i, k):
                    b = i % 2
                    K.op("act", (lambda k=k, b=b: lambda h: h.activation(out=u2[b][:, k, :], in_=hT[b][:, k, :], func=AF.Square))(),
                         reads=[hB[b][k]], writes=[u2B[b][k]])

                def pre_u(i, k):
                    b = i % 2
                    X = tiles[i][1]
                    tb = k % 2
                    K.op("dve", (lambda k=k, tb=tb, b=b, X=X: lambda h: h.scalar_tensor_tensor(
                        out=tmp[tb][:], in0=hT[b][:, k, :], scalar=par[:, l, X, kA, k:k + 1], in1=rstd[:],
                        op0=ALU.mult, op1=ALU.mult))(), reads=[hB[b][k], rB], writes=[tB[tb]])
                    K.op("act", (lambda k=k, tb=tb, b=b, X=X: lambda h: h.activation(
                        out=u2[b][:, k, :], in_=tmp[tb][:], func=AF.Identity, bias=par[:, l, X, kB_, k:k + 1], scale=1.0))(),
                        reads=[tB[tb]], writes=[u2B[b][k]])

                load(0)
                for k in range(KD):
                    pre_sq(0, k)
                stats(u2[0], u2B[0], rstd, rB)
                for k in range(KD):
                    pre_u(0, k)
                for i, (c0, X) in enumerate(tiles):
                    b = i % 2
                    nxt = i + 1 < len(tiles)
                    if nxt:
                        load(i + 1)
                    h_, hb = hT[b], hB[b]
                    u, uB = u2[b], u2B[b]
                    for m in range(KF):
                        pb = m % 2
                        for half, bank in ((0, 1 + pb), (1, 3 + pb)):
                            for k in range(KD):
                                c_ = half * FFN + m * 128
                                K.op("pe", (lambda k=k, c_=c_, bank=bank, u=u: lambda h: h.matmul(
                                    ps[bank][:, 0:TF], lhsT=wi[:, k, c_:c_ + 128], rhs=u[:, k, :],
                                    start=(k == 0), stop=(k == KD - 1)))(), reads=[wiB[k], uB[k]], writes=[psB[bank]])
                        K.op("act", (lambda pb=pb: lambda h: h.activation(out=sg[pb][:], in_=ps[1 + pb][:, 0:TF], func=AF.Silu))(),
                             reads=[psB[1 + pb]], writes=[sgB[pb]])
                        K.op("dve", (lambda pb=pb, m=m: lambda h: h.tensor_tensor(
                            out=act[:, m, :], in0=sg[pb][:], in1=ps[3 + pb][:, 0:TF], op=ALU.mult))(),
                            reads=[sgB[pb], psB[3 + pb]], writes=[actB[m]])
                        if nxt and 6 <= m < 6 + KD:
                            pre_sq(i + 1, m - 6)
                    if nxt:
                        stats(u2[1 - b], u2B[1 - b], rstd, rB)
                    for j in range(KD):
                        bank = 5 + j % 2
                        for m in range(KF):
                            K.op("pe", (lambda j=j, m=m, bank=bank: lambda h: h.matmul(
                                ps[bank][:, 0:TF], lhsT=wo[:, m, j * 128:(j + 1) * 128], rhs=act[:, m, :],
                                start=(m == 0), stop=(m == KF - 1)))(), reads=[woB[m // 11], actB[m]], writes=[psB[bank]])
                        if nxt:
                            pre_u(i + 1, j)
                        K.op("act", (lambda j=j, bank=bank: lambda h: h.activation(out=ysb[:, j, :], in_=ps[bank][:, 0:TF], func=AF.Copy))(),
                             reads=[psB[bank]], writes=[yB[j]])
                        K.op("act", (lambda j=j, bank=bank, u=u: lambda h: h.activation(out=u[:, j, :], in_=ps[bank][:, 0:TF], func=AF.Square))(),
                             reads=[psB[bank]], writes=[uB[j]])
                    stats(u, uB, rstd2, r2B)
                    for j in range(KD):
                        tb = j % 2
                        K.op("dve", (lambda j=j, tb=tb, X=X: lambda h: h.scalar_tensor_tensor(
                            out=tmp[tb][:], in0=ysb[:, j, :], scalar=par[:, l, X, kG, j:j + 1], in1=rstd2[:],
                            op0=ALU.mult, op1=ALU.mult))(), reads=[yB[j], r2B], writes=[tB[tb]])
                        K.op("pool", (lambda j=j, tb=tb, h_=h_: lambda h: h.tensor_tensor(
                            out=h_[:, j, :], in0=h_[:, j, :], in1=tmp[tb][:], op=ALU.add))(),
                            reads=[tB[tb]], writes=[hb[j]])
                    if final_out is not None and X == 0:
                        K.dma("sp", final_out[:, c0:c0 + TF].rearrange("(k p) t -> p k t", p=128), h_[:], reads=hb)
                    else:
                        K.dma("sp", Hout[:, c0:c0 + TF].rearrange("(k p) t -> p k t", p=128), h_[:],
                              reads=hb, writes=[DB(hout_name, c0 // TF)])
                K.flush()

        def phase_proj(l, Hin, hin_name):
            TP = 512
            with ExitStack() as ph:
                win = sb("win", [128, KD, NW], BF16, ph)
                winB = K.Bs("winB", KD)
                hT = [sb("hT%d" % i, [128, KD, TP], F32, ph) for i in range(2)]
                hB = [K.Bs("hB%d_" % i, KD) for i in range(2)]
                rc = [sb("rc%d" % i, [128, TP], F32, ph) for i in range(2)]
                rs_ = [sb("rs%d" % i, [128, TP], F32, ph) for i in range(2)]
                rcB = K.Bs("rcB", 2)
                u = sb("u", [128, KD, TP], BF16, ph)
                uB = K.Bs("uB", KD)
                rstd = sb("rstd", [128, TP], F32, ph)
                rB = K.B("rstd")
                tmp = [sb("tmp%d" % i, [128, TP], F32, ph) for i in range(2)]
                tB = K.Bs("tmpB", 2)
                sqp = sb("sqp", [128, TP], BF16, ph)
                sqB = K.B("sqp")
                rq = sb("rq", [128, TP], F32, ph)
                rqB = K.B("rq")
                t1 = sb("t1", [128, TP], F32, ph)
                t2 = sb("t2", [128, TP], F32, ph)
                t1B, t2B = K.B("t1"), K.B("t2")
                qa = sb("qa", [128, 3, TP], BF16, ph)
                ka = sb("ka", [128, TP], BF16, ph)
                qb = sb("qb", [128, 3, TP], BF16, ph)
                kb = sb("kb", [128, 3, TP], BF16, ph)
                zz = sb("zz", [128, 2, TP], F32, ph)
                cbs = sb("cbs", [128, 2, TP], F32, ph)
                cxs = sb("cxs", [128, TP], F32, ph)
                vv = sb("vv", [128, 4, 8, 65], BF16, ph)
                zer = sb("zer", [128, 2, 2], F32, ph)
                qaB, kaB, qbB, kbB, zzB, cbB, cxB, vvB, zeB = (K.B(n) for n in ("qa", "ka", "qb", "kb", "zz", "cbs", "cxs", "vv", "zer"))
                for k in range(KD):
                    K.dma("pool", win[:, k, :], w_inx[l, k * 128:(k + 1) * 128, :], writes=[winB[k]])
                K.op("dve", lambda h: h.memset(vv[:], 1.0), writes=[vvB])
                K.op("dve", lambda h: h.memset(zer[:], 0.0), writes=[zeB])
                K.dma("sp", ZZ[:, :, 0:1], zer[:, :, 0:1], reads=[zeB])
                K.dma("sp", ZZ[:, :, S + 1:S + 3], zer[:, :, 0:2], reads=[zeB])
                K.dma("sp", ZZ[:, :, NT + 3:NT + 4], zer[:, :, 0:1], reads=[zeB])
                tiles = [(t * TP, TP, 0) for t in range(S // TP)] + [(S, LC, 1)]

                def load(i):
                    c0, n, X = tiles[i]
                    b = i % 2
                    K.dma("sp", hT[b][:, :, 0:n], Hin[:, c0:c0 + n].rearrange("(k p) t -> p k t", p=128),
                          reads=[DB(hin_name, j) for j in range(c0 // TF, (c0 + n) // TF)], writes=hB[b])
                    if X == 0:
                        K.dma("sp", rc[b][:], ropeC[:, c0:c0 + n], writes=[rcB[b]])
                        K.dma("sp", rs_[b][:], ropeS[:, c0:c0 + n], writes=[rcB[b]])
                load(0)
                for i, (c0, n, X) in enumerate(tiles):
                    b = i % 2
                    if i + 1 < len(tiles):
                        load(i + 1)
                    h_, hb = hT[b], hB[b]
                    for k in range(KD):
                        K.op("act", (lambda k=k, h_=h_, n=n: lambda h: h.activation(out=u[:, k, 0:n], in_=h_[:, k, 0:n], func=AF.Square))(),
                             reads=[hb[k]], writes=[uB[k]])
                    for k in range(KD):
                        K.op("pe", (lambda k=k, n=n: lambda h: h.matmul(ps[0][:, 0:n], lhsT=onesm[:], rhs=u[:, k, 0:n],
                                                                       start=(k == 0), stop=(k == KD - 1)))(),
                             reads=[uB[k]], writes=[psB[0]])
                    K.op("act", (lambda n=n: lambda h: h.activation(out=rstd[:, 0:n], in_=ps[0][:, 0:n], func=AF.Sqrt, bias=EPS, scale=1.0))(),
                         reads=[psB[0]], writes=[rB])
                    K.op("dve", (lambda n=n: lambda h: h.reciprocal(out=rstd[:, 0:n], in_=rstd[:, 0:n]))(), reads=[rB], writes=[rB])
                    for k in range(KD):
                        tb = k % 2
                        K.op("dve", (lambda k=k, tb=tb, h_=h_, X=X, n=n: lambda h: h.scalar_tensor_tensor(
                            out=tmp[tb][:, 0:n], in0=h_[:, k, 0:n], scalar=par[:, l, X, 3, k:k + 1], in1=rstd[:, 0:n],
                            op0=ALU.mult, op1=ALU.mult))(), reads=[hb[k], rB], writes=[tB[tb]])
                        K.op("act", (lambda k=k, tb=tb, X=X, n=n: lambda h: h.activation(
                            out=u[:, k, 0:n], in_=tmp[tb][:, 0:n], func=AF.Identity, bias=par[:, l, X, 4, k:k + 1], scale=1.0))(),
                            reads=[tB[tb]], writes=[uB[k]])

                    def proj(g, bank):
                        for k in range(KD):
                            K.op("pe", (lambda k=k, g=g, bank=bank, n=n: lambda h: h.matmul(
                                ps[bank][:, 0:n], lhsT=win[:, k, g * 128:(g + 1) * 128], rhs=u[:, k, 0:n],
                                start=(k == 0), stop=(k == KD - 1)))(), reads=[winB[k], uB[k]], writes=[psB[bank]])

                    ng = [(c, 3 + c, ("q", 0), ("q", 1), qa[:, c, 0:n], qaB) for c in range(3)]
                    ng.append((6, 7, ("k", 122), ("k", 123), ka[:, 0:n], kaB))
                    for gi, (g0, g1, s0, s1, dst, dB) in enumerate(ng):
                        ba, bb_ = 1 + (gi % 2) * 2, 2 + (gi % 2) * 2
                        proj(g0, ba)
                        if X == 0:
                            proj(g1, bb_)
                        sc0 = qsc[:, l, 0:1] if s0[0] == "q" else smp[:, l, s0[1]:s0[1] + 1]
                        sc1 = qsc[:, l, 1:2] if s1[0] == "q" else smp[:, l, s1[1]:s1[1] + 1]
                        K.op("act", (lambda ba=ba, n=n: lambda h: h.activation(out=sqp[:, 0:n], in_=ps[ba][:, 0:n], func=AF.Square))(),
                             reads=[psB[ba]], writes=[sqB])
                        K.op("pe", (lambda n=n: lambda h: h.matmul(ps[5][:, 0:n], lhsT=bd64[:], rhs=sqp[:, 0:n], start=True, stop=True))(),
                             reads=[sqB, cB], writes=[psB[5]])
                        K.op("act", (lambda n=n: lambda h: h.activation(out=rq[:, 0:n], in_=ps[5][:, 0:n], func=AF.Sqrt, bias=EPS, scale=1.0))(),
                             reads=[psB[5]], writes=[rqB])
                        K.op("dve", (lambda n=n: lambda h: h.reciprocal(out=rq[:, 0:n], in_=rq[:, 0:n]))(), reads=[rqB], writes=[rqB])
                        if X == 0:
                            K.op("dve", (lambda ba=ba, sc0=sc0, n=n: lambda h: h.scalar_tensor_tensor(
                                out=t1[:, 0:n], in0=ps[ba][:, 0:n], scalar=sc0, in1=rq[:, 0:n], op0=ALU.mult, op1=ALU.mult))(),
                                reads=[psB[ba], rqB], writes=[t1B])
                            K.op("dve", (lambda bb_=bb_, sc1=sc1, n=n: lambda h: h.scalar_tensor_tensor(
                                out=t2[:, 0:n], in0=ps[bb_][:, 0:n], scalar=sc1, in1=rq[:, 0:n], op0=ALU.mult, op1=ALU.mult))(),
                                reads=[psB[bb_], rqB], writes=[t2B])
                            K.op("pool", (lambda b=b, n=n: lambda h: h.tensor_tensor(out=t1[:, 0:n], in0=t1[:, 0:n], in1=rc[b][:, 0:n], op=ALU.mult))(),
                                 reads=[rcB[b]], writes=[t1B])
                            K.op("pool", (lambda b=b, n=n: lambda h: h.tensor_tensor(out=t2[:, 0:n], in0=t2[:, 0:n], in1=rs_[b][:, 0:n], op=ALU.mult))(),
                                 reads=[rcB[b]], writes=[t2B])
                            K.op("pool", (lambda dst=dst, n=n: lambda h: h.tensor_tensor(out=dst, in0=t1[:, 0:n], in1=t2[:, 0:n], op=ALU.add))(),
                                 reads=[t1B, t2B], writes=[dB])
                        else:
                            K.op("dve", (lambda ba=ba, sc0=sc0, dst=dst, n=n: lambda h: h.scalar_tensor_tensor(
                                out=dst, in0=ps[ba][:, 0:n], scalar=sc0, in1=rq[:, 0:n], op0=ALU.mult, op1=ALU.mult))(),
                                reads=[psB[ba], rqB], writes=[dB])
                    for c in range(3):
                        bank = 6 + c % 2
                        proj(8 + c, bank)
                        K.op("act", (lambda c=c, bank=bank, n=n: lambda h: h.activation(out=qb[:, c, 0:n], in_=ps[bank][:, 0:n], func=AF.Copy, scale=0.125))(),
                             reads=[psB[bank]], writes=[qbB])
                    for c in range(3):
                        bank = 6 + (c + 1) % 2
                        proj(11 + c, bank)
                        K.op("act", (lambda c=c, bank=bank, n=n: lambda h: h.activation(out=kb[:, c, 0:n], in_=ps[bank][:, 0:n], func=AF.Copy))(),
                             reads=[psB[bank]], writes=[kbB])
                    for c in range(2):
                        proj(14 + c, 6)
                        proj(18 + c, 7)
                        K.op("act", (lambda n=n: lambda h: h.activation(out=cxs[:, 0:n], in_=ps[6][:, 0:n], func=AF.Copy))(),
                             reads=[psB[6]], writes=[cxB])
                        K.op("dve", (lambda c=c, n=n: lambda h: h.tensor_tensor(out=zz[:, c, 0:n], in0=cxs[:, 0:n], in1=ps[7][:, 0:n], op=ALU.mult))(),
                             reads=[cxB, psB[7]], writes=[zzB])
                        proj(16 + c, 6)
                        K.op("act", (lambda c=c, n=n: lambda h: h.activation(out=cbs[:, c, 0:n], in_=ps[6][:, 0:n], func=AF.Copy))(),
                             reads=[psB[6]], writes=[cbB])
                    for s in range(n // 128):
                        bank = 1 + s % 2
                        for k in range(KD):
                            K.op("pe", (lambda k=k, s=s, bank=bank: lambda h: h.matmul(
                                ps[bank][:, :], lhsT=u[:, k, s * 128:(s + 1) * 128], rhs=win[:, k, 2560:3072],
                                start=(k == 0), stop=(k == KD - 1)))(), reads=[winB[k], uB[k]], writes=[psB[bank]])
                        K.op("dve", (lambda s=s, bank=bank: lambda h: h.tensor_copy(
                            out=vv[:, s, :, 0:64], in_=ps[bank][:, :].rearrange("p (h d) -> p h d", d=64)))(),
                            reads=[psB[bank]], writes=[vvB])
                    tl = [DB("proj", j) for j in range(c0 // 128, (c0 + n) // 128)]
                    K.dma("sp", QA[:, :, c0:c0 + n], qa[:, :, 0:n], reads=[qaB], writes=tl)
                    K.dma("sp", KA[:, c0:c0 + n], ka[:, 0:n], reads=[kaB], writes=tl)
                    K.dma("sp", QB[:, :, c0:c0 + n], qb[:, :, 0:n], reads=[qbB], writes=tl)
                    K.dma("sp", KBd[:, :, c0:c0 + n], kb[:, :, 0:n], reads=[kbB], writes=tl)
                    zc0 = c0 + 1 if X == 0 else c0 + 3
                    K.dma("sp", ZZ[:, :, zc0:zc0 + n], zz[:, :, 0:n], reads=[zzB], writes=tl)
                    K.dma("sp", CBd[:, :, c0:c0 + n], cbs[:, :, 0:n], reads=[cbB], writes=tl)
                    K.dma("sp", VV[c0:c0 + n].rearrange("(s p) h d -> p s h d", p=128), vv[:, 0:n // 128], reads=[vvB], writes=tl)
                K.flush()

        def phase_attn(l, Hin, Hout, hin_name, hout_name, do_ctx):
            TP = 512
            HP = 256
            with ExitStack() as ph:
                kaT = sb("kaT", [128, NT], BF16, ph)
                vaT = sb("vaT", [128, NKT, 2, 65], BF16, ph)
                kbc = sb("kbc", [128, 3, LC], BF16, ph)
                vbc = sb("vbc", [128, 2, 6, 65], BF16, ph)
                woh = sb("woh", [64, 12, D], BF16, ph)
                woc = sb("woc", [128, 2, D], BF16, ph)
                bI = sb("bI", [128, 5, 6, 128], F32, ph)
                bE = sb("bE", [128, 2, 4, 2, 128], F32, ph)
                resB, bEB = K.B("resident"), K.B("bE")
                qa = [sb("qa%d" % i, [128, 3, TP], BF16, ph) for i in range(2)]
                qaB = K.Bs("qaB", 2)
                qb = sb("qb", [128, 3, TP], BF16, ph)
                kbw = sb("kbw", [128, 3, 1024], BF16, ph)
                vbw = sb("vbw", [128, 8, 6, 65], BF16, ph)
                zt = sb("zt", [128, 2, TP + 2], F32, ph)
                cbt = sb("cbt", [128, 2, TP], F32, ph)
                hT = sb("hT", [128, KD, HP], F32, ph)
                ib = K.B("inB")
                hb = K.Bs("hB", KD)
                NPT = 3
                pT = [sb("pT%d" % i, [128, 7 * 128], BF16, ph) for i in range(NPT)]
                pB = K.Bs("pB", NPT)
                pA = [sb("pA%d" % i, [128, 2, TP], BF16, ph) for i in range(2)]
                pAB = K.Bs("pAB", 2)
                sbias2 = [sb("sbias%d" % i, [128, 5, 128], F32, ph) for i in range(2)]
                sbB2 = K.Bs("sbB", 2)
                oT = sb("oT", [64, 12, TP], BF16, ph)
                oB = K.Bs("oB", 12)
                ocT = sb("ocT", [128, 2, TP], BF16, ph)
                ocB = K.B("ocT")
                rec = sb("rec", [128, 2, TP], F32, ph)
                recB = K.B("rec")
                rsB = K.Bs("rsB", 4)
                rsn = [0]
                pcnt = [0]
                pend = []
                bcs = sb("bcs", [64, 2, TP], F32, ph)
                bcB = K.B("bcs")
                ysb = sb("ysb", [128, KD, HP], F32, ph)
                yB = K.Bs("yB", KD)
                sq = sb("sq", [128, KD, HP], BF16, ph)
                sqB = K.Bs("sqB", KD)
                rstd = sb("rstd", [128, HP], F32, ph)
                rB = K.B("rstd")
                tmp = [sb("tmp%d" % i, [128, HP], F32, ph) for i in range(2)]
                tB = K.Bs("tmpB", 2)
                K.dma("sp", kaT[:], KA[:, :], writes=[resB])
                K.dma("sp", vaT[:], VV[:, 0:2, :].rearrange("(s p) h d -> p s h d", p=128), writes=[resB])
                K.dma("sp", kbc[:], KBd[:, :, S:NT], writes=[resB])
                K.dma("sp", vbc[:], VV[S:NT, 2:8, :].rearrange("(s p) h d -> p s h d", p=128), writes=[resB])
                K.dma("pool", woh[:], w_o[l, 0:768, :].rearrange("(h p) n -> p h n", p=64), writes=[resB])
                K.dma("pool", woc[:], w_o[l, 768:1024, :].rearrange("(k p) n -> p k n", p=128), writes=[resB])
                K.dma("sp", bI[:], biasI[l].rearrange("p (a h q) -> p a h q", a=5, h=6), writes=[resB])
                tiles = [(t * TP, TP, 0) for t in range(S // TP)]
                if do_ctx:
                    tiles.append((S, LC, 1))
                ntl = S // TP
                assert ntl >= 2
                bEv = biasE[l].rearrange("p (c a h q) -> p c a h q", c=4, a=4, h=6)

                def kt0_of(t):
                    return max(0, min(4 * t - 2, NQT - 8))

                def load_q(i):
                    c0, n, X = tiles[i]
                    K.dma("sp", qa[i % 2][:, :, 0:n], QA[:, :, c0:c0 + n], writes=[qaB[i % 2]])

                load_q(0)
                pcount = [0]
                for i, (c0, n, X) in enumerate(tiles):
                    b = i % 2
                    t = c0 // TP
                    if i + 1 < len(tiles):
                        load_q(i + 1)
                    K.dma("sp", qb[:, :, 0:n], QB[:, :, c0:c0 + n], writes=[ib])
                    zc0 = c0 if X == 0 else c0 + 2
                    K.dma("sp", zt[:, :, 0:n + 2], ZZ[:, :, zc0:zc0 + n + 2], writes=[ib])
                    K.dma("sp", cbt[:, :, 0:n], CBd[:, :, c0:c0 + n], writes=[ib])
                    if X == 0:
                        k0 = kt0_of(t)
                        K.dma("sp", kbw[:], KBd[:, :, k0 * 128:k0 * 128 + 1024], writes=[ib])
                        K.dma("sp", vbw[:], VV[k0 * 128:k0 * 128 + 1024, 2:8, :].rearrange("(s p) h d -> p s h d", p=128), writes=[ib])
                    qab = qaB[b]

                    def normalize2(ob, h0, h1, n=n):
                        o2 = psall[:, ob * 512:(ob + 2) * 512].rearrange("p (b q) -> p b q", b=2)
                        slot = rsn[0] % 4
                        rsn[0] += 1
                        K.op("dve", (lambda n=n, o2=o2: lambda h: h.reciprocal(out=rec[64:65, :, 0:n], in_=o2[64:65, :, 0:n]))(),
                             reads=[psB[ob], psB[ob + 1]], writes=[recB])
                        K.dma("sp", RS[slot, :, :, 0:n], rec[64:65, :, 0:n], reads=[recB], writes=[rsB[slot]])
                        K.dma("sp", bcs[:, :, 0:n], RS[slot, :, :, 0:n].partition_broadcast(64), reads=[rsB[slot]], writes=[bcB])

                        def part2(ob=ob, h0=h0, h1=h1, n=n):
                            for bb, hidx in ((0, h0), (1, h1)):
                                K.op("dve", (lambda bb=bb, hidx=hidx, n=n, ob=ob: lambda h: h.tensor_tensor(
                                    out=oT[:, hidx, 0:n], in0=ps[ob + bb][0:64, 0:n], in1=bcs[:, bb, 0:n], op=ALU.mult))(),
                                    reads=[psB[ob + bb], bcB], writes=[oB[hidx]])
                        pend.append(part2)

                    def flush_pend():
                        while pend:
                            pend.pop(0)()

                    kts = list(range(NKT)) if X == 0 else list(range(NQT, NKT))
                    for c in range(3):
                        def qk(ki, c=c, n=n, b=b):
                            kt = kts[ki]
                            for half in range(2):
                                sbank = 2 * (ki % 2) + half
                                lo = 64 * half
                                K.op("pe", (lambda kt=kt, c=c, lo=lo, sbank=sbank, b=b, n=n: lambda h: h.matmul(
                                    ps[sbank][:, 0:n], lhsT=kaT[lo:lo + 64, kt * 128:(kt + 1) * 128], rhs=qa[b][lo:lo + 64, c, 0:n],
                                    start=True, stop=True))(), reads=[resB, qab], writes=[psB[sbank]])
                        ob = 4 + 2 * (pcnt[0] % 2)
                        pcnt[0] += 1
                        qk(0)
                        for ki, kt in enumerate(kts):
                            if ki + 1 < len(kts):
                                qk(ki + 1)
                            pp = ki % 2
                            K.op("act", (lambda pp=pp, n=n: lambda h: h.activation(
                                out=pA[pp][:, :, 0:n], in_=psall[:, pp * 1024:(pp + 1) * 1024].rearrange("p (h q) -> p h q", h=2)[:, :, 0:n],
                                func=AF.Exp))(), reads=[psB[2 * pp], psB[2 * pp + 1]], writes=[pAB[pp]])
                            for half in range(2):
                                K.op("pe", (lambda kt=kt, half=half, pp=pp, ki=ki, n=n, ob=ob: lambda h: h.matmul(
                                    ps[ob + half][0:65, 0:n], lhsT=vaT[:, kt, half, :], rhs=pA[pp][:, half, 0:n],
                                    start=(ki == 0), stop=(ki == len(kts) - 1)))(), reads=[resB, pAB[pp]], writes=[psB[ob + half]])
                            if ki == min(8, len(kts) - 1):
                                flush_pend()
                        normalize2(ob, c, c + 3)
                    for c in range(3):
                        if X == 0 and (t == 0 or t == ntl - 1):
                            cls0 = 0 if t == 0 else 2
                            K.dma("sp", bE[:], bEv[:, cls0:cls0 + 2, :, 2 * c:2 * c + 2, :], writes=[bEB])
                        ob = 4 + 2 * (pcnt[0] % 2)
                        pcnt[0] += 1
                        for qs in range(n // 128):
                            q0 = qs * 128
                            if qs == 1:
                                flush_pend()
                            if X == 0:
                                qt = c0 // 128 + qs
                                if qt < 2 or qt >= NQT - 2:
                                    wk = list(range(0, 4)) if qt < 2 else list(range(NQT - 4, NQT))
                                    cls = qt if qt < 2 else qt - (NQT - 2)
                                    bsrc = lambda a0, a1, half, cls=cls: bE[:, cls, a0:a1, half, :]
                                    brd = bEB
                                else:
                                    wk = list(range(qt - 2, qt + 3))
                                    bsrc = lambda a0, a1, half, c=c: bI[:, a0:a1, 2 * c + half, :]
                                    brd = resB
                            else:
                                wk = []
                            nw = len(wk)
                            for half in range(2):
                                lo = 64 * half
                                for a, kt in enumerate(wk):
                                    bank = 2 * half + (0 if a < 4 else 1)
                                    col = (a % 4) * 128
                                    sl = (kt - k0) * 128
                                    K.op("pe", (lambda bank=bank, col=col, sl=sl, lo=lo, c=c, q0=q0: lambda h: h.matmul(
                                        ps[bank][:, col:col + 128], lhsT=kbw[lo:lo + 64, c, sl:sl + 128],
                                        rhs=qb[lo:lo + 64, c, q0:q0 + 128], start=True, stop=True))(),
                                        reads=[ib], writes=[psB[bank]])
                                for a in range(2):
                                    bank = 2 * half + 1
                                    col = 128 + a * 128
                                    K.op("pe", (lambda bank=bank, col=col, a=a, lo=lo, c=c, q0=q0: lambda h: h.matmul(
                                        ps[bank][:, col:col + 128], lhsT=kbc[lo:lo + 64, c, a * 128:(a + 1) * 128],
                                        rhs=qb[lo:lo + 64, c, q0:q0 + 128], start=True, stop=True))(),
                                        reads=[ib, resB], writes=[psB[bank]])
                            for half in range(2):
                                hh = 2 * c + half
                                pi = pcount[0] % NPT
                                pcount[0] += 1
                                sbias, sbB = sbias2[half], sbB2[half]
                                if nw:
                                    na = min(nw, 4)
                                    K.op("dve", (lambda half=half, na=na, bsrc=bsrc, sbias=sbias: lambda h: h.tensor_tensor(
                                        out=sbias[:, 0:na, :], in0=ps[2 * half][:, 0:na * 128].rearrange("p (a q) -> p a q", q=128),
                                        in1=bsrc(0, na, half), op=ALU.add))(),
                                        reads=[psB[2 * half], brd], writes=[sbB])
                                    if nw > 4:
                                        K.op("dve", (lambda half=half, bsrc=bsrc, sbias=sbias: lambda h: h.tensor_tensor(
                                            out=sbias[:, 4:5, :], in0=ps[2 * half + 1][:, 0:128].rearrange("p (a q) -> p a q", q=128),
                                            in1=bsrc(4, 5, half), op=ALU.add))(),
                                            reads=[psB[2 * half + 1], brd], writes=[sbB], nosame=True)
                                    K.op("act", (lambda pi=pi, nw=nw, sbias=sbias: lambda h: h.activation(
                                        out=pT[pi][:, 0:nw * 128].rearrange("p (a q) -> p a q", q=128), in_=sbias[:, 0:nw, :], func=AF.Exp))(),
                                        reads=[sbB], writes=[pB[pi]])
                                bank = 2 * half + 1
                                K.op("act", (lambda pi=pi, bank=bank, nw=nw: lambda h: h.activation(
                                    out=pT[pi][:, nw * 128:(nw + 2) * 128], in_=ps[bank][:, 128:384], func=AF.Exp))(),
                                    reads=[psB[bank]], writes=[pB[pi]], nosame=(nw > 0))
                                nk = nw + 2
                                for a in range(nk):
                                    if a < nw:
                                        lhs = (lambda hh=hh, sl=wk[a] - k0: vbw[:, sl, hh, :])
                                    else:
                                        lhs = (lambda a=a, hh=hh, nw=nw: vbc[:, a - nw, hh, :])
                                    K.op("pe", (lambda a=a, lhs=lhs, half=half, pi=pi, q0=q0, nk=nk, ob=ob: lambda h: h.matmul(
                                        ps[ob + half][0:65, q0:q0 + 128], lhsT=lhs(), rhs=pT[pi][:, a * 128:(a + 1) * 128],
                                        start=(a == 0), stop=(a == nk - 1)))(), reads=[ib, resB, pB[pi]], writes=[psB[ob + half]])
                        flush_pend()
                        normalize2(ob, 6 + 2 * c, 6 + 2 * c + 1)
                    flush_pend()
                    cacc0 = ysb[:, 0:2, :].rearrange("p a q -> p (a q)")
                    cacc1 = ysb[:, 2:4, :].rearrange("p a q -> p (a q)")
                    ca0B, ca1B = [yB[0], yB[1]], [yB[2], yB[3]]
                    for c in range(2):
                        w0 = smp[:, l, 124 + c * 3:125 + c * 3]
                        w1 = smp[:, l, 125 + c * 3:126 + c * 3]
                        w2 = smp[:, l, 126 + c * 3:127 + c * 3]
                        K.op("pool", (lambda c=c, w0=w0, n=n: lambda h: h.tensor_scalar(
                            out=cacc0[:, 0:n], in0=zt[:, c, 0:n], scalar1=w0, scalar2=None, op0=ALU.mult))(),
                            reads=[ib, cB], writes=ca0B)
                        for (wt_, off) in ((w1, 1), (w2, 2)):
                            K.op("pool", (lambda c=c, wt_=wt_, off=off, n=n: lambda h: h.tensor_scalar(
                                out=cacc1[:, 0:n], in0=zt[:, c, off:n + off], scalar1=wt_, scalar2=None, op0=ALU.mult))(),
                                reads=[ib, cB], writes=ca1B)
                            K.op("pool", (lambda n=n: lambda h: h.tensor_tensor(
                                out=cacc0[:, 0:n], in0=cacc0[:, 0:n], in1=cacc1[:, 0:n], op=ALU.add))(),
                                reads=ca1B, writes=ca0B)
                        K.op("pool", (lambda c=c, n=n: lambda h: h.tensor_tensor(
                            out=ocT[:, c, 0:n], in0=cacc0[:, 0:n], in1=cbt[:, c, 0:n], op=ALU.mult))(),
                            reads=[ib] + ca0B, writes=[ocB])
                    for hf in range(n // HP):
                        o0 = hf * HP
                        K.dma("sp", hT[:], Hin[:, c0 + o0:c0 + o0 + HP].rearrange("(k p) t -> p k t", p=128), writes=hb)
                        for j in range(KD):
                            bank = j % 2
                            for hh in range(12):
                                K.op("pe", (lambda j=j, hh=hh, bank=bank, o0=o0: lambda h: h.matmul(
                                    ps[bank][:, 0:HP], lhsT=woh[:, hh, j * 128:(j + 1) * 128], rhs=oT[:, hh, o0:o0 + HP],
                                    start=(hh == 0), stop=False))(), reads=[resB, oB[hh]], writes=[psB[bank]])
                            for k in range(2):
                                K.op("pe", (lambda j=j, k=k, bank=bank, o0=o0: lambda h: h.matmul(
                                    ps[bank][:, 0:HP], lhsT=woc[:, k, j * 128:(j + 1) * 128], rhs=ocT[:, k, o0:o0 + HP],
                                    start=False, stop=(k == 1)))(), reads=[resB, ocB], writes=[psB[bank]])
                            K.op("act", (lambda j=j, bank=bank: lambda h: h.activation(out=ysb[:, j, :], in_=ps[bank][:, 0:HP], func=AF.Copy))(),
                                 reads=[psB[bank]], writes=[yB[j]])
                            K.op("act", (lambda j=j, bank=bank: lambda h: h.activation(out=sq[:, j, :], in_=ps[bank][:, 0:HP], func=AF.Square))(),
                                 reads=[psB[bank]], writes=[sqB[j]])
                        for k in range(KD):
                            K.op("pe", (lambda k=k: lambda h: h.matmul(ps[2][:, 0:HP], lhsT=onesm[:], rhs=sq[:, k, :],
                                                                       start=(k == 0), stop=(k == KD - 1)))(),
                                 reads=[sqB[k], cB], writes=[psB[2]])
                        K.op("act", lambda h: h.activation(out=rstd[:], in_=ps[2][:, 0:HP], func=AF.Sqrt, bias=EPS, scale=1.0),
                             reads=[psB[2]], writes=[rB])
                        K.op("dve", lambda h: h.reciprocal(out=rstd[:], in_=rstd[:]), reads=[rB], writes=[rB])
                        for j in range(KD):
                            tb = j % 2
                            K.op("dve", (lambda j=j, tb=tb, X=X: lambda h: h.scalar_tensor_tensor(
                                out=tmp[tb][:], in0=ysb[:, j, :], scalar=par[:, l, X, 5, j:j + 1], in1=rstd[:],
                                op0=ALU.mult, op1=ALU.mult))(), reads=[yB[j], rB], writes=[tB[tb]])
                            K.op("pool", (lambda j=j, tb=tb: lambda h: h.tensor_tensor(
                                out=hT[:, j, :], in0=hT[:, j, :], in1=tmp[tb][:], op=ALU.add))(),
                                reads=[tB[tb]], writes=[hb[j]])
                        K.dma("sp", Hout[:, c0 + o0:c0 + o0 + HP].rearrange("(k p) t -> p k t", p=128), hT[:], reads=hb)
                K.flush()

        phase_ffn(0, 0, xT, HA, "x", "HA", True)
        phase_proj(0, HA, "HA")
        phase_attn(0, HA, HB, "HA", "HB", True)
        phase_ffn(0, 1, HB, HA, "HB", "HA", True)
        phase_ffn(1, 0, HA, HB, "HA", "HB", True)
        phase_proj(1, HB, "HB")
        phase_attn(1, HB, HA, "HB", "HA", False)
        phase_ffn(1, 1, HA, None, "HA", "none", False, final_out=outT)
    return nc


def _rope_tables(S):
    pos = np.arange(S, dtype=np.int32)
    row = (pos // GW).astype(np.float32)
    col = (pos % GW).astype(np.float32)
    freq = (1.0 / (np.float32(10000.0) ** (np.arange(16, dtype=np.float32) / np.float32(16)))).astype(np.float32)
    C = np.zeros((128, S), np.float32)
    Sg = np.zeros((128, S), np.float32)
    for p in range(128):
        d = p % 64
        a, j, f = d // 32, (d // 16) % 2, d % 16
        ang = ((row if a == 0 else col) * freq[f]).astype(np.float32)
        C[p] = np.cos(ang)
        Sg[p] = np.sin(ang) * (-1.0 if j == 0 else 1.0)
    return C, Sg


def _bias_mats(rpb_l, qt, kts, rows):
    out = np.full((128, len(kts), 6, 128), NEG, np.float32)
    j = np.arange(128)
    for a, kt in enumerate(kts):
        kr = 2 * kt + j // 64
        kc = j % 64
        qr = 2 * qt + j // 64
        qc = j % 64
        rs = np.clip(qr - 4, 0, rows - 8)
        cs = np.clip(qc - 8, 0, GW - 16)
        okr = (kr[:, None] >= rs[None, :]) & (kr[:, None] < rs[None, :] + 8)
        okc = (kc[:, None] >= cs[None, :]) & (kc[:, None] < cs[None, :] + 16)
        ok = okr & okc
        dr = np.clip(kr[:, None] - qr[None, :] + 7, 0, 14)
        dc = np.clip(kc[:, None] - qc[None, :] + 15, 0, 30)
        for hh in range(6):
            vals = rpb_l[hh][dr, dc]
            out[:, a, hh, :] = np.where(ok, vals, np.float32(NEG))
    return out


def _prep_inputs(inp, S):
    x, c, ctx, c_ctx = inp["x"], inp["c"], inp["ctx"], inp["c_ctx"]
    B = x.shape[0]
    rows = S // GW
    NQT = S // 128
    w_in = inp["w_in"]
    perm = np.arange(64) ^ 16
    cols = []
    qa_h = lambda h: np.arange(h * 64, (h + 1) * 64)
    for cc in range(3):
        cols += [qa_h(cc), qa_h(cc + 3)]
    for cc in range(3):
        cols += [qa_h(cc)[perm], qa_h(cc + 3)[perm]]
    ka0 = 768
    cols += [np.arange(ka0, ka0 + 64), np.arange(ka0 + 64, ka0 + 128)]
    cols += [np.arange(ka0, ka0 + 64)[perm], np.arange(ka0 + 64, ka0 + 128)[perm]]
    cols += [np.arange(384, 768)]
    cols += [np.arange(768 + 256, 768 + 256 + 384)]
    cx0 = 768 + 1024
    cols += [np.arange(cx0, cx0 + 768)]
    cols += [np.arange(768 + 128, 768 + 256)]
    cols += [np.arange(768 + 256 + 384, 768 + 1024)]
    cols = np.concatenate(cols)
    assert cols.shape[0] == NW
    w_inx = np.ascontiguousarray(w_in[:, :, cols])
    smallp = np.zeros((2, 128, NS), np.float32)
    p = np.arange(128)
    for l in range(2):
        smallp[l, :, 0:72] = inp["b_ada"][l].reshape(72, 128).T
        smallp[l, :, 72:120] = inp["norm_g"][l].reshape(6, 8, 128).transpose(2, 0, 1).reshape(128, 48)
        smallp[l, :, 120] = inp["qk_g"][l, 0][p % 64]
        smallp[l, :, 121] = inp["qk_g"][l, 0][(p % 64) ^ 16]
        smallp[l, :, 122] = inp["qk_g"][l, 1][p % 64]
        smallp[l, :, 123] = inp["qk_g"][l, 1][(p % 64) ^ 16]
        smallp[l, :, 124:130] = inp["conv_w"][l].reshape(3, 2, 128).transpose(2, 1, 0).reshape(128, 6)
    C, Sg = _rope_tables(S)
    biasI = np.zeros((2, 128, 5 * 6 * 128), np.float32)
    biasE = np.zeros((2, 128, 4 * 4 * 6 * 128), np.float32)
    for l in range(2):
        biasI[l] = _bias_mats(inp["rpb"][l], 2, [0, 1, 2, 3, 4], rows).reshape(128, -1)
        be = [_bias_mats(inp["rpb"][l], qt, kts, rows) for qt, kts in
              ((0, [0, 1, 2, 3]), (1, [0, 1, 2, 3]), (NQT - 2, list(range(NQT - 4, NQT))), (NQT - 1, list(range(NQT - 4, NQT))))]
        biasE[l] = np.stack(be, axis=1).reshape(128, -1)
    shared = dict(smallp=smallp, w_ada=np.ascontiguousarray(inp["w_ada"]), w_inx=w_inx, w_o=np.ascontiguousarray(inp["w_o"]),
                  ffn_wi=np.ascontiguousarray(inp["ffn_wi"]), ffn_wo=np.ascontiguousarray(inp["ffn_wo"]),
                  ropeC=C, ropeS=Sg, biasI=biasI, biasE=biasE)
    maps = []
    for b in range(B):
        m = dict(shared)
        m["xT"] = np.ascontiguousarray(np.concatenate([x[b].T, ctx[b].T], axis=1))
        cv = np.stack([c[b].reshape(8, 128).T, c_ctx.reshape(8, 128).T], axis=-1)
        m["cvec"] = np.ascontiguousarray(cv.astype(np.float32))
        maps.append(m)
    return maps


_CACHE = {}


def run(inp, debug=False):
    inp = {k: np.asarray(v, dtype=np.float32) for k, v in inp.items()}
    S = inp["x"].shape[1]
    B = inp["x"].shape[0]
    key = (S, debug)
    if key not in _CACHE:
        _CACHE[key] = build_program(S, debug)
    nc = _CACHE[key]
    maps = _prep_inputs(inp, S)
    res = run_bass_kernel_spmd(nc, maps, core_ids=list(range(B)))
    out = np.stack([np.ascontiguousarray(r["outT"].T) for r in res.results], axis=0)
    return out.astype(np.float32), res


def kernel(**inputs):
    out, _ = run(inputs)
    return out
```

```python
import numpy as np
from contextlib import ExitStack
import concourse.bass as bass
import concourse.mybir as mybir
from concourse.bass_utils import run_bass_kernel_spmd

F32 = mybir.dt.float32
BF16 = mybir.dt.bfloat16
AF = mybir.ActivationFunctionType
ALU = mybir.AluOpType

D = 1024
KD = 8
FFN = 2816
KF = 22
LC = 256
GW = 64
NEG = -30000.0
EPS = 1e-6
NS = 132
NW = 3072
TF = 256
NDMA = 24
SAME_ENG_SYNC = False


class Buf:
    __slots__ = ("name", "w", "rc", "rd")

    def __init__(self, name):
        self.name = name
        self.w = []
        self.rc = {}
        self.rd = []


class Eng:
    def __init__(self, name, sem):
        self.name = name
        self.sem = sem
        self.base = 0
        self.reset()

    def reset(self):
        self.items = []
        self.marks = []
        self.waited = {}
        self.vals = []


class KB:
    HM = {"pe": "tensor", "act": "scalar", "dve": "vector", "pool": "gpsimd", "sp": "sync"}

    def __init__(self, nc, es):
        self.nc = nc
        self.engs = {}
        for n in self.HM:
            self.engs[n] = Eng(n, es.enter_context(nc.semaphore("s_" + n)))
        self.dsem = [es.enter_context(nc.semaphore("d%d" % i)) for i in range(NDMA)]
        self.dcnt = [0] * NDMA
        self.drr = 0
        self.bufs = []

    def B(self, name):
        b = Buf(name)
        self.bufs.append(b)
        return b

    def Bs(self, name, n):
        return [self.B("%s%d" % (name, i)) for i in range(n)]

    def _wait(self, eng, tok, nosame=False):
        if tok[0] == "c":
            pe, seq = tok[1], tok[2]
            if pe is eng and (nosame or eng.name == "pe" or eng.name == "sp" or not SAME_ENG_SYNC):
                return
            if eng.waited.get(pe.name, -1) >= seq:
                return
            eng.waited[pe.name] = seq
            pe.marks[seq] = True
            eng.items.append(("wc", pe, seq))
        else:
            idx, val = tok[1], tok[2]
            if eng.waited.get(("d", idx), 0) >= val:
                return
            eng.waited[("d", idx)] = val
            eng.items.append(("wd", idx, val))

    def _deps(self, eng, reads, writes, nosame=False):
        for b in reads:
            for t in b.w:
                self._wait(eng, t)
        for b in writes:
            for t in b.w:
                self._wait(eng, t, nosame)
            for n, s in b.rc.items():
                self._wait(eng, ("c", self.engs[n], s))
            for t in b.rd:
                self._wait(eng, t)

    def op(self, en, fn, reads=(), writes=(), nosame=False):
        eng = self.engs[en]
        self._deps(eng, reads, writes, nosame)
        seq = len(eng.marks)
        eng.marks.append(False)
        eng.items.append(("op", fn, seq))
        tok = ("c", eng, seq)
        for b in writes:
            b.w = [tok]
            b.rc = {}
            b.rd = []
        for b in reads:
            b.rc[en] = seq

    def dma(self, en, out, in_, reads=(), writes=()):
        eng = self.engs[en]
        self._deps(eng, reads, writes)
        idx = self.drr
        self.drr = (self.drr + 1) % NDMA
        self.dcnt[idx] += 16
        tok = ("d", idx, self.dcnt[idx])
        eng.items.append(("dma", out, in_, idx))
        for b in writes:
            b.w = [tok]
            b.rc = {}
            b.rd = []
        for b in reads:
            b.rd.append(tok)

    def flush(self):
        engs = self.engs
        for e in engs.values():
            if e.marks:
                e.marks[-1] = True
        for f in engs.values():
            for e in engs.values():
                if e is not f and e.marks:
                    f.items.append(("wc", e, len(e.marks) - 1))
            for i in range(NDMA):
                if self.dcnt[i] > 0:
                    f.items.append(("wd", i, self.dcnt[i]))
        for e in engs.values():
            v = e.base
            vals = []
            for m in e.marks:
                if m:
                    v += 1
                vals.append(v)
            e.vals = vals
            e.newbase = v
        dsem = self.dsem
        with self.nc.Block() as block:
            for name, e in engs.items():
                def mk(e):
                    def body(h):
                        for it in e.items:
                            k = it[0]
                            if k == "op":
                                ins = it[1](h)
                                if e.marks[it[2]]:
                                    ins.then_inc(e.sem, 1)
                            elif k == "wc":
                                h.wait_ge(it[1].sem, it[1].vals[it[2]])
                            elif k == "wd":
                                h.wait_ge(dsem[it[1]], it[2])
                            else:
                                h.dma_start(out=it[1], in_=it[2], allow_slow_non_contiguous=True).then_inc(dsem[it[3]], 16)
                    return body
                getattr(block, self.HM[name])(mk(e))
        for e in engs.values():
            e.base = e.newbase
            e.reset()
        for b in self.bufs:
            b.w = []
            b.rc = {}
            b.rd = []


def build_program(S, debug=False):
    NT = S + LC
    NQT = S // 128
    NKT = NT // 128
    nc = bass.Bass("TRN2", target_bir_lowering=False)
    dk = "ExternalOutput" if debug else "Internal"

    def din(name, shape, dt=F32):
        return nc.dram_tensor(name, list(shape), dt, kind="ExternalInput").ap()

    def dscr(name, shape, dt=F32):
        return nc.dram_tensor(name, list(shape), dt, kind=dk).ap()

    xT = din("xT", [D, NT])
    cvec = din("cvec", [128, KD, 2])
    smallp = din("smallp", [2, 128, NS])
    w_ada = din("w_ada", [2, D, 9 * D])
    w_inx = din("w_inx", [2, D, NW])
    w_o = din("w_o", [2, D, D])
    ffn_wi = din("ffn_wi", [2, 2, D, 2 * FFN])
    ffn_wo = din("ffn_wo", [2, 2, FFN, D])
    ropeC = din("ropeC", [128, S])
    ropeS = din("ropeS", [128, S])
    biasI = din("biasI", [2, 128, 5 * 6 * 128])
    biasE = din("biasE", [2, 128, 4 * 4 * 6 * 128])
    outT = nc.dram_tensor("outT", [D, S], F32, kind="ExternalOutput").ap()

    HA = dscr("HA", [D, NT])
    HB = dscr("HB", [D, NT])
    QA = dscr("QA", [128, 3, NT], BF16)
    KA = dscr("KA", [128, NT], BF16)
    QB = dscr("QB", [128, 3, NT], BF16)
    KBd = dscr("KBd", [128, 3, NT], BF16)
    VV = dscr("VV", [NT, 8, 65], BF16)
    ZZ = dscr("ZZ", [128, 2, NT + 4])
    CBd = dscr("CBd", [128, 2, NT])
    RS = dscr("RS", [4, 1, 2, 512])

    es = ExitStack()
    with es:
        K = KB(nc, es)

        uid = [0]

        def sb(name, shape, dt=F32, stack=es):
            uid[0] += 1
            return stack.enter_context(nc.sbuf_tensor("%s_%d" % (name, uid[0]), list(shape), dt))

        psall = es.enter_context(nc.psum_tensor("psall", [128, 4096], F32))
        ps = [psall[:, i * 512:(i + 1) * 512] for i in range(8)]
        psB = K.Bs("psB", 8)
        par = sb("par", [128, 2, 2, 9, KD])
        smp = sb("smp", [128, 2, NS])
        qsc = sb("qsc", [128, 2, 2])
        onesm = sb("onesm", [128, 128], BF16)
        bd64 = sb("bd64", [128, 128], BF16)
        onesb = sb("onesb", [128, 64], BF16)
        cB = K.B("consts")
        dramB = {}

        def DB(name, i):
            key = (name, i)
            if key not in dramB:
                dramB[key] = K.B("dram_%s_%d" % (name, i))
            return dramB[key]

        with ExitStack() as ph:
            cv = sb("cv", [128, KD, 2], F32, ph)
            scv = sb("scv", [128, KD, 2], F32, ph)
            wa = [sb("wa%d" % i, [128, KD, 1024], F32, ph) for i in range(2)]
            waB = K.Bs("waB", 2)
            mods = sb("mods", [128, 2, 72], F32, ph)
            cvB, scB, modB = K.B("cv"), K.B("scv"), K.B("mods")
            K.op("dve", lambda h: h.memset(onesm[:], 1.0 / 1024), writes=[cB])
            K.op("dve", lambda h: h.memset(bd64[:], 0.0), writes=[cB])
            K.op("dve", lambda h: h.memset(bd64[0:64, 0:64], 1.0 / 64), writes=[cB])
            K.op("dve", lambda h: h.memset(bd64[64:128, 64:128], 1.0 / 64), writes=[cB])
            K.op("dve", lambda h: h.memset(onesb[:], 1.0), writes=[cB])
            K.dma("sp", cv[:], cvec[:, :, :], writes=[cvB])
            K.dma("sp", smp[:], smallp.rearrange("l p n -> p l n"), writes=[cB])
            K.op("act", lambda h: h.activation(out=scv[:], in_=cv[:], func=AF.Silu), reads=[cvB], writes=[scB])
            it = 0
            for l in range(2):
                for i in range(9):
                    b = it % 2
                    it += 1
                    K.dma("sp", wa[b][:], w_ada[l].rearrange("(k p) n -> p k n", p=128)[:, :, i * 1024:(i + 1) * 1024],
                          writes=[waB[b]])
                    for j in range(8):
                        idx = i * 8 + j
                        for k in range(KD):
                            K.op("pe", (lambda b=b, j=j, k=k, idx=idx: lambda h: h.matmul(
                                ps[0][:, 2 * idx:2 * idx + 2], lhsT=wa[b][:, k, j * 128:(j + 1) * 128], rhs=scv[:, k, :],
                                start=(k == 0), stop=(k == KD - 1)))(), reads=[waB[b], scB], writes=[psB[0]])
                pv = ps[0][:, 0:144].rearrange("p (n t) -> p n t", t=2)
                for X in range(2):
                    K.op("dve", (lambda X=X, l=l, pv=pv: lambda h: h.tensor_tensor(
                        out=mods[:, X, :], in0=pv[:, :, X], in1=smp[:, l, 0:72], op=ALU.add))(),
                        reads=[psB[0], cB], writes=[modB])

                    def m(i, X=X):
                        return mods[:, X, i * 8:(i + 1) * 8]

                    def g(gi, l=l):
                        return smp[:, l, 72 + gi * 8:72 + (gi + 1) * 8]
                    for (kind, mi, gi, wres) in ((0, 1, 0, None), (3, 4, 2, None), (6, 7, 4, None)):
                        K.op("dve", (lambda kind=kind, mi=mi, gi=gi, X=X, l=l, m=m, g=g: lambda h: h.scalar_tensor_tensor(
                            out=par[:, l, X, kind, :], in0=m(mi), scalar=1.0, in1=g(gi), op0=ALU.add, op1=ALU.mult))(),
                            reads=[modB, cB], writes=[cB])
                    for (kind, mi) in ((1, 0), (4, 3), (7, 6)):
                        K.op("dve", (lambda kind=kind, mi=mi, X=X, l=l, m=m: lambda h: h.tensor_copy(
                            out=par[:, l, X, kind, :], in_=m(mi)))(), reads=[modB], writes=[cB])
                    for (kind, mi, gi, wres) in ((2, 2, 1, 0.5), (5, 5, 3, 1.0), (8, 8, 5, 0.5)):
                        K.op("dve", (lambda kind=kind, mi=mi, gi=gi, wres=wres, X=X, l=l, m=m, g=g: lambda h: h.scalar_tensor_tensor(
                            out=par[:, l, X, kind, :], in0=m(mi), scalar=wres, in1=g(gi), op0=ALU.mult, op1=ALU.mult))(),
                            reads=[modB, cB], writes=[cB])
                K.op("dve", (lambda l=l: lambda h: h.tensor_scalar(
                    out=qsc[:, l, :], in0=smp[:, l, 120:122], scalar1=0.125, scalar2=None, op0=ALU.mult))(),
                    reads=[cB], writes=[cB])
            K.flush()

        def phase_ffn(l, which, Hin, Hout, hin_name, hout_name, do_ctx, final_out=None):
            kA, kB_, kG = (0, 1, 2) if which == 0 else (6, 7, 8)
            with ExitStack() as ph:
                wi = sb("wi", [128, KD, 2 * FFN], BF16, ph)
                wo = sb("wo", [128, KF, D], BF16, ph)
                wiB = K.Bs("wiB", KD)
                woB = K.Bs("woB", 2)
                hT = [sb("hT%d" % i, [128, KD, TF], F32, ph) for i in range(2)]
                hB = [K.Bs("hB%d_" % i, KD) for i in range(2)]
                u2 = [sb("u%d" % i, [128, KD, TF], BF16, ph) for i in range(2)]
                u2B = [K.Bs("uB%d_" % i, KD) for i in range(2)]
                rstd2 = sb("rstd2", [128, TF], F32, ph)
                r2B = K.B("rstd2")
                act = sb("actb", [128, KF, TF], BF16, ph)
                actB = K.Bs("actB", KF)
                ysb = sb("ysb", [128, KD, TF], F32, ph)
                yB = K.Bs("yB", KD)
                rstd = sb("rstd", [128, TF], F32, ph)
                rB = K.B("rstd")
                tmp = [sb("tmp%d" % i, [128, TF], F32, ph) for i in range(2)]
                tB = K.Bs("tmpB", 2)
                sg = [sb("sg%d" % i, [128, TF], F32, ph) for i in range(2)]
                sgB = K.Bs("sgB", 2)
                for k in range(KD):
                    K.dma("pool", wi[:, k, :], ffn_wi[l, which, k * 128:(k + 1) * 128, :], writes=[wiB[k]])
                wov = ffn_wo[l, which].rearrange("(m p) n -> p m n", p=128)
                K.dma("pool", wo[:, 0:11, :], wov[:, 0:11, :], writes=[woB[0]])
                K.dma("pool", wo[:, 11:22, :], wov[:, 11:22, :], writes=[woB[1]])
                tiles = [(t * TF, 0) for t in range(S // TF)]
                if do_ctx:
                    tiles.append((S, 1))

                def load(i):
                    c0, X = tiles[i]
                    b = i % 2
                    K.dma("sp", hT[b][:], Hin[:, c0:c0 + TF].rearrange("(k p) t -> p k t", p=128),
                          reads=[DB(hin_name, c0 // TF)], writes=hB[b])

                def stats(ub, ubB, rs, rsB):
                    for k in range(KD):
                        K.op("pe", (lambda k=k, ub=ub: lambda h: h.matmul(ps[0][:, 0:TF], lhsT=onesm[:], rhs=ub[:, k, :],
                                                                          start=(k == 0), stop=(k == KD - 1)))(),
                             reads=[ubB[k], cB], writes=[psB[0]])
                    K.op("act", (lambda rs=rs: lambda h: h.activation(out=rs[:], in_=ps[0][:, 0:TF], func=AF.Sqrt, bias=EPS, scale=1.0))(),
                         reads=[psB[0]], writes=[rsB])
                    K.op("dve", (lambda rs=rs: lambda h: h.reciprocal(out=rs[:], in_=rs[:]))(), reads=[rsB], writes=[rsB])

                def pre_sq(i, k):
                    b = i % 2
                    K.op("act", (lambda k=k, b=b: lambda h: h.activation(out=u2[b][:, k, :], in_=hT[b][:, k, :], func=AF.Square))(),
                         reads=[hB[b][k]], writes=[u2B[b][k]])

                def pre_u(i, k):
                    b = i % 2
                    X = tiles[i][1]
                    tb = k % 2
                    K.op("dve", (lambda k=k, tb=tb, b=b, X=X: lambda h: h.scalar_tensor_tensor(
                        out=tmp[tb][:], in0=hT[b][:, k, :], scalar=par[:, l, X, kA, k:k + 1], in1=rstd[:],
                        op0=ALU.mult, op1=ALU.mult))(), reads=[hB[b][k], rB], writes=[tB[tb]])
                    K.op("act", (lambda k=k, tb=tb, b=b, X=X: lambda h: h.activation(
                        out=u2[b][:, k, :], in_=tmp[tb][:], func=AF.Identity, bias=par[:, l, X, kB_, k:k + 1], scale=1.0))(),
                        reads=[tB[tb]], writes=[u2B[b][k]])

                load(0)
                for k in range(KD):
                    pre_sq(0, k)
                stats(u2[0], u2B[0], rstd, rB)
                for k in range(KD):
                    pre_u(0, k)
                for i, (c0, X) in enumerate(tiles):
                    b = i % 2
                    nxt = i + 1 < len(tiles)
                    if nxt:
                        load(i + 1)
                    h_, hb = hT[b], hB[b]
                    u, uB = u2[b], u2B[b]
                    for m in range(KF):
                        pb = m % 2
                        for half, bank in ((0, 1 + pb), (1, 3 + pb)):
                            for k in range(KD):
                                c_ = half * FFN + m * 128
                                K.op("pe", (lambda k=k, c_=c_, bank=bank, u=u: lambda h: h.matmul(
                                    ps[bank][:, 0:TF], lhsT=wi[:, k, c_:c_ + 128], rhs=u[:, k, :],
                                    start=(k == 0), stop=(k == KD - 1)))(), reads=[wiB[k], uB[k]], writes=[psB[bank]])
                        K.op("act", (lambda pb=pb: lambda h: h.activation(out=sg[pb][:], in_=ps[1 + pb][:, 0:TF], func=AF.Silu))(),
                             reads=[psB[1 + pb]], writes=[sgB[pb]])
                        K.op("dve", (lambda pb=pb, m=m: lambda h: h.tensor_tensor(
                            out=act[:, m, :], in0=sg[pb][:], in1=ps[3 + pb][:, 0:TF], op=ALU.mult))(),
                            reads=[sgB[pb], psB[3 + pb]], writes=[actB[m]])
                        if nxt and 6 <= m < 6 + KD:
                            pre_sq(i + 1, m - 6)
                    if nxt:
                        stats(u2[1 - b], u2B[1 - b], rstd, rB)
                    for j in range(KD):
                        bank = 5 + j % 2
                        for m in range(KF):
                            K.op("pe", (lambda j=j, m=m, bank=bank: lambda h: h.matmul(
                                ps[bank][:, 0:TF], lhsT=wo[:, m, j * 128:(j + 1) * 128], rhs=act[:, m, :],
                                start=(m == 0), stop=(m == KF - 1)))(), reads=[woB[m // 11], actB[m]], writes=[psB[bank]])
                        if nxt:
                            pre_u(i + 1, j)
                        K.op("act", (lambda j=j, bank=bank: lambda h: h.activation(out=ysb[:, j, :], in_=ps[bank][:, 0:TF], func=AF.Copy))(),
                             reads=[psB[bank]], writes=[yB[j]])
                        K.op("act", (lambda j=j, bank=bank, u=u: lambda h: h.activation(out=u[:, j, :], in_=ps[bank][:, 0:TF], func=AF.Square))(),
                             reads=[psB[bank]], writes=[uB[j]])
                    stats(u, uB, rstd2, r2B)
                    for j in range(KD):
                        tb = j % 2
                        K.op("dve", (lambda j=j, tb=tb, X=X: lambda h: h.scalar_tensor_tensor(
                            out=tmp[tb][:], in0=ysb[:, j, :], scalar=par[:, l, X, kG, j:j + 1], in1=rstd2[:],
                            op0=ALU.mult, op1=ALU.mult))(), reads=[yB[j], r2B], writes=[tB[tb]])
                        K.op("pool", (lambda j=j, tb=tb, h_=h_: lambda h: h.tensor_tensor(
                            out=h_[:, j, :], in0=h_[:, j, :], in1=tmp[tb][:], op=ALU.add))(),
                            reads=[tB[tb]], writes=[hb[j]])
                    if final_out is not None and X == 0:
                        K.dma("sp", final_out[:, c0:c0 + TF].rearrange("(k p) t -> p k t", p=128), h_[:], reads=hb)
                    else:
                        K.dma("sp", Hout[:, c0:c0 + TF].rearrange("(k p) t -> p k t", p=128), h_[:],
                              reads=hb, writes=[DB(hout_name, c0 // TF)])
                K.flush()

        def phase_proj(l, Hin, hin_name):
            TP = 512
            with ExitStack() as ph:
                win = sb("win", [128, KD, NW], BF16, ph)
                winB = K.Bs("winB", KD)
                hT = [sb("hT%d" % i, [128, KD, TP], F32, ph) for i in range(2)]
                hB = [K.Bs("hB%d_" % i, KD) for i in range(2)]
                rc = [sb("rc%d" % i, [128, TP], F32, ph) for i in range(2)]
                rs_ = [sb("rs%d" % i, [128, TP], F32, ph) for i in range(2)]
                rcB = K.Bs("rcB", 2)
                u = sb("u", [128, KD, TP], BF16, ph)
                uB = K.Bs("uB", KD)
                rstd = sb("rstd", [128, TP], F32, ph)
                rB = K.B("rstd")
                tmp = [sb("tmp%d" % i, [128, TP], F32, ph) for i in range(2)]
                tB = K.Bs("tmpB", 2)
                sqp = sb("sqp", [128, TP], BF16, ph)
                sqB = K.B("sqp")
                rq = sb("rq", [128, TP], F32, ph)
                rqB = K.B("rq")
                t1 = sb("t1", [128, TP], F32, ph)
                t2 = sb("t2", [128, TP], F32, ph)
                t1B, t2B = K.B("t1"), K.B("t2")
                qa = sb("qa", [128, 3, TP], BF16, ph)
                ka = sb("ka", [128, TP], BF16, ph)
                qb = sb("qb", [128, 3, TP], BF16, ph)
                kb = sb("kb", [128, 3, TP], BF16, ph)
                zz = sb("zz", [128, 2, TP], F32, ph)
                cbs = sb("cbs", [128, 2, TP], F32, ph)
                cxs = sb("cxs", [128, TP], F32, ph)
                vv = sb("vv", [128, 4, 8, 65], BF16, ph)
                zer = sb("zer", [128, 2, 2], F32, ph)
                qaB, kaB, qbB, kbB, zzB, cbB, cxB, vvB, zeB = (K.B(n) for n in ("qa", "ka", "qb", "kb", "zz", "cbs", "cxs", "vv", "zer"))
                for k in range(KD):
                    K.dma("pool", win[:, k, :], w_inx[l, k * 128:(k + 1) * 128, :], writes=[winB[k]])
                K.op("dve", lambda h: h.memset(vv[:], 1.0), writes=[vvB])
                K.op("dve", lambda h: h.memset(zer[:], 0.0), writes=[zeB])
                K.dma("sp", ZZ[:, :, 0:1], zer[:, :, 0:1], reads=[zeB])
                K.dma("sp", ZZ[:, :, S + 1:S + 3], zer[:, :, 0:2], reads=[zeB])
                K.dma("sp", ZZ[:, :, NT + 3:NT + 4], zer[:, :, 0:1], reads=[zeB])
                tiles = [(t * TP, TP, 0) for t in range(S // TP)] + [(S, LC, 1)]

                def load(i):
                    c0, n, X = tiles[i]
                    b = i % 2
                    K.dma("sp", hT[b][:, :, 0:n], Hin[:, c0:c0 + n].rearrange("(k p) t -> p k t", p=128),
                          reads=[DB(hin_name, j) for j in range(c0 // TF, (c0 + n) // TF)], writes=hB[b])
                    if X == 0:
                        K.dma("sp", rc[b][:], ropeC[:, c0:c0 + n], writes=[rcB[b]])
                        K.dma("sp", rs_[b][:], ropeS[:, c0:c0 + n], writes=[rcB[b]])
                load(0)
                for i, (c0, n, X) in enumerate(tiles):
                    b = i % 2
                    if i + 1 < len(tiles):
                        load(i + 1)
                    h_, hb = hT[b], hB[b]
                    for k in range(KD):
                        K.op("act", (lambda k=k, h_=h_, n=n: lambda h: h.activation(out=u[:, k, 0:n], in_=h_[:, k, 0:n], func=AF.Square))(),
                             reads=[hb[k]], writes=[uB[k]])
                    for k in range(KD):
                        K.op("pe", (lambda k=k, n=n: lambda h: h.matmul(ps[0][:, 0:n], lhsT=onesm[:], rhs=u[:, k, 0:n],
                                                                       start=(k == 0), stop=(k == KD - 1)))(),
                             reads=[uB[k]], writes=[psB[0]])
                    K.op("act", (lambda n=n: lambda h: h.activation(out=rstd[:, 0:n], in_=ps[0][:, 0:n], func=AF.Sqrt, bias=EPS, scale=1.0))(),
                         reads=[psB[0]], writes=[rB])
                    K.op("dve", (lambda n=n: lambda h: h.reciprocal(out=rstd[:, 0:n], in_=rstd[:, 0:n]))(), reads=[rB], writes=[rB])
                    for k in range(KD):
                        tb = k % 2
                        K.op("dve", (lambda k=k, tb=tb, h_=h_, X=X, n=n: lambda h: h.scalar_tensor_tensor(
                            out=tmp[tb][:, 0:n], in0=h_[:, k, 0:n], scalar=par[:, l, X, 3, k:k + 1], in1=rstd[:, 0:n],
                            op0=ALU.mult, op1=ALU.mult))(), reads=[hb[k], rB], writes=[tB[tb]])
                        K.op("act", (lambda k=k, tb=tb, X=X, n=n: lambda h: h.activation(
                            out=u[:, k, 0:n], in_=tmp[tb][:, 0:n], func=AF.Identity, bias=par[:, l, X, 4, k:k + 1], scale=1.0))(),
                            reads=[tB[tb]], writes=[uB[k]])

                    def proj(g, bank):
                        for k in range(KD):
                            K.op("pe", (lambda k=k, g=g, bank=bank, n=n: lambda h: h.matmul(
                                ps[bank][:, 0:n], lhsT=win[:, k, g * 128:(g + 1) * 128], rhs=u[:, k, 0:n],
                                start=(k == 0), stop=(k == KD - 1)))(), reads=[winB[k], uB[k]], writes=[psB[bank]])

                    ng = [(c, 3 + c, ("q", 0), ("q", 1), qa[:, c, 0:n], qaB) for c in range(3)]
                    ng.append((6, 7, ("k", 122), ("k", 123), ka[:, 0:n], kaB))
                    for gi, (g0, g1, s0, s1, dst, dB) in enumerate(ng):
                        ba, bb_ = 1 + (gi % 2) * 2, 2 + (gi % 2) * 2
                        proj(g0, ba)
                        if X == 0:
                            proj(g1, bb_)
                        sc0 = qsc[:, l, 0:1] if s0[0] == "q" else smp[:, l, s0[1]:s0[1] + 1]
                        sc1 = qsc[:, l, 1:2] if s1[0] == "q" else smp[:, l, s1[1]:s1[1] + 1]
                        K.op("act", (lambda ba=ba, n=n: lambda h: h.activation(out=sqp[:, 0:n], in_=ps[ba][:, 0:n], func=AF.Square))(),
                             reads=[psB[ba]], writes=[sqB])
                        K.op("pe", (lambda n=n: lambda h: h.matmul(ps[5][:, 0:n], lhsT=bd64[:], rhs=sqp[:, 0:n], start=True, stop=True))(),
                             reads=[sqB, cB], writes=[psB[5]])
                        K.op("act", (lambda n=n: lambda h: h.activation(out=rq[:, 0:n], in_=ps[5][:, 0:n], func=AF.Sqrt, bias=EPS, scale=1.0))(),
                             reads=[psB[5]], writes=[rqB])
                        K.op("dve", (lambda n=n: lambda h: h.reciprocal(out=rq[:, 0:n], in_=rq[:, 0:n]))(), reads=[rqB], writes=[rqB])
                        if X == 0:
                            K.op("dve", (lambda ba=ba, sc0=sc0, n=n: lambda h: h.scalar_tensor_tensor(
                                out=t1[:, 0:n], in0=ps[ba][:, 0:n], scalar=sc0, in1=rq[:, 0:n], op0=ALU.mult, op1=ALU.mult))(),
                                reads=[psB[ba], rqB], writes=[t1B])
                            K.op("dve", (lambda bb_=bb_, sc1=sc1, n=n: lambda h: h.scalar_tensor_tensor(
                                out=t2[:, 0:n], in0=ps[bb_][:, 0:n], scalar=sc1, in1=rq[:, 0:n], op0=ALU.mult, op1=ALU.mult))(),
                                reads=[psB[bb_], rqB], writes=[t2B])
                            K.op("pool", (lambda b=b, n=n: lambda h: h.tensor_tensor(out=t1[:, 0:n], in0=t1[:, 0:n], in1=rc[b][:, 0:n], op=ALU.mult))(),
                                 reads=[rcB[b]], writes=[t1B])
                            K.op("pool", (lambda b=b, n=n: lambda h: h.tensor_tensor(out=t2[:, 0:n], in0=t2[:, 0:n], in1=rs_[b][:, 0:n], op=ALU.mult))(),
                                 reads=[rcB[b]], writes=[t2B])
                            K.op("pool", (lambda dst=dst, n=n: lambda h: h.tensor_tensor(out=dst, in0=t1[:, 0:n], in1=t2[:, 0:n], op=ALU.add))(),
                                 reads=[t1B, t2B], writes=[dB])
                        else:
                            K.op("dve", (lambda ba=ba, sc0=sc0, dst=dst, n=n: lambda h: h.scalar_tensor_tensor(
                                out=dst, in0=ps[ba][:, 0:n], scalar=sc0, in1=rq[:, 0:n], op0=ALU.mult, op1=ALU.mult))(),
                                reads=[psB[ba], rqB], writes=[dB])
                    for c in range(3):
                        bank = 6 + c % 2
                        proj(8 + c, bank)
                        K.op("act", (lambda c=c, bank=bank, n=n: lambda h: h.activation(out=qb[:, c, 0:n], in_=ps[bank][:, 0:n], func=AF.Copy, scale=0.125))(),
                             reads=[psB[bank]], writes=[qbB])
                    for c in range(3):
                        bank = 6 + (c + 1) % 2
                        proj(11 + c, bank)
                        K.op("act", (lambda c=c, bank=bank, n=n: lambda h: h.activation(out=kb[:, c, 0:n], in_=ps[bank][:, 0:n], func=AF.Copy))(),
                             reads=[psB[bank]], writes=[kbB])
                    for c in range(2):
                        proj(14 + c, 6)
                        proj(18 + c, 7)
                        K.op("act", (lambda n=n: lambda h: h.activation(out=cxs[:, 0:n], in_=ps[6][:, 0:n], func=AF.Copy))(),
                             reads=[psB[6]], writes=[cxB])
                        K.op("dve", (lambda c=c, n=n: lambda h: h.tensor_tensor(out=zz[:, c, 0:n], in0=cxs[:, 0:n], in1=ps[7][:, 0:n], op=ALU.mult))(),
                             reads=[cxB, psB[7]], writes=[zzB])
                        proj(16 + c, 6)
                        K.op("act", (lambda c=c, n=n: lambda h: h.activation(out=cbs[:, c, 0:n], in_=ps[6][:, 0:n], func=AF.Copy))(),
                             reads=[psB[6]], writes=[cbB])
                    for s in range(n // 128):
                        bank = 1 + s % 2
                        for k in range(KD):
                            K.op("pe", (lambda k=k, s=s, bank=bank: lambda h: h.matmul(
                                ps[bank][:, :], lhsT=u[:, k, s * 128:(s + 1) * 128], rhs=win[:, k, 2560:3072],
                                start=(k == 0), stop=(k == KD - 1)))(), reads=[winB[k], uB[k]], writes=[psB[bank]])
                        K.op("dve", (lambda s=s, bank=bank: lambda h: h.tensor_copy(
                            out=vv[:, s, :, 0:64], in_=ps[bank][:, :].rearrange("p (h d) -> p h d", d=64)))(),
                            reads=[psB[bank]], writes=[vvB])
                    tl = [DB("proj", j) for j in range(c0 // 128, (c0 + n) // 128)]
                    K.dma("sp", QA[:, :, c0:c0 + n], qa[:, :, 0:n], reads=[qaB], writes=tl)
                    K.dma("sp", KA[:, c0:c0 + n], ka[:, 0:n], reads=[kaB], writes=tl)
                    K.dma("sp", QB[:, :, c0:c0 + n], qb[:, :, 0:n], reads=[qbB], writes=tl)
                    K.dma("sp", KBd[:, :, c0:c0 + n], kb[:, :, 0:n], reads=[kbB], writes=tl)
                    zc0 = c0 + 1 if X == 0 else c0 + 3
                    K.dma("sp", ZZ[:, :, zc0:zc0 + n], zz[:, :, 0:n], reads=[zzB], writes=tl)
                    K.dma("sp", CBd[:, :, c0:c0 + n], cbs[:, :, 0:n], reads=[cbB], writes=tl)
                    K.dma("sp", VV[c0:c0 + n].rearrange("(s p) h d -> p s h d", p=128), vv[:, 0:n // 128], reads=[vvB], writes=tl)
                K.flush()

        def phase_attn(l, Hin, Hout, hin_name, hout_name, do_ctx):
            TP = 512
            HP = 256
            with ExitStack() as ph:
                kaT = sb("kaT", [128, NT], BF16, ph)
                vaT = sb("vaT", [128, NKT, 2, 65], BF16, ph)
                kbc = sb("kbc", [128, 3, LC], BF16, ph)
                vbc = sb("vbc", [128, 2, 6, 65], BF16, ph)
                woh = sb("woh", [64, 12, D], BF16, ph)
                woc = sb("woc", [128, 2, D], BF16, ph)
                bI = sb("bI", [128, 5, 6, 128], F32, ph)
                bE = sb("bE", [128, 2, 4, 2, 128], F32, ph)
                resB, bEB = K.B("resident"), K.B("bE")
                qa = [sb("qa%d" % i, [128, 3, TP], BF16, ph) for i in range(2)]
                qaB = K.Bs("qaB", 2)
                qb = sb("qb", [128, 3, TP], BF16, ph)
                kbw = sb("kbw", [128, 3, 1024], BF16, ph)
                vbw = sb("vbw", [128, 8, 6, 65], BF16, ph)
                zt = sb("zt", [128, 2, TP + 2], F32, ph)
                cbt = sb("cbt", [128, 2, TP], F32, ph)
                hT = sb("hT", [128, KD, HP], F32, ph)
                ib = K.B("inB")
                hb = K.Bs("hB", KD)
                NPT = 3
                pT = [sb("pT%d" % i, [128, 7 * 128], BF16, ph) for i in range(NPT)]
                pB = K.Bs("pB", NPT)
                pA = [sb("pA%d" % i, [128, 2, TP], BF16, ph) for i in range(2)]
                pAB = K.Bs("pAB", 2)
                sbias2 = [sb("sbias%d" % i, [128, 5, 128], F32, ph) for i in range(2)]
                sbB2 = K.Bs("sbB", 2)
                oT = sb("oT", [64, 12, TP], BF16, ph)
                oB = K.Bs("oB", 12)
                ocT = sb("ocT", [128, 2, TP], BF16, ph)
                ocB = K.B("ocT")
                rec = sb("rec", [128, 2, TP], F32, ph)
                recB = K.B("rec")
                rsB = K.Bs("rsB", 4)
                rsn = [0]
                pcnt = [0]
                pend = []
                bcs = sb("bcs", [64, 2, TP], F32, ph)
                bcB = K.B("bcs")
                ysb = sb("ysb", [128, KD, HP], F32, ph)
                yB = K.Bs("yB", KD)
                sq = sb("sq", [128, KD, HP], BF16, ph)
                sqB = K.Bs("sqB", KD)
                rstd = sb("rstd", [128, HP], F32, ph)
                rB = K.B("rstd")
                tmp = [sb("tmp%d" % i, [128, HP], F32, ph) for i in range(2)]
                tB = K.Bs("tmpB", 2)
                K.dma("sp", kaT[:], KA[:, :], writes=[resB])
                K.dma("sp", vaT[:], VV[:, 0:2, :].rearrange("(s p) h d -> p s h d", p=128), writes=[resB])
                K.dma("sp", kbc[:], KBd[:, :, S:NT], writes=[resB])
                K.dma("sp", vbc[:], VV[S:NT, 2:8, :].rearrange("(s p) h d -> p s h d", p=128), writes=[resB])
                K.dma("pool", woh[:], w_o[l, 0:768, :].rearrange("(h p) n -> p h n", p=64), writes=[resB])
                K.dma("pool", woc[:], w_o[l, 768:1024, :].rearrange("(k p) n -> p k n", p=128), writes=[resB])
                K.dma("sp", bI[:], biasI[l].rearrange("p (a h q) -> p a h q", a=5, h=6), writes=[resB])
                tiles = [(t * TP, TP, 0) for t in range(S // TP)]
                if do_ctx:
                    tiles.append((S, LC, 1))
                ntl = S // TP
                assert ntl >= 2
                bEv = biasE[l].rearrange("p (c a h q) -> p c a h q", c=4, a=4, h=6)

                def kt0_of(t):
                    return max(0, min(4 * t - 2, NQT - 8))

                def load_q(i):
                    c0, n, X = tiles[i]
                    K.dma("sp", qa[i % 2][:, :, 0:n], QA[:, :, c0:c0 + n], writes=[qaB[i % 2]])

                load_q(0)
                pcount = [0]
                for i, (c0, n, X) in enumerate(tiles):
                    b = i % 2
                    t = c0 // TP
                    if i + 1 < len(tiles):
                        load_q(i + 1)
                    K.dma("sp", qb[:, :, 0:n], QB[:, :, c0:c0 + n], writes=[ib])
                    zc0 = c0 if X == 0 else c0 + 2
                    K.dma("sp", zt[:, :, 0:n + 2], ZZ[:, :, zc0:zc0 + n + 2], writes=[ib])
                    K.dma("sp", cbt[:, :, 0:n], CBd[:, :, c0:c0 + n], writes=[ib])
                    if X == 0:
                        k0 = kt0_of(t)
                        K.dma("sp", kbw[:], KBd[:, :, k0 * 128:k0 * 128 + 1024], writes=[ib])
                        K.dma("sp", vbw[:], VV[k0 * 128:k0 * 128 + 1024, 2:8, :].rearrange("(s p) h d -> p s h d", p=128), writes=[ib])
                    qab = qaB[b]

                    def normalize2(ob, h0, h1, n=n):
                        o2 = psall[:, ob * 512:(ob + 2) * 512].rearrange("p (b q) -> p b q", b=2)
                        slot = rsn[0] % 4
                        rsn[0] += 1
                        K.op("dve", (lambda n=n, o2=o2: lambda h: h.reciprocal(out=rec[64:65, :, 0:n], in_=o2[64:65, :, 0:n]))(),
                             reads=[psB[ob], psB[ob + 1]], writes=[recB])
                        K.dma("sp", RS[slot, :, :, 0:n], rec[64:65, :, 0:n], reads=[recB], writes=[rsB[slot]])
                        K.dma("sp", bcs[:, :, 0:n], RS[slot, :, :, 0:n].partition_broadcast(64), reads=[rsB[slot]], writes=[bcB])

                        def part2(ob=ob, h0=h0, h1=h1, n=n):
                            for bb, hidx in ((0, h0), (1, h1)):
                                K.op("dve", (lambda bb=bb, hidx=hidx, n=n, ob=ob: lambda h: h.tensor_tensor(
                                    out=oT[:, hidx, 0:n], in0=ps[ob + bb][0:64, 0:n], in1=bcs[:, bb, 0:n], op=ALU.mult))(),
                                    reads=[psB[ob + bb], bcB], writes=[oB[hidx]])
                        pend.append(part2)

                    def flush_pend():
                        while pend:
                            pend.pop(0)()

                    kts = list(range(NKT)) if X == 0 else list(range(NQT, NKT))
                    for c in range(3):
                        def qk(ki, c=c, n=n, b=b):
                            kt = kts[ki]
                            for half in range(2):
                                sbank = 2 * (ki % 2) + half
                                lo = 64 * half
                                K.op("pe", (lambda kt=kt, c=c, lo=lo, sbank=sbank, b=b, n=n: lambda h: h.matmul(
                                    ps[sbank][:, 0:n], lhsT=kaT[lo:lo + 64, kt * 128:(kt + 1) * 128], rhs=qa[b][lo:lo + 64, c, 0:n],
                                    start=True, stop=True))(), reads=[resB, qab], writes=[psB[sbank]])
                        ob = 4 + 2 * (pcnt[0] % 2)
                        pcnt[0] += 1
                        qk(0)
                        for ki, kt in enumerate(kts):
                            if ki + 1 < len(kts):
                                qk(ki + 1)
                            pp = ki % 2
                            K.op("act", (lambda pp=pp, n=n: lambda h: h.activation(
                                out=pA[pp][:, :, 0:n], in_=psall[:, pp * 1024:(pp + 1) * 1024].rearrange("p (h q) -> p h q", h=2)[:, :, 0:n],
                                func=AF.Exp))(), reads=[psB[2 * pp], psB[2 * pp + 1]], writes=[pAB[pp]])
                            for half in range(2):
                                K.op("pe", (lambda kt=kt, half=half, pp=pp, ki=ki, n=n, ob=ob: lambda h: h.matmul(
                                    ps[ob + half][0:65, 0:n], lhsT=vaT[:, kt, half, :], rhs=pA[pp][:, half, 0:n],
                                    start=(ki == 0), stop=(ki == len(kts) - 1)))(), reads=[resB, pAB[pp]], writes=[psB[ob + half]])
                            if ki == min(8, len(kts) - 1):
                                flush_pend()
                        normalize2(ob, c, c + 3)
                    for c in range(3):
                        if X == 0 and (t == 0 or t == ntl - 1):
                            cls0 = 0 if t == 0 else 2
                            K.dma("sp", bE[:], bEv[:, cls0:cls0 + 2, :, 2 * c:2 * c + 2, :], writes=[bEB])
                        ob = 4 + 2 * (pcnt[0] % 2)
                        pcnt[0] += 1
                        for qs in range(n // 128):
                            q0 = qs * 128
                            if qs == 1:
                                flush_pend()
                            if X == 0:
                                qt = c0 // 128 + qs
                                if qt < 2 or qt >= NQT - 2:
                                    wk = list(range(0, 4)) if qt < 2 else list(range(NQT - 4, NQT))
                                    cls = qt if qt < 2 else qt - (NQT - 2)
                                    bsrc = lambda a0, a1, half, cls=cls: bE[:, cls, a0:a1, half, :]
                                    brd = bEB
                                else:
                                    wk = list(range(qt - 2, qt + 3))
                                    bsrc = lambda a0, a1, half, c=c: bI[:, a0:a1, 2 * c + half, :]
                                    brd = resB
                            else:
                                wk = []
                            nw = len(wk)
                            for half in range(2):
                                lo = 64 * half
                                for a, kt in enumerate(wk):
                                    bank = 2 * half + (0 if a < 4 else 1)
                                    col = (a % 4) * 128
                                    sl = (kt - k0) * 128
                                    K.op("pe", (lambda bank=bank, col=col, sl=sl, lo=lo, c=c, q0=q0: lambda h: h.matmul(
                                        ps[bank][:, col:col + 128], lhsT=kbw[lo:lo + 64, c, sl:sl + 128],
                                        rhs=qb[lo:lo + 64, c, q0:q0 + 128], start=True, stop=True))(),
                                        reads=[ib], writes=[psB[bank]])
                                for a in range(2):
                                    bank = 2 * half + 1
                                    col = 128 + a * 128
                                    K.op("pe", (lambda bank=bank, col=col, a=a, lo=lo, c=c, q0=q0: lambda h: h.matmul(
                                        ps[bank][:, col:col + 128], lhsT=kbc[lo:lo + 64, c, a * 128:(a + 1) * 128],
                                        rhs=qb[lo:lo + 64, c, q0:q0 + 128], start=True, stop=True))(),
                                        reads=[ib, resB], writes=[psB[bank]])
                            for half in range(2):
                                hh = 2 * c + half
                                pi = pcount[0] % NPT
                                pcount[0] += 1
                                sbias, sbB = sbias2[half], sbB2[half]
                                if nw:
                                    na = min(nw, 4)
                                    K.op("dve", (lambda half=half, na=na, bsrc=bsrc, sbias=sbias: lambda h: h.tensor_tensor(
                                        out=sbias[:, 0:na, :], in0=ps[2 * half][:, 0:na * 128].rearrange("p (a q) -> p a q", q=128),
                                        in1=bsrc(0, na, half), op=ALU.add))(),
                                        reads=[psB[2 * half], brd], writes=[sbB])
                                    if nw > 4:
                                        K.op("dve", (lambda half=half, bsrc=bsrc, sbias=sbias: lambda h: h.tensor_tensor(
                                            out=sbias[:, 4:5, :], in0=ps[2 * half + 1][:, 0:128].rearrange("p (a q) -> p a q", q=128),
                                            in1=bsrc(4, 5, half), op=ALU.add))(),
                                            reads=[psB[2 * half + 1], brd], writes=[sbB], nosame=True)
                                    K.op("act", (lambda pi=pi, nw=nw, sbias=sbias: lambda h: h.activation(
                                        out=pT[pi][:, 0:nw * 128].rearrange("p (a q) -> p a q", q=128), in_=sbias[:, 0:nw, :], func=AF.Exp))(),
                                        reads=[sbB], writes=[pB[pi]])
                                bank = 2 * half + 1
                                K.op("act", (lambda pi=pi, bank=bank, nw=nw: lambda h: h.activation(
                                    out=pT[pi][:, nw * 128:(nw + 2) * 128], in_=ps[bank][:, 128:384], func=AF.Exp))(),
                                    reads=[psB[bank]], writes=[pB[pi]], nosame=(nw > 0))
                                nk = nw + 2
                                for a in range(nk):
                                    if a < nw:
                                        lhs = (lambda hh=hh, sl=wk[a] - k0: vbw[:, sl, hh, :])
                                    else:
                                        lhs = (lambda a=a, hh=hh, nw=nw: vbc[:, a - nw, hh, :])
                                    K.op("pe", (lambda a=a, lhs=lhs, half=half, pi=pi, q0=q0, nk=nk, ob=ob: lambda h: h.matmul(
                                        ps[ob + half][0:65, q0:q0 + 128], lhsT=lhs(), rhs=pT[pi][:, a * 128:(a + 1) * 128],
                                        start=(a == 0), stop=(a == nk - 1)))(), reads=[ib, resB, pB[pi]], writes=[psB[ob + half]])
                        flush_pend()
                        normalize2(ob, 6 + 2 * c, 6 + 2 * c + 1)
                    flush_pend()
                    cacc0 = ysb[:, 0:2, :].rearrange("p a q -> p (a q)")
                    cacc1 = ysb[:, 2:4, :].rearrange("p a q -> p (a q)")
                    ca0B, ca1B = [yB[0], yB[1]], [yB[2], yB[3]]
                    for c in range(2):
                        w0 = smp[:, l, 124 + c * 3:125 + c * 3]
                        w1 = smp[:, l, 125 + c * 3:126 + c * 3]
                        w2 = smp[:, l, 126 + c * 3:127 + c * 3]
                        K.op("pool", (lambda c=c, w0=w0, n=n: lambda h: h.tensor_scalar(
                            out=cacc0[:, 0:n], in0=zt[:, c, 0:n], scalar1=w0, scalar2=None, op0=ALU.mult))(),
                            reads=[ib, cB], writes=ca0B)
                        for (wt_, off) in ((w1, 1), (w2, 2)):
                            K.op("pool", (lambda c=c, wt_=wt_, off=off, n=n: lambda h: h.tensor_scalar(
                                out=cacc1[:, 0:n], in0=zt[:, c, off:n + off], scalar1=wt_, scalar2=None, op0=ALU.mult))(),
                                reads=[ib, cB], writes=ca1B)
                            K.op("pool", (lambda n=n: lambda h: h.tensor_tensor(
                                out=cacc0[:, 0:n], in0=cacc0[:, 0:n], in1=cacc1[:, 0:n], op=ALU.add))(),
                                reads=ca1B, writes=ca0B)
                        K.op("pool", (lambda c=c, n=n: lambda h: h.tensor_tensor(
                            out=ocT[:, c, 0:n], in0=cacc0[:, 0:n], in1=cbt[:, c, 0:n], op=ALU.mult))(),
                            reads=[ib] + ca0B, writes=[ocB])
                    for hf in range(n // HP):
                        o0 = hf * HP
                        K.dma("sp", hT[:], Hin[:, c0 + o0:c0 + o0 + HP].rearrange("(k p) t -> p k t", p=128), writes=hb)
                        for j in range(KD):
                            bank = j % 2
                            for hh in range(12):
                                K.op("pe", (lambda j=j, hh=hh, bank=bank, o0=o0: lambda h: h.matmul(
                                    ps[bank][:, 0:HP], lhsT=woh[:, hh, j * 128:(j + 1) * 128], rhs=oT[:, hh, o0:o0 + HP],
                                    start=(hh == 0), stop=False))(), reads=[resB, oB[hh]], writes=[psB[bank]])
                            for k in range(2):
                                K.op("pe", (lambda j=j, k=k, bank=bank, o0=o0: lambda h: h.matmul(
                                    ps[bank][:, 0:HP], lhsT=woc[:, k, j * 128:(j + 1) * 128], rhs=ocT[:, k, o0:o0 + HP],
                                    start=False, stop=(k == 1)))(), reads=[resB, ocB], writes=[psB[bank]])
                            K.op("act", (lambda j=j, bank=bank: lambda h: h.activation(out=ysb[:, j, :], in_=ps[bank][:, 0:HP], func=AF.Copy))(),
                                 reads=[psB[bank]], writes=[yB[j]])
                            K.op("act", (lambda j=j, bank=bank: lambda h: h.activation(out=sq[:, j, :], in_=ps[bank][:, 0:HP], func=AF.Square))(),
                                 reads=[psB[bank]], writes=[sqB[j]])
                        for k in range(KD):
                            K.op("pe", (lambda k=k: lambda h: h.matmul(ps[2][:, 0:HP], lhsT=onesm[:], rhs=sq[:, k, :],
                                                                       start=(k == 0), stop=(k == KD - 1)))(),
                                 reads=[sqB[k], cB], writes=[psB[2]])
                        K.op("act", lambda h: h.activation(out=rstd[:], in_=ps[2][:, 0:HP], func=AF.Sqrt, bias=EPS, scale=1.0),
                             reads=[psB[2]], writes=[rB])
                        K.op("dve", lambda h: h.reciprocal(out=rstd[:], in_=rstd[:]), reads=[rB], writes=[rB])
                        for j in range(KD):
                            tb = j % 2
                            K.op("dve", (lambda j=j, tb=tb, X=X: lambda h: h.scalar_tensor_tensor(
                                out=tmp[tb][:], in0=ysb[:, j, :], scalar=par[:, l, X, 5, j:j + 1], in1=rstd[:],
                                op0=ALU.mult, op1=ALU.mult))(), reads=[yB[j], rB], writes=[tB[tb]])
                            K.op("pool", (lambda j=j, tb=tb: lambda h: h.tensor_tensor(
                                out=hT[:, j, :], in0=hT[:, j, :], in1=tmp[tb][:], op=ALU.add))(),
                                reads=[tB[tb]], writes=[hb[j]])
                        K.dma("sp", Hout[:, c0 + o0:c0 + o0 + HP].rearrange("(k p) t -> p k t", p=128), hT[:], reads=hb)
                K.flush()

        phase_ffn(0, 0, xT, HA, "x", "HA", True)
        phase_proj(0, HA, "HA")
        phase_attn(0, HA, HB, "HA", "HB", True)
        phase_ffn(0, 1, HB, HA, "HB", "HA", True)
        phase_ffn(1, 0, HA, HB, "HA", "HB", True)
        phase_proj(1, HB, "HB")
        phase_attn(1, HB, HA, "HB", "HA", False)
        phase_ffn(1, 1, HA, None, "HA", "none", False, final_out=outT)
    return nc


def _rope_tables(S):
    pos = np.arange(S, dtype=np.int32)
    row = (pos // GW).astype(np.float32)
    col = (pos % GW).astype(np.float32)
    freq = (1.0 / (np.float32(10000.0) ** (np.arange(16, dtype=np.float32) / np.float32(16)))).astype(np.float32)
    C = np.zeros((128, S), np.float32)
    Sg = np.zeros((128, S), np.float32)
    for p in range(128):
        d = p % 64
        a, j, f = d // 32, (d // 16) % 2, d % 16
        ang = ((row if a == 0 else col) * freq[f]).astype(np.float32)
        C[p] = np.cos(ang)
        Sg[p] = np.sin(ang) * (-1.0 if j == 0 else 1.0)
    return C, Sg


def _bias_mats(rpb_l, qt, kts, rows):
    out = np.full((128, len(kts), 6, 128), NEG, np.float32)
    j = np.arange(128)
    for a, kt in enumerate(kts):
        kr = 2 * kt + j // 64
        kc = j % 64
        qr = 2 * qt + j // 64
        qc = j % 64
        rs = np.clip(qr - 4, 0, rows - 8)
        cs = np.clip(qc - 8, 0, GW - 16)
        okr = (kr[:, None] >= rs[None, :]) & (kr[:, None] < rs[None, :] + 8)
        okc = (kc[:, None] >= cs[None, :]) & (kc[:, None] < cs[None, :] + 16)
        ok = okr & okc
        dr = np.clip(kr[:, None] - qr[None, :] + 7, 0, 14)
        dc = np.clip(kc[:, None] - qc[None, :] + 15, 0, 30)
        for hh in range(6):
            vals = rpb_l[hh][dr, dc]
            out[:, a, hh, :] = np.where(ok, vals, np.float32(NEG))
    return out


def _prep_inputs(inp, S):
    x, c, ctx, c_ctx = inp["x"], inp["c"], inp["ctx"], inp["c_ctx"]
    B = x.shape[0]
    rows = S // GW
    NQT = S // 128
    w_in = inp["w_in"]
    perm = np.arange(64) ^ 16
    cols = []
    qa_h = lambda h: np.arange(h * 64, (h + 1) * 64)
    for cc in range(3):
        cols += [qa_h(cc), qa_h(cc + 3)]
    for cc in range(3):
        cols += [qa_h(cc)[perm], qa_h(cc + 3)[perm]]
    ka0 = 768
    cols += [np.arange(ka0, ka0 + 64), np.arange(ka0 + 64, ka0 + 128)]
    cols += [np.arange(ka0, ka0 + 64)[perm], np.arange(ka0 + 64, ka0 + 128)[perm]]
    cols += [np.arange(384, 768)]
    cols += [np.arange(768 + 256, 768 + 256 + 384)]
    cx0 = 768 + 1024
    cols += [np.arange(cx0, cx0 + 768)]
    cols += [np.arange(768 + 128, 768 + 256)]
    cols += [np.arange(768 + 256 + 384, 768 + 1024)]
    cols = np.concatenate(cols)
    assert cols.shape[0] == NW
    w_inx = np.ascontiguousarray(w_in[:, :, cols])
    smallp = np.zeros((2, 128, NS), np.float32)
    p = np.arange(128)
    for l in range(2):
        smallp[l, :, 0:72] = inp["b_ada"][l].reshape(72, 128).T
        smallp[l, :, 72:120] = inp["norm_g"][l].reshape(6, 8, 128).transpose(2, 0, 1).reshape(128, 48)
        smallp[l, :, 120] = inp["qk_g"][l, 0][p % 64]
        smallp[l, :, 121] = inp["qk_g"][l, 0][(p % 64) ^ 16]
        smallp[l, :, 122] = inp["qk_g"][l, 1][p % 64]
        smallp[l, :, 123] = inp["qk_g"][l, 1][(p % 64) ^ 16]
        smallp[l, :, 124:130] = inp["conv_w"][l].reshape(3, 2, 128).transpose(2, 1, 0).reshape(128, 6)
    C, Sg = _rope_tables(S)
    biasI = np.zeros((2, 128, 5 * 6 * 128), np.float32)
    biasE = np.zeros((2, 128, 4 * 4 * 6 * 128), np.float32)
    for l in range(2):
        biasI[l] = _bias_mats(inp["rpb"][l], 2, [0, 1, 2, 3, 4], rows).reshape(128, -1)
        be = [_bias_mats(inp["rpb"][l], qt, kts, rows) for qt, kts in
              ((0, [0, 1, 2, 3]), (1, [0, 1, 2, 3]), (NQT - 2, list(range(NQT - 4, NQT))), (NQT - 1, list(range(NQT - 4, NQT))))]
        biasE[l] = np.stack(be, axis=1).reshape(128, -1)
    shared = dict(smallp=smallp, w_ada=np.ascontiguousarray(inp["w_ada"]), w_inx=w_inx, w_o=np.ascontiguousarray(inp["w_o"]),
                  ffn_wi=np.ascontiguousarray(inp["ffn_wi"]), ffn_wo=np.ascontiguousarray(inp["ffn_wo"]),
                  ropeC=C, ropeS=Sg, biasI=biasI, biasE=biasE)
    maps = []
    for b in range(B):
        m = dict(shared)
        m["xT"] = np.ascontiguousarray(np.concatenate([x[b].T, ctx[b].T], axis=1))
        cv = np.stack([c[b].reshape(8, 128).T, c_ctx.reshape(8, 128).T], axis=-1)
        m["cvec"] = np.ascontiguousarray(cv.astype(np.float32))
        maps.append(m)
    return maps


_CACHE = {}


def run(inp, debug=False):
    inp = {k: np.asarray(v, dtype=np.float32) for k, v in inp.items()}
    S = inp["x"].shape[1]
    B = inp["x"].shape[0]
    key = (S, debug)
    if key not in _CACHE:
        _CACHE[key] = build_program(S, debug)
    nc = _CACHE[key]
    maps = _prep_inputs(inp, S)
    res = run_bass_kernel_spmd(nc, maps, core_ids=list(range(B)))
    out = np.stack([np.ascontiguousarray(r["outT"].T) for r in res.results], axis=0)
    return out.astype(np.float32), res


def kernel(**inputs):
    out, _ = run(inputs)
    return out
```
